# Optimizing a Trainium2 kernel written in Bass

```python
import math
import jax, jax.numpy as jnp
from jax import lax
import numpy as np

D_MODEL = 1024
BATCH = 4
SEQ = 8192
DEPTH = 1
DEC_BATCH = 16
DEC_SEQ = 4096
PAST_LEN = 128

M_HEADS = 4
M_HEAD_DIM = 256
M_WIDTH = M_HEADS * M_HEAD_DIM
M_CHUNK = 128
M_CONV = 3
A_PATTERNS = ((128, 1), (512, 4), (2048, 16))
N_GROUPS = 3
A_HEADS = 8
A_HEAD_DIM = 64
A_WIDTH = A_HEADS * A_HEAD_DIM
A_QBLOCK = 64
N_BUCKETS = 32
MAX_DISTANCE = 1024
EPS = 1e-6
N_BRANCHES = 2
IN_SPLITS = (2 * M_WIDTH, M_WIDTH, M_WIDTH, M_WIDTH, 2 * M_HEADS, 2 * M_HEADS,
             3 * N_GROUPS * A_WIDTH, A_WIDTH, N_BRANCHES * D_MODEL)
IN_COLS = sum(IN_SPLITS)

kernel_name = "hybrid_mlstm_dilated_attn_encoder"


def rms_norm(x, w):
    xf = x.astype(jnp.float32)
    y = xf * lax.rsqrt(jnp.mean(xf * xf, axis=-1, keepdims=True) + EPS)
    return (y * w.astype(jnp.float32)).astype(x.dtype)


def centred_dwconv(x, w, b):
    k = w.shape[0]
    s = x.shape[1]
    pad = k // 2
    xp = jnp.pad(x, ((0, 0), (pad, pad), (0, 0)))
    y = b
    for j in range(k):
        y = y + xp[:, j:j + s] * w[j]
    return y


def mlstm_scan(q, k, v, li, lf):
    n_, h_, s_, e_ = q.shape
    lc = M_CHUNK
    nc = s_ // lc

    def chunks(t):
        return jnp.moveaxis(t.reshape(n_, h_, nc, lc, *t.shape[3:]), 2, 0)

    tril = jnp.tril(jnp.ones((lc, lc), dtype=bool))

    def step(carry, xs):
        c_mat, n_vec, m = carry
        qc, kc, vc, ic, fc = xs
        bc = jnp.cumsum(fc, axis=-1)
        d_log = jnp.where(tril, bc[..., :, None] - bc[..., None, :] + ic[..., None, :], -jnp.inf)
        a = bc + m[..., None]
        mt = jnp.maximum(a, jnp.max(d_log, axis=-1))
        sc = jnp.einsum('nhtk,nhsk->nhts', qc, kc) * jnp.exp(d_log - mt[..., None])
        wa = jnp.exp(a - mt)
        num = jnp.einsum('nhts,nhsv->nhtv', sc, vc) + wa[..., None] * jnp.einsum('nhtk,nhkv->nhtv', qc, c_mat)
        den = jnp.sum(sc, axis=-1) + wa * jnp.einsum('nhtk,nhk->nht', qc, n_vec)
        h = num / jnp.maximum(jnp.abs(den), jnp.exp(-mt))[..., None]
        bl = bc[..., -1]
        g = bl[..., None] - bc + ic
        m_new = jnp.maximum(bl + m, jnp.max(g, axis=-1))
        ws = jnp.exp(g - m_new[..., None])
        decay = jnp.exp(bl + m - m_new)
        kw = kc * ws[..., None]
        c_new = decay[..., None, None] * c_mat + jnp.einsum('nhsk,nhsv->nhkv', kw, vc)
        n_new = decay[..., None] * n_vec + jnp.sum(kw, axis=2)
        return (c_new, n_new, m_new), h

    f32 = jnp.float32
    carry0 = (jnp.zeros((n_, h_, e_, e_), f32), jnp.zeros((n_, h_, e_), f32), jnp.zeros((n_, h_), f32))
    _, hs = lax.scan(step, carry0, (chunks(q), chunks(k), chunks(v), chunks(li), chunks(lf)))
    return jnp.moveaxis(hs, 0, 2).reshape(n_, h_, s_, e_)


def mlstm_mixer(qk_pre, v_pre, o_pre, i_pre, f_pre, conv_w, conv_b, i_bias, f_bias, head_norm_w):
    f32 = jnp.float32
    b_, s_, _ = v_pre.shape
    qk = jax.nn.silu(centred_dwconv(qk_pre.astype(f32), conv_w.astype(f32), conv_b.astype(f32)))

    def heads(t):
        return t.reshape(b_, s_, M_HEADS, M_HEAD_DIM).transpose(0, 2, 1, 3)

    q = heads(qk[..., :M_WIDTH])
    k = heads(qk[..., M_WIDTH:]) * (M_HEAD_DIM ** -0.5)
    v = heads(v_pre.astype(f32))
    li = (i_pre.astype(f32).reshape(b_, s_, 2, M_HEADS) + i_bias.astype(f32)).transpose(2, 0, 3, 1)
    lf = jax.nn.log_sigmoid(f_pre.astype(f32).reshape(b_, s_, 2, M_HEADS) + f_bias.astype(f32)).transpose(2, 0, 3, 1)

    def both(t_fwd, t_bwd):
        return jnp.concatenate([t_fwd, jnp.flip(t_bwd, axis=2)], axis=0)

    h = mlstm_scan(both(q, q), both(k, k), both(v, v), both(li[0], li[1]), both(lf[0], lf[1]))
    h = h[:b_] + jnp.flip(h[b_:], axis=2)
    h = h * lax.rsqrt(jnp.mean(h * h, axis=-1, keepdims=True) + EPS) * head_norm_w.astype(f32)[:, None, :]
    h = h.transpose(0, 2, 1, 3).reshape(b_, s_, M_WIDTH)
    return h * jax.nn.sigmoid(o_pre.astype(f32))


def t5_bucket(rel):
    nb = N_BUCKETS // 2
    exact = nb // 2
    n = np.abs(rel)
    large = exact + (np.log(np.maximum(n, 1) / exact) / math.log(MAX_DISTANCE / exact) * (nb - exact)).astype(np.int32)
    large = np.minimum(large, nb - 1)
    return (rel > 0).astype(np.int32) * nb + np.where(n < exact, n, large)


def dilated_group(q, k, v, dil, half, bias_table):
    b_, s_, h_, e_ = q.shape
    l_ = s_ // dil
    nblk = -(-l_ // A_QBLOCK)
    lp = nblk * A_QBLOCK
    kb_len = A_QBLOCK + 2 * half

    def residue(t):
        return t.reshape(b_, l_, dil, h_, e_).transpose(0, 2, 3, 1, 4)

    qr = jnp.pad(residue(q), ((0, 0),) * 3 + ((0, lp - l_), (0, 0))).reshape(b_, dil, h_, nblk, A_QBLOCK, e_)
    pad_kv = ((0, 0),) * 3 + ((half, lp - l_ + half), (0, 0))
    idx = np.arange(nblk)[:, None] * A_QBLOCK + np.arange(kb_len)[None, :]
    kb = jnp.pad(residue(k), pad_kv)[:, :, :, idx]
    vb = jnp.pad(residue(v), pad_kv)[:, :, :, idx]
    off = np.arange(kb_len)[None, :] - half - np.arange(A_QBLOCK)[:, None]
    key_pos = idx - half
    mask = (np.abs(off) <= half)[None] & ((key_pos >= 0) & (key_pos < l_))[:, None, :]
    bias = bias_table.astype(jnp.float32)[t5_bucket(off * dil)].transpose(2, 0, 1)
    logits = jnp.einsum('bghnqe,bghnke->bghnqk', qr, kb) + bias[:, None]
    logits = jnp.where(mask, logits, -jnp.inf)
    mx = jnp.max(logits, axis=-1)
    p = jnp.exp(logits - mx[..., None])
    den = jnp.sum(p, axis=-1)
    o = jnp.einsum('bghnqk,bghnke->bghnqe', p, vb)

    def back(t):
        tail = t.shape[5:]
        t = t.reshape(b_, dil, h_, lp, *tail)[:, :, :, :l_]
        return jnp.moveaxis(t, 3, 1).reshape(b_, s_, h_, *tail)

    return back(o), back(mx), back(den)


def dilated_attention(qkv, rel_table):
    b_, s_, _ = qkv.shape
    qkv = qkv.astype(jnp.float32).reshape(b_, s_, 3, N_GROUPS, A_HEADS, A_HEAD_DIM)
    outs, maxes, dens = [], [], []
    for g, (win, dil) in enumerate(A_PATTERNS):
        o, mx, den = dilated_group(qkv[:, :, 0, g] * (A_HEAD_DIM ** -0.5), qkv[:, :, 1, g], qkv[:, :, 2, g],
                                   dil, win // (2 * dil), rel_table[:, g])
        outs.append(o)
        maxes.append(mx)
        dens.append(den)
    mx = jnp.stack(maxes)
    w = jnp.exp(mx - jnp.max(mx, axis=0))
    num = jnp.einsum('gbsh,gbshe->bshe', w, jnp.stack(outs))
    den = jnp.sum(w * jnp.stack(dens), axis=0)
    return (num / den[..., None]).reshape(b_, s_, A_WIDTH)


def encoder_layer(x, pre_w, w_in, conv_w, conv_b, i_bias, f_bias, head_norm_w, w_pm, w_pa, w_out, post_w, rel_table):
    b_, s_, _ = x.shape
    h = rms_norm(x, pre_w)
    cols = jnp.einsum('bsd,dc->bsc', h, w_in)
    m_qk, m_v, m_o, m_z, m_i, m_f, a_qkv, a_z, gate = jnp.split(cols, np.cumsum(IN_SPLITS)[:-1], axis=-1)
    y_m = mlstm_mixer(m_qk, m_v, m_o, m_i, m_f, conv_w, conv_b, i_bias, f_bias, head_norm_w) * jax.nn.silu(m_z.astype(jnp.float32))
    y_a = dilated_attention(a_qkv, rel_table) * jax.nn.silu(a_z.astype(jnp.float32))
    p_m = jnp.einsum('bsc,cd->bsd', y_m.astype(x.dtype), w_pm)
    p_a = jnp.einsum('bsc,cd->bsd', y_a.astype(x.dtype), w_pa)
    g = jax.nn.sigmoid(gate.astype(jnp.float32)).reshape(b_, s_, N_BRANCHES, D_MODEL)
    merged = g[:, :, 0] * p_m + g[:, :, 1] * p_a
    out = jnp.einsum('bsd,de->bse', merged.astype(x.dtype), w_out)
    return x + rms_norm(out, post_w)


def setup_inputs(seed: int = 0) -> dict:
    key = jax.random.key(seed)
    ks = jax.random.split(key, 14)
    f32 = jnp.float32
    nrm = lambda k_, shape: jax.random.normal(k_, shape, f32)
    return {
        "x_prompt": nrm(ks[0], (BATCH, SEQ, D_MODEL)),
        "x_sample": nrm(ks[1], (DEC_BATCH, DEC_SEQ, D_MODEL)),
        "pre_norm_w": 1.0 + 0.1 * nrm(ks[2], (DEPTH, D_MODEL)),
        "w_in": nrm(ks[3], (DEPTH, D_MODEL, IN_COLS)) * D_MODEL ** -0.5,
        "m_conv_w": nrm(ks[4], (DEPTH, M_CONV, 2 * M_WIDTH)) * M_CONV ** -0.5,
        "m_conv_b": 0.01 * nrm(ks[5], (DEPTH, 2 * M_WIDTH)),
        "m_igate_b": 0.1 * nrm(ks[6], (DEPTH, 2, M_HEADS)),
        "m_fgate_b": jnp.linspace(3.0, 6.0, M_HEADS, dtype=f32)[None, None, :] + 0.1 * nrm(ks[7], (DEPTH, 2, M_HEADS)),
        "m_head_norm_w": 1.0 + 0.1 * nrm(ks[8], (DEPTH, M_HEADS, M_HEAD_DIM)),
        "w_proj_m": nrm(ks[9], (DEPTH, M_WIDTH, D_MODEL)) * M_WIDTH ** -0.5,
        "w_proj_a": nrm(ks[10], (DEPTH, A_WIDTH, D_MODEL)) * A_WIDTH ** -0.5,
        "w_out": nrm(ks[11], (DEPTH, D_MODEL, D_MODEL)) * D_MODEL ** -0.5,
        "post_norm_w": 1.0 + 0.1 * nrm(ks[12], (DEPTH, D_MODEL)),
        "rel_bias_table": 0.5 * nrm(ks[13], (N_BUCKETS, N_GROUPS, A_HEADS)),
    }


def reference(x_prompt, x_sample, pre_norm_w, w_in, m_conv_w, m_conv_b, m_igate_b, m_fgate_b, m_head_norm_w,
              w_proj_m, w_proj_a, w_out, post_norm_w, rel_bias_table):
    def trunk(x):
        for l in range(DEPTH):
            x = encoder_layer(x, pre_norm_w[l], w_in[l], m_conv_w[l], m_conv_b[l], m_igate_b[l], m_fgate_b[l],
                              m_head_norm_w[l], w_proj_m[l], w_proj_a[l], w_out[l], post_norm_w[l], rel_bias_table)
        return x

    y_prompt = trunk(x_prompt)
    y_sample = trunk(x_sample)
    return (y_prompt, y_sample)
```

```python
import contextlib
import math
import numpy as np
import ml_dtypes
import concourse.bass as bass
import concourse.mybir as mybir
from concourse.bass_utils import run_bass_kernel_spmd

F32 = mybir.dt.float32
BF16 = mybir.dt.bfloat16
ALU = mybir.AluOpType
AF = mybir.ActivationFunctionType
AX = mybir.AxisListType

COMPUTE = ("pe", "act", "dve", "pool")

T = 12288
SEG = 4096
NSEG = 3
D = 1024
INC = 12304
NCH = 96
EPS = 1e-6
DILS = (1, 4, 16)
NEG = -30000.0
SBUF_WORDS = 207 * 256

C_MQK, C_MV, C_MO, C_MZ, C_MI, C_MF, C_AQKV, C_AZ, C_GATE = 0, 2048, 3072, 4096, 5120, 5128, 5136, 9744, 10256


class Prog:
    def __init__(self, nc):
        self.nc = nc
        self.ops = []
        self.barriers = []
        self.bankmap = {}

    def op(self, eng, fn, reads=(), writes=(), dma=None, multi=()):
        self.ops.append(dict(eng=eng, fn=fn, reads=tuple(reads), writes=tuple(writes), dma=dma,
                             multi=tuple(multi), deps=set(), signal=False))
        return len(self.ops) - 1

    def barrier(self):
        self.barriers.append(len(self.ops))

    def _analyze(self):
        last_w = {}
        readers = {}
        last_bank = {}
        bar_deps = set()
        bi = 0
        last_by_stream = {}
        for i, o in enumerate(self.ops):
            while bi < len(self.barriers) and self.barriers[bi] <= i:
                bar_deps = set(last_by_stream.values())
                bi += 1
            deps = set(bar_deps)
            for t in o["reads"]:
                deps.update(last_w.get(t, ()))
            for t in o["writes"]:
                deps.update(last_w.get(t, ()))
                deps.update(readers.get(t, ()))
            for t in o["multi"]:
                deps.update(readers.get(t, ()))
            obanks = set()
            for t in o["reads"] + o["writes"] + o["multi"]:
                obanks.update(self.bankmap.get(t, ()))
            for b in obanks:
                lb = last_bank.get(b)
                if lb is not None and self.ops[lb]["eng"] != o["eng"]:
                    deps.add(lb)
                last_bank[b] = i
            deps.discard(i)
            keep = set()
            for d in deps:
                od = self.ops[d]
                if od["dma"] is None and o["dma"] is None and od["eng"] == o["eng"] and o["eng"] == "pe":
                    continue
                keep.add(d)
            o["deps"] = keep
            for d in keep:
                self.ops[d]["signal"] = True
            for t in o["reads"]:
                readers.setdefault(t, []).append(i)
            for t in o["writes"]:
                last_w[t] = [i]
                readers[t] = []
            for t in o["multi"]:
                last_w.setdefault(t, []).append(i)
            key = ("d", o["dma"]) if o["dma"] is not None else ("e", o["eng"])
            last_by_stream[key] = i

    def emit(self):
        nc = self.nc
        self._analyze()
        dma_keys = []
        for o in self.ops:
            if o["dma"] is not None and o["dma"] not in dma_keys:
                dma_keys.append(o["dma"])
        engs_used = []
        for o in self.ops:
            if o["eng"] not in engs_used:
                engs_used.append(o["eng"])
        with contextlib.ExitStack() as st:
            esem = {e: st.enter_context(nc.semaphore("s_" + e)) for e in COMPUTE}
            first, last = {}, {}
            for i, o in enumerate(self.ops):
                if o["dma"] is not None:
                    first.setdefault(o["dma"], i)
                    last[o["dma"]] = i
            phys_last = []
            kmap = {}
            for k in sorted(dma_keys, key=lambda k: first[k]):
                chosen = None
                for pi, pl in enumerate(phys_last):
                    if any(pl < b <= first[k] for b in self.barriers):
                        chosen = pi
                        break
                if chosen is None:
                    phys_last.append(last[k])
                    chosen = len(phys_last) - 1
                else:
                    phys_last[chosen] = last[k]
                kmap[k] = "p%d" % chosen
            for o in self.ops:
                if o["dma"] is not None:
                    o["lkey"] = o["dma"]
                    o["dma"] = kmap[o["dma"]]

            def same_batch(a, b):
                return (self.ops[a]["lkey"] == self.ops[b]["lkey"]
                        and not any(a < x <= b for x in self.barriers))
            dma_keys = ["p%d" % i for i in range(len(phys_last))]
            dsem = {k: st.enter_context(nc.semaphore("d_" + k)) for k in dma_keys}
            cnt = {e: 0 for e in COMPUTE}
            dcnt = {k: 0 for k in dma_keys}
            for o in self.ops:
                if o["dma"] is not None:
                    dcnt[o["dma"]] += 16
                    o["sem"], o["val"], o["semname"] = dsem[o["dma"]], dcnt[o["dma"]], "d_" + o["dma"]
                elif o["signal"]:
                    cnt[o["eng"]] += 1
                    o["sem"], o["val"], o["semname"] = esem[o["eng"]], cnt[o["eng"]], "s_" + o["eng"]
            prev_batch_last = {}
            by_eng = {}
            for i, o in enumerate(self.ops):
                by_eng.setdefault(o["eng"], []).append(i)
            for ename, idxs in by_eng.items():
                batch = []
                def close(batch):
                    if not batch:
                        return
                    lastv = self.ops[batch[-1]]["val"]
                    for bi_ in batch:
                        self.ops[bi_]["val"] = lastv
                for i in idxs:
                    o = self.ops[i]
                    if o["dma"] is None:
                        close(batch)
                        batch = []
                        continue
                    if batch and same_batch(batch[-1], i):
                        batch.append(i)
                    else:
                        close(batch)
                        batch = [i]
                close(batch)
            lastop = {}
            prev_in_eng = {}
            for i, o in enumerate(self.ops):
                if o["dma"] is None:
                    prev_in_eng[o["eng"]] = None
                    continue
                k = o["dma"]
                pi_ = prev_in_eng.get(o["eng"])
                sb_ = pi_ is not None and same_batch(pi_, i)
                if not sb_ and k in lastop:
                    o["deps"].add(lastop[k])
                lastop[k] = i
                prev_in_eng[o["eng"]] = i
            final = dict(dcnt)
            self._simulate()
            block = st.enter_context(nc.Block())
            hooks = {"pe": block.tensor, "act": block.scalar, "dve": block.vector,
                     "pool": block.gpsimd, "sp": block.sync}

            def make(ename):
                def body(e):
                    seen = {}
                    for o in self.ops:
                        if o["eng"] != ename:
                            continue
                        need = {}
                        for d in o["deps"]:
                            od = self.ops[d]
                            key = od["semname"]
                            if od["val"] > need.get(key, (None, 0))[1]:
                                need[key] = (od["sem"], od["val"])
                        for key, (sem, val) in need.items():
                            if seen.get(key, 0) >= val:
                                continue
                            e.wait_ge(sem, val)
                            seen[key] = val
                        ins = o["fn"](e)
                        if o["dma"] is not None:
                            ins.then_inc(o["sem"], 16)
                        elif o["signal"]:
                            ins.then_inc(o["sem"], 1)
                    if ename == "sp":
                        for k in dma_keys:
                            if seen.get("d_" + k, 0) < final[k]:
                                e.wait_ge(dsem[k], final[k])
                return body

            names = list(engs_used)
            if "sp" not in names:
                names.append("sp")
            for ename in names:
                hooks[ename](make(ename))


def _prog_simulate(self):
    streams = {}
    for i, o in enumerate(self.ops):
        streams.setdefault(o["eng"], []).append(i)
    pos = {e: 0 for e in streams}
    sem = {}
    progress = True
    ndone = 0
    while progress:
        progress = False
        for e, idxs in streams.items():
            while pos[e] < len(idxs):
                o = self.ops[idxs[pos[e]]]
                ok = True
                for d in o["deps"]:
                    od = self.ops[d]
                    if sem.get(od["semname"], 0) < od["val"]:
                        ok = False
                        break
                if not ok:
                    break
                if o["dma"] is not None:
                    sem[o["semname"]] = sem.get(o["semname"], 0) + 16
                elif o["signal"]:
                    sem[o["semname"]] = sem.get(o["semname"], 0) + 1
                pos[e] += 1
                ndone += 1
                progress = True
    stuck = {e: idxs[pos[e]] for e, idxs in streams.items() if pos[e] < len(idxs)}
    if stuck:
        msg = []
        for e, i in stuck.items():
            o = self.ops[i]
            need = [(self.ops[d]["semname"], self.ops[d]["val"], sem.get(self.ops[d]["semname"], 0), d) for d in o["deps"] if sem.get(self.ops[d]["semname"], 0) < self.ops[d]["val"]]
            msg.append((e, i, o.get("lkey"), need[:4]))
        raise RuntimeError("semaphore program deadlocks: %r" % (msg,))


Prog._simulate = _prog_simulate


class Mem:
    def __init__(self, big):
        self.big = big
        self.off = 0

    def mark(self):
        return self.off

    def reset(self, m):
        self.off = m

    def _shape(self, v, shape):
        if len(shape) == 2:
            return v
        if len(shape) == 3:
            return v.rearrange("p (a b) -> p a b", a=shape[1])
        if len(shape) == 4:
            return v.rearrange("p (a b c) -> p a b c", a=shape[1], b=shape[2])
        raise ValueError

    def f32(self, shape):
        n = int(np.prod(shape[1:]))
        a = self.off
        self.off += n + (n & 1)
        assert self.off <= SBUF_WORDS, ("SBUF overflow", self.off)
        return self._shape(self.big[0:shape[0], a:a + n], shape)

    def bf16(self, shape):
        n = int(np.prod(shape[1:]))
        nw = (n + 1) // 2
        nw += nw & 1
        a = self.off
        self.off += nw
        assert self.off <= SBUF_WORDS, ("SBUF overflow", self.off)
        v = self.big[0:shape[0], a:a + nw].bitcast(BF16)[:, 0:n]
        return self._shape(v, shape)


class Ring:
    def __init__(self, name, tiles):
        self.name, self.tiles, self.i = name, tiles, 0

    def next(self):
        k = self.i % len(self.tiles)
        self.i += 1
        return self.tiles[k], (self.name, k), "%s%d" % (self.name, k)


def t5_bucket(rel):
    nb = 16
    exact = 8
    n = np.abs(rel)
    large = exact + (np.log(np.maximum(n, 1) / exact) / math.log(1024 / exact) * (nb - exact)).astype(np.int32)
    large = np.minimum(large, nb - 1)
    return (rel > 0).astype(np.int32) * nb + np.where(n < exact, n, large)


def bias_onehots():
    oh = np.zeros((3, 2, 33, 256), np.float32)
    for g, d in enumerate(DILS):
        for j in range(255):
            if j <= 127:
                m = 63 - j
                oh[g, 0, t5_bucket(np.array(m * d)), j] = 1.0
            else:
                oh[g, 0, 32, j] = NEG
            if j >= 127:
                m = 191 - j
                oh[g, 1, t5_bucket(np.array(m * d)), j] = 1.0
            else:
                oh[g, 1, 32, j] = NEG
        oh[g, :, 32, 255] = NEG
    return oh


def build_program(debug=False, stop_after=99):
    nc = bass.Bass("TRN2", target_bir_lowering=False)
    din = lambda n, s, dt=F32: nc.dram_tensor(n, list(s), dt, kind="ExternalInput")
    x_h = din("x", [T, D])
    x = x_h.ap()
    jflag = din("jflag", [128, 1]).ap()
    w_in = din("w_in", [D, INC]).ap()
    prew = din("prew", [128, 8]).ap()
    convw = din("convw", [128, 16, 3]).ap()
    convb = din("convb", [128, 16]).ap()
    gbias_h = din("gbias", [1, 16])
    hnw_h = din("hnw", [1, 1024])
    w_pm = din("w_pm", [1024, 1024]).ap()
    w_pa = din("w_pa", [512, 1024]).ap()
    w_out = din("w_out", [1024, 1024]).ap()
    postw_h = din("postw", [1, 1024])
    relx = din("relx", [33, 24]).ap()
    ident_in = din("ident", [128, 128]).ap()
    tri_in = din("tri", [128, 128]).ap()
    oh_in = din("oh", [3, 2, 33, 256]).ap()
    y = nc.dram_tensor("y", [T, D], F32, kind="ExternalOutput").ap()

    skind = "ExternalOutput" if debug else "Internal"
    scr = lambda n, s, dt=BF16: nc.dram_tensor(n, list(s), dt, kind=skind)
    wb = scr("wb", [D, INC]).ap()
    qkT = scr("qkT", [2048, T]).ap()
    mv = scr("mv", [T, 1024]).ap()
    goz = scr("goz", [T, 1024]).ap()
    aqk = scr("aqk", [6, 512, T]).ap()
    av_ntile = []
    for g, d in enumerate(DILS):
        av_ntile.append(d * (8192 // d // 128 + 1) + d * (4096 // d // 128 + 1))
    av = [scr("av%d" % g, [av_ntile[g], 128, 512]).ap() for g in range(3)]
    azT = scr("azT", [512, T]).ap()
    gT = scr("gT", [2048, T]).ap()
    yAT = scr("yAT", [1024, T]).ap()
    yBT = scr("yBT", [512, T]).ap()
    gs_h = scr("gs", [48, 128, 256], F32)
    gates_dbg = scr("gates_dbg", [128, 96 * 16], F32).ap() if debug else None

    def av_tile_index(g, seg, r, jloc):
        d = DILS[g]
        L = SEG // d
        if seg < 2:
            return r * (2 * L // 128 + 1) + seg * (L // 128) + jloc
        return d * (2 * L // 128 + 1) + r * (L // 128 + 1) + jloc

    with contextlib.ExitStack() as st:
        big = st.enter_context(nc.sbuf_tensor("big", [128, SBUF_WORDS], F32))
        psum = st.enter_context(nc.psum_tensor("psum", [128, 8 * 512], F32))
        M = Mem(big)
        P = Prog(nc)
        BM = P.bankmap
        BM["pb0"] = (0,)
        for k_ in range(4):
            BM[("pacc", k_)] = (k_,)
            BM["g%d" % k_] = (k_,)
        for k_ in range(2):
            BM[("ptr", k_)] = (4 + k_,)
            BM[("bA", k_)] = (k_,)
            BM[("bY", k_)] = (k_,)
            BM[("bN", k_)] = (2 + k_,)
            BM[("bKV", k_)] = (4 + 2 * k_, 5 + 2 * k_)
            BM[("Sps0", k_)] = (k_,)
            BM[("Sps1", k_)] = (6 + k_,)
            BM[("accps0", k_)] = (2 + k_,)
            BM[("accps1", k_)] = (4 + k_,)
            BM[("pm", k_)] = (k_,)
            BM[("pa", k_)] = (2 + k_,)
            BM[("ops", k_)] = (4 + 2 * k_, 5 + 2 * k_)
        BM[("pacc2", 0)] = (4, 5)
        BM[("pacc2", 1)] = (6, 7)
        bank = lambda b: psum[:, b * 512:(b + 1) * 512]
        bank_bf = lambda b: psum[:, b * 512:(b + 1) * 512].bitcast(BF16)

        ident_f = M.f32([128, 128])
        ident_b = M.bf16([128, 128])
        tri_f = M.f32([128, 128])
        triT_f = M.f32([128, 128])
        mask_f = M.bf16([128, 128])
        mask_b = M.bf16([128, 128])
        ones_f = M.f32([128, 128])
        jcol = M.f32([128, 1])
        gates = M.f32([128, NCH, 16])
        wk = M.f32([128, NCH, 8])
        flo = M.f32([128, NCH, 8])
        dec = M.f32([128, NCH, 8])
        decj = M.f32([128, 8])
        hhalo = M.bf16([128, 8, 2])
        P.op("sp", lambda e: e.dma_start(out=ident_f, in_=ident_in), writes=["ident_f"], dma="c0")
        P.op("sp", lambda e: e.dma_start(out=tri_f, in_=tri_in), writes=["tri_f"], dma="c1")
        P.op("sp", lambda e: e.dma_start(out=jcol, in_=jflag), writes=["jcol"], dma="c2")
        P.op("dve", lambda e: e.tensor_copy(out=ident_b, in_=ident_f), reads=["ident_f"], writes=["ident_b"])
        P.op("dve", lambda e: e.tensor_copy(out=mask_f, in_=tri_f), reads=["tri_f"], writes=["mask_f"])
        P.op("dve", lambda e: e.memset(ones_f, 1.0), writes=["ones_f"])
        P.op("pe", lambda e: e.transpose(bank(0)[:, 0:128], tri_f, ident_f), reads=["tri_f", "ident_f"], writes=["pb0"])
        P.op("dve", lambda e: e.tensor_copy(out=triT_f, in_=bank(0)[:, 0:128]), reads=["pb0"], writes=["triT_f"])
        P.op("dve", lambda e: e.tensor_copy(out=mask_b, in_=triT_f), reads=["triT_f"], writes=["mask_b"])
        const_mark = M.mark()

        prew_sb = M.f32([128, 8])
        P.op("sp", lambda e: e.dma_start(out=prew_sb, in_=prew), writes=["prew"], dma="c0")
        wst = Ring("wst", [M.f32([128, 2048]) for _ in range(2)])
        wbs = Ring("wbs", [M.bf16([128, 2048]) for _ in range(2)])
        n = 0
        for k in range(8):
            for c0 in range(0, INC, 2048):
                w = min(2048, INC - c0)
                s_t, s_tok, s_key = wst.next()
                b_t, b_tok, b_key = wbs.next()
                P.op("sp", lambda e, s_t=s_t, k=k, c0=c0, w=w: e.dma_start(out=s_t[:, 0:w], in_=w_in[k * 128:(k + 1) * 128, c0:c0 + w]),
                     writes=[s_tok], dma=s_key)
                if n % 2 == 0:
                    P.op("dve", lambda e, s_t=s_t, b_t=b_t, k=k, w=w: e.tensor_scalar(out=b_t[:, 0:w], in0=s_t[:, 0:w], scalar1=prew_sb[:, k:k + 1], scalar2=None, op0=ALU.mult),
                         reads=[s_tok, "prew"], writes=[b_tok])
                else:
                    P.op("act", lambda e, s_t=s_t, b_t=b_t, k=k, w=w: e.activation(out=b_t[:, 0:w], in_=s_t[:, 0:w], func=AF.Copy, scale=prew_sb[:, k:k + 1]),
                         reads=[s_tok, "prew"], writes=[b_tok])
                P.op("sp", lambda e, b_t=b_t, k=k, c0=c0, w=w: e.dma_start(out=wb[k * 128:(k + 1) * 128, c0:c0 + w], in_=b_t[:, 0:w]),
                     reads=[b_tok], multi=["wb"], dma=b_key)
                n += 1
        P.barrier()
        M.reset(const_mark)
        if stop_after <= 0:
            P.emit()
            return nc

        cw_sb = M.f32([128, 16, 3])
        cb_sb = M.f32([128, 16])
        hnw_bc = M.f32([128, 1024])
        P.op("sp", lambda e: e.dma_start(out=cw_sb, in_=convw), writes=["cw"], dma="c0")
        P.op("sp", lambda e: e.dma_start(out=cb_sb, in_=convb), writes=["cb"], dma="c1")
        P.op("sp", lambda e: e.dma_start(out=hnw_bc, in_=hnw_h.ap().partition_broadcast(128)), writes=["hnw"], dma="c2")
        hT = M.bf16([128, 8, SEG + 2])
        xring = Ring("xt", [M.f32([128, 1024]) for _ in range(2)])
        xsring = Ring("xs", [M.bf16([128, 1024]) for _ in range(2)])
        ssr = Ring("ss", [M.f32([128, 2]) for _ in range(2)])
        wblk = Ring("wblk", [M.bf16([128, 8, 512]) for _ in range(3)])
        pre_r = Ring("pre", [M.f32([128, SEG + 2]) for _ in range(2)])
        cv1 = M.f32([128, SEG])
        fmst = Ring("fmst", [M.bf16([128, SEG]) for _ in range(2)])
        tmst = Ring("tmst", [M.bf16([128, 512]) for _ in range(3)])
        sg1 = M.f32([128, 1024])
        sqj = sg1
        sg2 = M.f32([128, 512])
        sg3 = M.f32([128, 512])
        pacc = Ring("pacc", [bank(b) for b in (0, 1, 2, 3)])
        ptr = Ring("ptr", [bank_bf(b) for b in (4, 5)])
        pacc2 = Ring("pacc2", [psum[:, 4 * 512:6 * 512], psum[:, 6 * 512:8 * 512]])

        def rmsnorm_to_hT(rows, nrow, dst_fn, tag):
            xt, xtok, xkey = xring.next()
            xs, xstok, _ = xsring.next()
            ss, sstok, _ = ssr.next()
            pt, pttok, _ = ptr.next()
            P.op("sp", lambda e: e.dma_start(out=xt[0:nrow, :], in_=rows), writes=[xtok], dma=xkey)
            P.op("act", lambda e: e.activation(out=sqj[0:nrow, :], in_=xt[0:nrow, :], func=AF.Square, accum_out=ss[0:nrow, 0:1]),
                 reads=[xtok], writes=[sstok, "sg1"])
            P.op("act", lambda e: e.activation(out=ss[0:nrow, 1:2], in_=ss[0:nrow, 0:1], func=AF.Ln, scale=1.0 / D, bias=EPS),
                 reads=[sstok], writes=[sstok])
            P.op("act", lambda e: e.activation(out=ss[0:nrow, 0:1], in_=ss[0:nrow, 1:2], func=AF.Exp, scale=-0.5),
                 reads=[sstok], writes=[sstok])
            P.op("dve", lambda e: e.tensor_scalar(out=xs[0:nrow, :], in0=xt[0:nrow, :], scalar1=ss[0:nrow, 0:1], scalar2=None, op0=ALU.mult),
                 reads=[xtok, sstok], writes=[xstok])
            for k in range(8):
                P.op("pe", lambda e, k=k: e.transpose(pt[:, k * 128:k * 128 + nrow], xs[0:nrow, k * 128:(k + 1) * 128], ident_b[0:nrow, 0:nrow]),
                     reads=[xstok, "ident_b"], writes=[pttok])
            return pt, pttok

        pt, pttok = rmsnorm_to_hT(x[SEG - 1:SEG + 1, :], 2, None, "halo")
        P.op("dve", lambda e: e.tensor_scalar(out=hhalo, in0=pt.rearrange("p (k t) -> p k t", k=8)[:, :, 0:2], scalar1=jcol[:, 0:1], scalar2=None, op0=ALU.mult),
             reads=[pttok, "jcol"], writes=["hhalo"])

        evac_rr = [0]

        def evac_engine():
            evac_rr[0] += 1
            return "act" if evac_rr[0] % 2 == 0 else "dve"

        wsched = []
        for c0_ in range(C_MQK, C_MV, 512):
            wsched.append((c0_, 512))
        for g_ in range(3):
            wsched.append((C_AQKV + g_ * 512, 512))
            wsched.append((C_AQKV + (3 + g_) * 512, 512))
        wsched.append((C_AZ, 512))
        for c0_ in range(C_GATE, INC, 512):
            wsched.append((c0_, 512))
        for j_ in range(2):
            wsched.append((C_MV + j_ * 512, 512))
        for j_ in range(2):
            wsched.append((C_MO + j_ * 512, 512))
            wsched.append((C_MZ + j_ * 512, 512))
        wsched.append((C_MI, 16))
        for g_ in range(3):
            wsched.append((C_AQKV + (6 + g_) * 512, 512))
        wsched = wsched * NSEG
        wstate = {"next": 0, "issued": []}

        def _issue_next():
            if wstate["next"] >= len(wsched):
                return
            c0, w = wsched[wstate["next"]]
            wstate["next"] += 1
            wt, wtok, wkey = wblk.next()
            P.op("sp", lambda e: e.dma_start(out=wt[:, :, 0:w], in_=wb.rearrange("(k p) c -> p k c", p=128)[:, :, c0:c0 + w]),
                 reads=["wb"], writes=[wtok], dma=wkey)
            wstate["issued"].append((c0, w, wt, wtok))

        def load_wblk(c0, w):
            if not wstate["issued"]:
                _issue_next()
            c0i, wi, wt, wtok = wstate["issued"].pop(0)
            assert (c0i, wi) == (c0, w), ((c0i, wi), (c0, w))
            if not wstate["issued"]:
                _issue_next()
            return wt, wtok

        for seg in range(NSEG):
            t0 = seg * SEG
            for i in range(SEG // 128):
                pt, pttok = rmsnorm_to_hT(x[t0 + i * 128:t0 + (i + 1) * 128, :], 128, None, "h")
                eng = evac_engine()
                dst = hT[:, :, 1 + i * 128:1 + (i + 1) * 128]
                src = pt.rearrange("p (k t) -> p k t", k=8)
                if eng == "act":
                    P.op("act", lambda e, dst=dst, src=src: e.activation(out=dst, in_=src, func=AF.Copy), reads=[pttok], multi=["hT"])
                else:
                    P.op("dve", lambda e, dst=dst, src=src: e.tensor_copy(out=dst, in_=src), reads=[pttok], multi=["hT"])
            if seg == 0:
                P.op("pool", lambda e: e.memset(hT[:, :, 0:1], 0.0), multi=["hT"])
                P.op("pool", lambda e: e.tensor_copy(out=hT[:, :, SEG + 1:SEG + 2], in_=hhalo[:, :, 1:2]), reads=["hhalo"], multi=["hT"])
            elif seg == 1:
                P.op("pool", lambda e: e.tensor_copy(out=hT[:, :, 0:1], in_=hhalo[:, :, 0:1]), reads=["hhalo"], multi=["hT"])
                P.op("pool", lambda e: e.memset(hT[:, :, SEG + 1:SEG + 2], 0.0), multi=["hT"])
            else:
                P.op("pool", lambda e: e.memset(hT[:, :, 0:1], 0.0), multi=["hT"])
                P.op("pool", lambda e: e.memset(hT[:, :, SEG + 1:SEG + 2], 0.0), multi=["hT"])

            def fm_block(c0, nchunk, kind, arg=None):
                wt, wtok = load_wblk(c0, nchunk * 128)
                for cc in range(nchunk):
                    col = c0 + cc * 128
                    if kind == "mqk":
                        chunk = (col - C_MQK) // 128
                        pre, pretok, _ = pre_r.next()
                        for tt in range(8):
                            pa, patok, _ = pacc.next()
                            for k in range(8):
                                P.op("pe", lambda e, pa=pa, k=k, cc=cc, tt=tt: e.matmul(pa, lhsT=wt[:, k, cc * 128:(cc + 1) * 128], rhs=hT[:, k, 1 + tt * 512:1 + (tt + 1) * 512], start=(k == 0), stop=(k == 7)),
                                     reads=[wtok, "hT"], writes=[patok])
                            eng = evac_engine()
                            dst = pre[:, 1 + tt * 512:1 + (tt + 1) * 512]
                            if eng == "act":
                                P.op("act", lambda e, pa=pa, dst=dst: e.activation(out=dst, in_=pa, func=AF.Copy), reads=[patok], multi=[pretok])
                            else:
                                P.op("dve", lambda e, pa=pa, dst=dst: e.tensor_copy(out=dst, in_=pa), reads=[patok], multi=[pretok])
                        pa, patok, _ = pacc.next()
                        for k in range(8):
                            P.op("pe", lambda e, pa=pa, k=k, cc=cc: e.matmul(pa[:, 0:2], lhsT=wt[:, k, cc * 128:(cc + 1) * 128], rhs=hT[:, k, 0:SEG + 2:SEG + 1], start=(k == 0), stop=(k == 7)),
                                 reads=[wtok, "hT"], writes=[patok])
                        P.op("dve", lambda e, pa=pa, pre=pre: e.tensor_copy(out=pre[:, 0:SEG + 2:SEG + 1], in_=pa[:, 0:2]), reads=[patok], multi=[pretok])
                        P.op("dve", lambda e, ch=chunk, pre=pre: e.tensor_scalar(out=cv1, in0=pre[:, 1:SEG + 1], scalar1=cw_sb[:, ch, 1:2], scalar2=cb_sb[:, ch:ch + 1], op0=ALU.mult, op1=ALU.add),
                             reads=[pretok, "cw", "cb"], writes=["cv1"])
                        P.op("dve", lambda e, ch=chunk, pre=pre: e.scalar_tensor_tensor(out=cv1, in0=pre[:, 0:SEG], scalar=cw_sb[:, ch, 0:1], in1=cv1, op0=ALU.mult, op1=ALU.add),
                             reads=[pretok, "cw", "cv1"], writes=["cv1"])
                        P.op("dve", lambda e, ch=chunk, pre=pre: e.scalar_tensor_tensor(out=cv1, in0=pre[:, 2:SEG + 2], scalar=cw_sb[:, ch, 2:3], in1=cv1, op0=ALU.mult, op1=ALU.add),
                             reads=[pretok, "cw", "cv1"], writes=["cv1"])
                        fs, fstok, fskey = fmst.next()
                        P.op("act", lambda e, fs=fs: e.activation(out=fs, in_=cv1, func=AF.Silu), reads=["cv1"], writes=[fstok])
                        P.op("sp", lambda e, fs=fs, col=col, t0=t0: e.dma_start(out=qkT[col:col + 128, t0:t0 + SEG], in_=fs),
                             reads=[fstok], multi=["qkT"], dma=fskey)
                    else:
                        fs, fstok, fskey = fmst.next()
                        for tt in range(8):
                            pa, patok, _ = pacc.next()
                            for k in range(8):
                                P.op("pe", lambda e, pa=pa, k=k, cc=cc, tt=tt: e.matmul(pa, lhsT=wt[:, k, cc * 128:(cc + 1) * 128], rhs=hT[:, k, 1 + tt * 512:1 + (tt + 1) * 512], start=(k == 0), stop=(k == 7)),
                                     reads=[wtok, "hT"], writes=[patok])
                            if kind == "aqk":
                                g, scale = arg
                                d = DILS[g]
                                if d == 1:
                                    dst = fs[:, tt * 512:(tt + 1) * 512]
                                    src = pa
                                else:
                                    na = 512 // d
                                    dst = fs.rearrange("p (r a) -> p a r", r=d)[:, tt * na:(tt + 1) * na, :]
                                    src = pa.rearrange("p (a r) -> p a r", r=d)
                                eng = evac_engine()
                                if eng == "act":
                                    P.op("act", lambda e, dst=dst, src=src, scale=scale: e.activation(out=dst, in_=src, func=AF.Copy, scale=scale), reads=[patok], multi=[fstok])
                                else:
                                    P.op("dve", lambda e, dst=dst, src=src, scale=scale: e.tensor_scalar(out=dst, in0=src, scalar1=scale, scalar2=None, op0=ALU.mult), reads=[patok], multi=[fstok])
                            else:
                                func = AF.Silu if kind == "az" else AF.Sigmoid
                                dst = fs[:, tt * 512:(tt + 1) * 512]
                                P.op("act", lambda e, dst=dst, pa=pa, func=func: e.activation(out=dst, in_=pa, func=func), reads=[patok], multi=[fstok])
                        if kind == "aqk":
                            idx = (col - C_AQKV) // 512
                            row = (col - C_AQKV) % 512
                            dstd = aqk[idx, row:row + 128, t0:t0 + SEG]
                            mt = "aqk"
                        elif kind == "az":
                            dstd = azT[col - C_AZ:col - C_AZ + 128, t0:t0 + SEG]
                            mt = "azT"
                        else:
                            dstd = gT[col - C_GATE:col - C_GATE + 128, t0:t0 + SEG]
                            mt = "gT"
                        P.op("sp", lambda e, fs=fs, dstd=dstd: e.dma_start(out=dstd, in_=fs), reads=[fstok], multi=[mt], dma=fskey)

            for c0 in range(C_MQK, C_MV, 512):
                fm_block(c0, 4, "mqk")
            for g in range(3):
                fm_block(C_AQKV + g * 512, 4, "aqk", (g, 0.125))
                fm_block(C_AQKV + (3 + g) * 512, 4, "aqk", (g, 1.0))
            fm_block(C_AZ, 4, "az")
            for c0 in range(C_GATE, INC, 512):
                fm_block(c0, 4, "gate")

            for j in range(2):
                wt, wtok = load_wblk(C_MV + j * 512, 512)
                for i in range(32):
                    pa, patok, _ = pacc.next()
                    for k in range(8):
                        P.op("pe", lambda e, pa=pa, k=k, i=i, wt=wt: e.matmul(pa, lhsT=hT[:, k, 1 + i * 128:1 + (i + 1) * 128], rhs=wt[:, k, :], start=(k == 0), stop=(k == 7)),
                             reads=[wtok, "hT"], writes=[patok])
                    ts_, tstok, tskey = tmst.next()
                    eng = evac_engine()
                    if eng == "act":
                        P.op("act", lambda e, ts_=ts_, pa=pa: e.activation(out=ts_, in_=pa, func=AF.Copy), reads=[patok], writes=[tstok])
                    else:
                        P.op("dve", lambda e, ts_=ts_, pa=pa: e.tensor_copy(out=ts_, in_=pa), reads=[patok], writes=[tstok])
                    P.op("sp", lambda e, ts_=ts_, i=i, j=j, t0=t0: e.dma_start(out=mv[t0 + i * 128:t0 + (i + 1) * 128, j * 512:(j + 1) * 512], in_=ts_),
                         reads=[tstok], multi=["mv"], dma=tskey)
            for j in range(2):
                wo, wotok = load_wblk(C_MO + j * 512, 512)
                wz, wztok = load_wblk(C_MZ + j * 512, 512)
                for i in range(32):
                    pa, patok, _ = pacc2.next()
                    for half, (wt_, wtk_) in enumerate(((wo, wotok), (wz, wztok))):
                        for k in range(8):
                            P.op("pe", lambda e, pa=pa, k=k, i=i, wt_=wt_, half=half: e.matmul(pa[:, half * 512:(half + 1) * 512], lhsT=hT[:, k, 1 + i * 128:1 + (i + 1) * 128], rhs=wt_[:, k, :], start=(k == 0), stop=(k == 7)),
                                 reads=[wtk_, "hT"], writes=[patok])
                    P.op("act", lambda e, pa=pa: e.activation(out=sg1, in_=pa, func=AF.Sigmoid), reads=[patok], writes=["sg1"])
                    P.op("dve", lambda e: e.tensor_tensor(out=sg2, in0=sg1[:, 0:512], in1=sg1[:, 512:1024], op=ALU.mult), reads=["sg1"], writes=["sg2"])
                    P.op("dve", lambda e, j=j: e.tensor_tensor(out=sg3, in0=sg2, in1=hnw_bc[:, j * 512:(j + 1) * 512], op=ALU.mult), reads=["sg2", "hnw"], writes=["sg3"])
                    ts_, tstok, tskey = tmst.next()
                    P.op("dve", lambda e, pa=pa, ts_=ts_: e.tensor_tensor(out=ts_, in0=pa[:, 512:1024], in1=sg3, op=ALU.mult), reads=[patok, "sg3"], writes=[tstok])
                    P.op("sp", lambda e, ts_=ts_, i=i, j=j, t0=t0: e.dma_start(out=goz[t0 + i * 128:t0 + (i + 1) * 128, j * 512:(j + 1) * 512], in_=ts_),
                         reads=[tstok], multi=["goz"], dma=tskey)
            wt, wtok = load_wblk(C_MI, 16)
            for i in range(32):
                pa, patok, _ = pacc.next()
                for k in range(8):
                    P.op("pe", lambda e, pa=pa, k=k, i=i, wt=wt: e.matmul(pa[:, 0:16], lhsT=hT[:, k, 1 + i * 128:1 + (i + 1) * 128], rhs=wt[:, k, 0:16], start=(k == 0), stop=(k == 7)),
                         reads=[wtok, "hT"], writes=[patok])
                P.op("dve", lambda e, pa=pa, i=i, seg=seg: e.tensor_copy(out=gates[:, seg * 32 + i, :], in_=pa[:, 0:16]), reads=[patok], multi=["gates"])
            for g, d in enumerate(DILS):
                wt, wtok = load_wblk(C_AQKV + (6 + g) * 512, 512)
                L = SEG // d
                nj = L // 128
                for r in range(d):
                    for jl in range(nj + 1):
                        if jl == 0:
                            p0, p1, a0 = 64, 128, 0
                        elif jl == nj:
                            p0, p1, a0 = 0, 64, L - 64
                        else:
                            p0, p1, a0 = 0, 128, 64 + 128 * (jl - 1)
                        cnt = p1 - p0
                        c_start = 1 + a0 * d + r
                        pa, patok, _ = pacc.next()
                        for k in range(8):
                            P.op("pe", lambda e, pa=pa, k=k, wt=wt, p0=p0, p1=p1, c_start=c_start, cnt=cnt, d=d: e.matmul(pa[p0:p1, :], lhsT=hT[:, k, c_start:c_start + (cnt - 1) * d + 1:d], rhs=wt[:, k, :], start=(k == 0), stop=(k == 7)),
                                 reads=[wtok, "hT"], writes=[patok])
                        ts_, tstok, tskey = tmst.next()
                        eng = evac_engine()
                        if eng == "act":
                            P.op("act", lambda e, ts_=ts_, pa=pa, p0=p0, p1=p1: e.activation(out=ts_[p0:p1, :], in_=pa[p0:p1, :], func=AF.Copy), reads=[patok], writes=[tstok])
                        else:
                            P.op("dve", lambda e, ts_=ts_, pa=pa, p0=p0, p1=p1: e.tensor_copy(out=ts_[p0:p1, :], in_=pa[p0:p1, :]), reads=[patok], writes=[tstok])
                        ti = av_tile_index(g, seg, r, jl)
                        P.op("sp", lambda e, ts_=ts_, g=g, ti=ti, p0=p0, p1=p1: e.dma_start(out=av[g][ti, p0:p1, :], in_=ts_[p0:p1, :]),
                             reads=[tstok], multi=["av%d" % g], dma=tskey)
            P.barrier()
        if debug:
            P.op("sp", lambda e: e.dma_start(out=gates_dbg, in_=gates.rearrange("p c g -> p (c g)")), reads=["gates"], dma="c0")
        M.reset(const_mark)
        if stop_after <= 1:
            P.emit()
            return nc
        LN16 = math.log(16.0)
        cneg = M.f32([128, 1])
        cone = M.f32([128, 1])
        ceps = M.f32([128, 1])
        P.op("pool", lambda e: e.memset(cneg, -LN16), writes=["cneg"])
        P.op("pool", lambda e: e.memset(cone, 1.0), writes=["cone"])
        P.op("pool", lambda e: e.memset(ceps, EPS), writes=["ceps"])
        ph2_mark = M.mark()
        gb = M.f32([128, NCH, 16])
        gsum = M.f32([128, NCH, 16])
        nl = M.f32([128, NCH, 8])
        uu = M.f32([128, NCH, 8])
        P.op("sp", lambda e: e.dma_start(out=gb, in_=bass.AP(gbias_h, 0, [[0, 128], [0, NCH], [1, 16]])), writes=["gb"], dma="c0")
        P.op("dve", lambda e: e.tensor_tensor(out=gsum, in0=gates, in1=gb, op=ALU.add), reads=["gates", "gb"], writes=["gsum"])
        P.op("act", lambda e: e.activation(out=nl, in_=gsum[:, :, 8:16], func=AF.Exp, scale=-1.0), reads=["gsum"], writes=["nl"])
        P.op("act", lambda e: e.activation(out=nl, in_=nl, func=AF.Ln, bias=cone[:, 0:1]), reads=["nl", "cone"], writes=["nl"])
        b3 = lambda b: bank(b)[:, 0:384].rearrange("p (c g) -> p c g", g=4)
        P.op("pe", lambda e: e.matmul(bank(0)[:, 0:384], lhsT=tri_f, rhs=nl[:, :, 0:4], start=True, stop=True), reads=["tri_f", "nl"], writes=["g0"])
        P.op("pe", lambda e: e.matmul(bank(1)[:, 0:384], lhsT=triT_f, rhs=nl[:, :, 4:8], start=True, stop=True), reads=["triT_f", "nl"], writes=["g1"])
        P.op("pe", lambda e: e.matmul(bank(2)[:, 0:384], lhsT=ones_f, rhs=nl[:, :, 0:4], start=True, stop=True), reads=["ones_f", "nl"], writes=["g2"])
        P.op("pe", lambda e: e.matmul(bank(3)[:, 0:384], lhsT=ones_f, rhs=nl[:, :, 4:8], start=True, stop=True), reads=["ones_f", "nl"], writes=["g3"])
        for dr in range(2):
            sl = slice(dr * 4, dr * 4 + 4)
            P.op("dve", lambda e, dr=dr, sl=sl: e.tensor_tensor(out=uu[:, :, sl], in0=gsum[:, :, sl], in1=b3(dr), op=ALU.add), reads=["gsum", "g%d" % dr], writes=["uu%d" % dr])
            P.op("act", lambda e, sl=sl: e.activation(out=wk[:, :, sl], in_=uu[:, :, sl], func=AF.Exp, bias=cneg[:, 0:1]), reads=["uu%d" % dr, "cneg"], writes=["wk%d" % dr])
            P.op("act", lambda e, dr=dr, sl=sl: e.activation(out=flo[:, :, sl], in_=b3(dr), func=AF.Exp), reads=["g%d" % dr], writes=["flo%d" % dr])
            P.op("act", lambda e, dr=dr, sl=sl: e.activation(out=dec[:, :, sl], in_=b3(2 + dr), func=AF.Exp, scale=-1.0), reads=["g%d" % (2 + dr)], writes=["dec%d" % dr])
        P.op("dve", lambda e: e.tensor_scalar(out=decj[:, 0:4], in0=dec[:, 31, 0:4], scalar1=jcol[:, 0:1], scalar2=None, op0=ALU.mult), reads=["dec0", "jcol"], writes=["decj0"])
        P.op("dve", lambda e: e.tensor_scalar(out=decj[:, 4:8], in0=dec[:, 32, 4:8], scalar1=jcol[:, 0:1], scalar2=None, op0=ALU.mult), reads=["dec1", "jcol"], writes=["decj1"])
        gtoks = ["wk0", "wk1", "flo0", "flo1", "dec0", "dec1", "decj0", "decj1"]
        P.barrier()
        M.reset(ph2_mark)
        if stop_after <= 2:
            if debug:
                for nm, tl in (("wk", wk), ("flo", flo), ("dec", dec)):
                    dd = nc.dram_tensor("dbg_" + nm, [128, NCH * 8], F32, kind="ExternalOutput").ap()
                    P.op("sp", lambda e, dd=dd, tl=tl: e.dma_start(out=dd, in_=tl.rearrange("p c g -> p (c g)")), reads=gtoks, dma="c0")
            P.emit()
            return nc

        hpart = M.f32([128, 64, 256])
        NS = 4
        qk_r = [Ring("qk%d" % dr, [M.bf16([128, 4, 128]) for _ in range(NS)]) for dr in range(2)]
        vx_r = [Ring("vx%d" % dr, [M.bf16([128, 258]) for _ in range(NS)]) for dr in range(2)]
        gz_r = [Ring("gz%d" % dr, [M.bf16([128, 256]) for _ in range(NS)]) for dr in range(2)]
        kw_r = [Ring("kw%d" % dr, [M.bf16([128, 256]) for _ in range(3)]) for dr in range(2)]
        sc_r = [Ring("sc%d" % dr, [M.bf16([128, 128]) for _ in range(3)]) for dr in range(2)]
        Tst = [M.f32([128, 2, 257]) for _ in range(2)]
        Cb = [M.bf16([128, 2, 258]) for _ in range(2)]
        sm_r = [Ring("sm%d" % dr, [M.f32([128, 4]) for _ in range(3)]) for dr in range(2)]
        hh_r = Ring("hh", [M.f32([128, 256]) for _ in range(4)])
        ns_r = [Ring("ns%d" % dr, [M.f32([128, 258]) for _ in range(3)]) for dr in range(2)]
        yv_r = Ring("yv", [M.bf16([128, 256]) for _ in range(6)])
        yst_r = Ring("yst", [M.bf16([128, 2, 128]) for _ in range(4)])
        sq3 = M.bf16([128, 256])
        for dr in range(2):
            for k2 in range(NS):
                vt_, _, _ = vx_r[dr].next()
                P.op("pool", lambda e, vt_=vt_: e.memset(vt_[:, 256:257], 1.0), multi=[("vx%d" % dr, k2)])
        qkT_v = qkT.rearrange("(a p) t -> p a t", p=128)
        yAT_v = yAT.rearrange("(a p) t -> p a t", p=128)
        maskd = [mask_f, mask_b]
        def A_regions(dr, par):
            base = 0
            return (bank_bf(dr)[:, 2 * base:2 * base + 256], bank(dr)[:, base + 128:base + 256], ("bA", dr))

        for h in range(4):
            for (cbase, nchk) in ((0, 64), (64, 32)):
                for dr in range(2):
                    P.op("pool", lambda e, dr=dr: e.memset(Tst[dr], 0.0), writes=[("T", dr)])
                    P.op("pool", lambda e, dr=dr: e.memset(Cb[dr], 0.0), writes=[("Cb", dr)])
                U = {}

                def chunk_of(i, dr):
                    return cbase + i if dr == 0 else cbase + nchk - 1 - i

                def loads(i):
                    fin = i >= nchk // 2
                    for dr in range(2):
                        c = chunk_of(i, dr)
                        qk_t, qktok, qkkey = qk_r[dr].next()
                        vx_t, vxtok, vxkey = vx_r[dr].next()
                        u = dict(c=c, qk=qk_t, qktok=qktok, vx=vx_t, vxtok=vxtok, fin=fin)
                        P.op("sp", lambda e, qk_t=qk_t, c=c, h=h: e.dma_start(out=qk_t[:, 0:2, :], in_=qkT_v[:, 2 * h:2 * h + 2, c * 128:(c + 1) * 128]), reads=["qkT"], multi=[qktok], dma=qkkey)
                        P.op("sp", lambda e, qk_t=qk_t, c=c, h=h: e.dma_start(out=qk_t[:, 2:4, :], in_=qkT_v[:, 8 + 2 * h:8 + 2 * h + 2, c * 128:(c + 1) * 128]), reads=["qkT"], multi=[qktok], dma=qkkey)
                        P.op("sp", lambda e, vx_t=vx_t, c=c, h=h: e.dma_start(out=vx_t[:, 0:256], in_=mv[c * 128:(c + 1) * 128, h * 256:(h + 1) * 256]), reads=["mv"], multi=[vxtok], dma=vxkey)
                        if fin:
                            gz_t, gztok, gzkey = gz_r[dr].next()
                            u.update(gz=gz_t, gztok=gztok)
                            P.op("sp", lambda e, gz_t=gz_t, c=c, h=h: e.dma_start(out=gz_t, in_=goz[c * 128:(c + 1) * 128, h * 256:(h + 1) * 256]), reads=["goz"], writes=[gztok], dma=gzkey)
                        U[(i, dr)] = u

                def stage12(i):
                    for dr in range(2):
                        u = U[(i, dr)]
                        c, qk_t, qktok = u["c"], u["qk"], u["qktok"]
                        lane = dr * 4 + h
                        ktp, stp, tA = A_regions(dr, i % 2)
                        for hf in range(2):
                            P.op("pe", lambda e, hf=hf, qk_t=qk_t, ktp=ktp: e.transpose(ktp[:, hf * 128:(hf + 1) * 128], qk_t[:, 2 + hf, :], ident_b), reads=[qktok, "ident_b"], writes=[tA])
                        for hf in range(2):
                            P.op("pe", lambda e, hf=hf, qk_t=qk_t, stp=stp: e.matmul(stp, lhsT=qk_t[:, 2 + hf, :], rhs=qk_t[:, hf, :], start=(hf == 0), stop=(hf == 1)), reads=[qktok], writes=[tA])
                    for dr in range(2):
                        u = U[(i, dr)]
                        c = u["c"]
                        lane = dr * 4 + h
                        ktp, stp, tA = A_regions(dr, i % 2)
                        kw_t, kwtok, _ = kw_r[dr].next()
                        sc_t, sctok, _ = sc_r[dr].next()
                        u.update(kw=kw_t, kwtok=kwtok, sc=sc_t, sctok=sctok)
                        wcol = wk[:, c, lane:lane + 1]
                        P.op("dve", lambda e, kw_t=kw_t, ktp=ktp, wcol=wcol: e.tensor_scalar(out=kw_t, in0=ktp, scalar1=wcol, scalar2=None, op0=ALU.mult), reads=[tA, "wk%d" % dr], writes=[kwtok])
                        P.op("dve", lambda e, sc_t=sc_t, stp=stp, wcol=wcol, dr=dr: e.scalar_tensor_tensor(out=sc_t, in0=stp, scalar=wcol, in1=maskd[dr], op0=ALU.mult, op1=ALU.mult), reads=[tA, "wk%d" % dr, "mask_f", "mask_b"], writes=[sctok])

                def stage3(i):
                    for dr in range(2):
                        u = U[(i, dr)]
                        kvp = psum[:, (4 + 2 * dr) * 512:(6 + 2 * dr) * 512]
                        for hf in range(2):
                            P.op("pe", lambda e, hf=hf, kw_t=u["kw"], vx_t=u["vx"], kvp=kvp: e.matmul(kvp[:, hf * 512:hf * 512 + 257], lhsT=kw_t[:, hf * 128:(hf + 1) * 128], rhs=vx_t[:, 0:257], start=True, stop=True), reads=[u["kwtok"], u["vxtok"]], writes=[("bKV", dr)])
                    for dr in range(2):
                        u = U[(i, dr)]
                        bN = bank(2 + dr)
                        P.op("pe", lambda e, sc_t=u["sc"], vx_t=u["vx"], bN=bN: e.matmul(bN[:, 0:257], lhsT=sc_t, rhs=vx_t[:, 0:257], start=True, stop=False), reads=[u["sctok"], u["vxtok"]], writes=[("bN", dr)])
                        for hf in range(2):
                            P.op("pe", lambda e, hf=hf, qk_t=u["qk"], bN=bN, dr=dr: e.matmul(bN[:, 0:257], lhsT=qk_t[:, hf, :], rhs=Cb[dr][:, hf, 0:257], start=False, stop=(hf == 1)), reads=[u["qktok"], ("Cb", dr)], writes=[("bN", dr)])

                def stage4(i):
                    for dr in range(2):
                        u = U[(i, dr)]
                        c = u["c"]
                        lane = dr * 4 + h
                        cprev = c - 1 if dr == 0 else c + 1
                        kvp = psum[:, (4 + 2 * dr) * 512:(6 + 2 * dr) * 512]
                        joined_prev = (cbase == 0) and ((dr == 0 and c == 32) or (dr == 1 and c == 31))
                        joined_next = (cbase == 0) and ((dr == 0 and c == 31) or (dr == 1 and c == 32))
                        dprev = decj[:, lane:lane + 1] if joined_prev else dec[:, (c if i == 0 else cprev), lane:lane + 1]
                        dnext = decj[:, lane:lane + 1] if joined_next else dec[:, c, lane:lane + 1]
                        kv3 = kvp.rearrange("p (a b) -> p a b", a=2)[:, :, 0:257]
                        P.op("dve", lambda e, dr=dr, dprev=dprev, kv3=kv3: e.scalar_tensor_tensor(out=Tst[dr], in0=Tst[dr], scalar=dprev, in1=kv3, op0=ALU.mult, op1=ALU.add), reads=[("T", dr), ("bKV", dr), "dec%d" % dr, "decj%d" % dr], writes=[("T", dr)])
                        if i < nchk - 1:
                            P.op("act", lambda e, dr=dr, dnext=dnext: e.activation(out=Cb[dr][:, :, 0:257], in_=Tst[dr], func=AF.Copy, scale=dnext), reads=[("T", dr), "dec%d" % dr, "decj%d" % dr], writes=[("Cb", dr)])

                def stage5(i):
                    for dr in range(2):
                        u = U[(i, dr)]
                        c, fin = u["c"], u["fin"]
                        lane = dr * 4 + h
                        bN = bank(2 + dr)
                        tN = ("bN", dr)
                        sm, smtok, _ = sm_r[dr].next()
                        ns, nstok, _ = ns_r[dr].next()
                        P.op("act", lambda e, ns=ns, bN=bN: e.activation(out=ns[:, 0:257], in_=bN[:, 0:257], func=AF.Copy), reads=[tN], writes=[nstok])
                        P.op("dve", lambda e, sm=sm, ns=ns: e.scalar_tensor_tensor(out=sm[:, 0:1], in0=ns[:, 256:257], scalar=-1.0, in1=ns[:, 256:257], op0=ALU.mult, op1=ALU.max), reads=[nstok], writes=[smtok])
                        P.op("dve", lambda e, sm=sm, c=c, lane=lane: e.tensor_tensor(out=sm[:, 1:2], in0=sm[:, 0:1], in1=flo[:, c, lane:lane + 1], op=ALU.max), reads=[smtok, "flo%d" % dr], writes=[smtok])
                        P.op("dve", lambda e, sm=sm: e.reciprocal(out=sm[:, 2:3], in_=sm[:, 1:2]), reads=[smtok], writes=[smtok])
                        hp_ = hpart[:, c - cbase, :]
                        hptok = ("hpart", c - cbase)
                        if not fin:
                            P.op("pool", lambda e, hp_=hp_, ns=ns, sm=sm: e.tensor_scalar(out=hp_, in0=ns[:, 0:256], scalar1=sm[:, 2:3], scalar2=None, op0=ALU.mult), reads=[nstok, smtok], writes=[hptok])
                        else:
                            hh, hhtok, _ = hh_r.next()
                            yv, yvtok, _ = yv_r.next()
                            gz_t, gztok = u["gz"], u["gztok"]
                            P.op("dve", lambda e, hh=hh, ns=ns, sm=sm, hp_=hp_: e.scalar_tensor_tensor(out=hh, in0=ns[:, 0:256], scalar=sm[:, 2:3], in1=hp_, op0=ALU.mult, op1=ALU.add), reads=[nstok, smtok, hptok], writes=[hhtok])
                            P.op("act", lambda e, hh=hh, sm=sm: e.activation(out=sq3, in_=hh, func=AF.Square, accum_out=sm[:, 3:4]), reads=[hhtok, smtok], writes=[smtok, "sq3"])
                            P.op("act", lambda e, sm=sm: e.activation(out=sm[:, 0:1], in_=sm[:, 3:4], func=AF.Ln, scale=1.0 / 256, bias=ceps[:, 0:1]), reads=[smtok, "ceps"], writes=[smtok])
                            P.op("act", lambda e, sm=sm: e.activation(out=sm[:, 1:2], in_=sm[:, 0:1], func=AF.Exp, scale=-0.5), reads=[smtok], writes=[smtok])
                            P.op("dve", lambda e, yv=yv, hh=hh, sm=sm, gz_t=gz_t: e.scalar_tensor_tensor(out=yv, in0=hh, scalar=sm[:, 1:2], in1=gz_t, op0=ALU.mult, op1=ALU.mult), reads=[hhtok, smtok, gztok], writes=[yvtok])
                            u.update(yv=yv, yvtok=yvtok)

                def stage6(i):
                    for dr in range(2):
                        u = U[(i, dr)]
                        if not u["fin"]:
                            continue
                        c = u["c"]
                        yv, yvtok = u["yv"], u["yvtok"]
                        yTp = bank_bf(dr)[:, 512:768]
                        yst, ysttok, ystkey = yst_r.next()
                        for hf in range(2):
                            P.op("pe", lambda e, hf=hf, yv=yv, yTp=yTp: e.transpose(yTp[:, hf * 128:(hf + 1) * 128], yv[:, hf * 128:(hf + 1) * 128], ident_b), reads=[yvtok, "ident_b"], writes=[("bY", dr)])
                        P.op("act", lambda e, yst=yst, yTp=yTp: e.activation(out=yst, in_=yTp.rearrange("p (a b) -> p a b", a=2), func=AF.Copy), reads=[("bY", dr)], writes=[ysttok])
                        P.op("act", lambda e, yst=yst, c=c, h=h: e.dma_start(out=yAT_v[:, 2 * h:2 * h + 2, c * 128:(c + 1) * 128], in_=yst), reads=[ysttok], multi=["yAT"], dma=ystkey)

                PF = 2
                for i in range(min(PF, nchk)):
                    loads(i)
                stage12(0)
                for i in range(nchk):
                    if i + PF < nchk:
                        loads(i + PF)
                    if i + 1 < nchk:
                        stage12(i + 1)
                    if i >= 1:
                        stage6(i - 1)
                    stage3(i)
                    stage4(i)
                    stage5(i)
                stage6(nchk - 1)
        P.barrier()
        M.reset(ph2_mark)
        if stop_after <= 3:
            P.emit()
            return nc
        bias_sb = M.f32([128, 12, 4, 128])
        rel_sb = M.f32([33, 24])
        oh_sb = M.f32([33, 6, 256])
        grep_r = Ring("grep", [M.f32([128, 256]) for _ in range(2)])
        P.op("sp", lambda e: e.dma_start(out=rel_sb, in_=relx), writes=["rel_sb"], dma="c0")
        P.op("sp", lambda e: e.dma_start(out=oh_sb, in_=oh_in.rearrange("g ab b j -> b (g ab) j")), writes=["oh_sb"], dma="c1")
        gs = gs_h.ap()
        for g in range(3):
            for ab in range(2):
                idx = g * 2 + ab
                for h8 in range(8):
                    row = idx * 8 + h8
                    gr_t, grtok, grkey = grep_r.next()
                    P.op("pe", lambda e, g=g, idx=idx, h8=h8: e.matmul(bank(0)[:, 0:256], lhsT=rel_sb[:, g * 8 + h8:g * 8 + h8 + 1].broadcast_to([33, 128]), rhs=oh_sb[:, idx, :], start=True, stop=True), reads=["rel_sb", "oh_sb"], writes=["pb0"])
                    P.op("dve", lambda e, gr_t=gr_t: e.tensor_copy(out=gr_t, in_=bank(0)[:, 0:256]), reads=["pb0"], writes=[grtok])
                    P.op("sp", lambda e, gr_t=gr_t, row=row: e.dma_start(out=gs[row], in_=gr_t), reads=[grtok], multi=["gs"], dma=grkey)
        for g in range(3):
            for hp in range(4):
                for hh in range(2):
                    for ab in range(2):
                        row = (g * 2 + ab) * 8 + hp * 2 + hh
                        src = bass.AP(gs_h, row * 128 * 256 + 127, [[255, 128], [1, 128]])
                        P.op("sp", lambda e, src=src, g=g, hp=hp, hh=hh, ab=ab: e.dma_start(out=bias_sb[:, g * 4 + hp, hh * 2 + ab, :], in_=src), reads=["gs"], multi=["bias_sb"], dma="c%d" % ((hh * 2 + ab) % 3))
        P.barrier()
        ph4_mark = M.mark()
        qT_r = Ring("aq", [M.bf16([128, SEG]) for _ in range(2)])
        kT_r = Ring("ak", [M.bf16([128, SEG + 16 * 128]) for _ in range(2)])
        vt_r = Ring("avt", [M.bf16([128, 48, 3, 64]) for _ in range(2)])
        acc_sb = [M.f32([128, SEG]) for _ in range(2)]
        tb_r = Ring("tb", [M.f32([128, 256]) for _ in range(10)])
        pT_r = Ring("pT", [M.bf16([128, 256]) for _ in range(10)])
        rd = M.f32([128, 1024])
        sz_r = Ring("sz", [M.bf16([128, 1024]) for _ in range(2)])
        tt_ = M.f32([128, 1024])
        yb_r = Ring("ybst", [M.bf16([128, 1024]) for _ in range(2)])
        S_r = [Ring("Sps0", [bank(0)[:, 0:256], bank(1)[:, 0:256]]),
               Ring("Sps1", [bank(6)[:, 0:256], bank(7)[:, 0:256]])]
        acc_r = [Ring("accps0", [bank(2), bank(3)]), Ring("accps1", [bank(4), bank(5)])]
        for seg in range(NSEG):
            t0 = seg * SEG
            ljoin, rjoin = (seg == 1), (seg == 0)
            for hp in range(4):
                for g, d in enumerate(DILS):
                    L = SEG // d
                    nj = L // 128
                    W = L + 128
                    qf, qtok, qkey = qT_r.next()
                    kf, ktok, kkey = kT_r.next()
                    vf, vtok, vkey = vt_r.next()
                    q3 = qf.rearrange("p (r a) -> p r a", r=d)
                    k3 = kf[:, 0:d * W].rearrange("p (r a) -> p r a", r=d)
                    rows = slice(hp * 128, (hp + 1) * 128)
                    P.op("sp", lambda e, qf=qf, g=g, rows=rows, t0=t0: e.dma_start(out=qf, in_=aqk[g, rows, t0:t0 + SEG]), reads=["aqk"], writes=[qtok], dma=qkey)
                    P.op("sp", lambda e, k3=k3, g=g, rows=rows, d=d, L=L, t0=t0: e.dma_start(out=k3[:, :, 64:64 + L], in_=aqk[3 + g, rows, t0:t0 + SEG].rearrange("p (r a) -> p r a", r=d)), reads=["aqk"], multi=[ktok], dma=kkey)
                    if ljoin:
                        P.op("sp", lambda e, k3=k3, g=g, rows=rows, d=d, L=L, t0=t0: e.dma_start(out=k3[:, :, 0:64], in_=aqk[3 + g, rows, t0 - SEG:t0].rearrange("p (r a) -> p r a", r=d)[:, :, L - 64:L]), reads=["aqk"], multi=[ktok], dma=kkey)
                    else:
                        P.op("pool", lambda e, k3=k3: e.memset(k3[:, :, 0:64], 0.0), multi=[ktok])
                    if rjoin:
                        P.op("sp", lambda e, k3=k3, g=g, rows=rows, d=d, L=L, t0=t0: e.dma_start(out=k3[:, :, 64 + L:128 + L], in_=aqk[3 + g, rows, t0 + SEG:t0 + 2 * SEG].rearrange("p (r a) -> p r a", r=d)[:, :, 0:64]), reads=["aqk"], multi=[ktok], dma=kkey)
                    else:
                        P.op("pool", lambda e, k3=k3, L=L: e.memset(k3[:, :, 64 + L:128 + L], 0.0), multi=[ktok])
                    for r in range(d):
                        ti0 = av_tile_index(g, seg, r, 0)
                        for hh in range(2):
                            P.op("sp", lambda e, vf=vf, g=g, r=r, ti0=ti0, nj=nj, hp=hp, hh=hh: e.dma_start(out=vf[:, r * (nj + 1):(r + 1) * (nj + 1), 2 * hh, :], in_=av[g][ti0:ti0 + nj + 1, :, hp * 128 + hh * 64:hp * 128 + (hh + 1) * 64].rearrange("t k e -> k t e")), reads=["av%d" % g], multi=[vtok], dma=vkey)
                    ntile = d * (nj + 1)
                    P.op("pool", lambda e, vf=vf, ntile=ntile: e.memset(vf[:, 0:ntile, 1, :], 1.0), multi=[vtok])
                    vfl = vf.rearrange("p t a e -> p t (a e)")
                    if ljoin:
                        P.op("dve", lambda e, vfl=vfl, nj=nj, ntile=ntile: e.tensor_scalar(out=vfl[0:64, 0:ntile:nj + 1, :], in0=vfl[0:64, 0:ntile:nj + 1, :], scalar1=jcol[0:64, 0:1], scalar2=None, op0=ALU.mult), reads=[vtok, "jcol"], writes=[vtok])
                    else:
                        P.op("pool", lambda e, vfl=vfl, nj=nj, ntile=ntile: e.memset(vfl[0:64, 0:ntile:nj + 1, :], 0.0), reads=[vtok], writes=[vtok])
                    if rjoin:
                        P.op("dve", lambda e, vfl=vfl, nj=nj, ntile=ntile: e.tensor_scalar(out=vfl[64:128, nj:ntile:nj + 1, :], in0=vfl[64:128, nj:ntile:nj + 1, :], scalar1=jcol[64:128, 0:1], scalar2=None, op0=ALU.mult), reads=[vtok, "jcol"], writes=[vtok])
                    else:
                        P.op("pool", lambda e, vfl=vfl, nj=nj, ntile=ntile: e.memset(vfl[64:128, nj:ntile:nj + 1, :], 0.0), reads=[vtok], writes=[vtok])
                    units = []
                    for r in range(d):
                        for i in range(nj):
                            for hh in range(2):
                                units.append((r, i, hh))
                    KSK = 3
                    UU = {}
                    accs = {}

                    def stA(un):
                        r, i, hh = un
                        S, Stok, _ = S_r[hh].next()
                        tb, tbtok, _ = tb_r.next()
                        pT, pTtok, _ = pT_r.next()
                        UU[un] = (pT, pTtok)
                        bsl = bias_sb[:, g * 4 + hp, 2 * hh:2 * hh + 2, :].rearrange("p a b -> p (a b)")
                        for ab in range(2):
                            j = i + ab
                            P.op("pe", lambda e, S=S, hh=hh, r=r, j=j, ab=ab, i=i, k3=k3, q3=q3: e.matmul(S[:, ab * 128:(ab + 1) * 128], lhsT=k3[hh * 64:(hh + 1) * 64, r, 128 * j:128 * j + 128], rhs=q3[hh * 64:(hh + 1) * 64, r, i * 128:(i + 1) * 128], start=True, stop=True), reads=[qtok, ktok], writes=[Stok])
                        P.op("dve", lambda e, tb=tb, S=S, bsl=bsl: e.tensor_tensor(out=tb, in0=S, in1=bsl, op=ALU.add), reads=[Stok, "bias_sb"], writes=[tbtok])
                        P.op("act", lambda e, tb=tb, pT=pT: e.activation(out=pT, in_=tb, func=AF.Exp), reads=[tbtok], writes=[pTtok])

                    def stD(un):
                        r, i, hh = un
                        pT, pTtok = UU.pop(un)
                        i0 = (i // 4) * 4
                        nq = min(4, nj - i0)
                        qi = i - i0
                        key = (r, i0, hh)
                        if key not in accs:
                            accs[key] = acc_r[hh].next()
                        ap_, aptok, _ = accs[key]
                        for ab in range(2):
                            ti = r * (nj + 1) + i + ab
                            P.op("pe", lambda e, hh=hh, ab=ab, ti=ti, qi=qi, pT=pT, ap_=ap_, vf=vf: e.matmul(ap_[:, qi * 128:(qi + 1) * 128], lhsT=vf[:, ti, hh:hh + 2, :].rearrange("p a e -> p (a e)"), rhs=pT[:, ab * 128:(ab + 1) * 128], start=(ab == 0), stop=(ab == 1)), reads=[vtok, pTtok], writes=[aptok])
                        if qi == nq - 1:
                            dst = acc_sb[hh].rearrange("p (a r) -> p r a", r=d)[:, r, i0 * 128:(i0 + nq) * 128]
                            srcp = ap_[:, 0:nq * 128]
                            if g == 0:
                                P.op("act", lambda e, dst=dst, srcp=srcp: e.activation(out=dst, in_=srcp, func=AF.Copy), reads=[aptok], multi=[("accsb", hh)])
                            else:
                                P.op("dve", lambda e, dst=dst, srcp=srcp: e.tensor_tensor(out=dst, in0=srcp, in1=dst, op=ALU.add), reads=[aptok, ("accsb", hh)], multi=[("accsb", hh)])
                            del accs[key]

                    for n_ in range(len(units) + KSK):
                        if n_ < len(units):
                            stA(units[n_])
                        if n_ >= KSK:
                            stD(units[n_ - KSK])
                for pc in range(4):
                    cs = slice(pc * 1024, (pc + 1) * 1024)
                    sz, sztok, szkey = sz_r.next()
                    yb, ybtok, ybkey = yb_r.next()
                    P.op("sp", lambda e, sz=sz, hp=hp, pc=pc, t0=t0: e.dma_start(out=sz, in_=azT[hp * 128:(hp + 1) * 128, t0 + pc * 1024:t0 + (pc + 1) * 1024]), reads=["azT"], writes=[sztok], dma=szkey)
                    P.op("act", lambda e, cs=cs: e.activation(out=rd[0:64, :], in_=acc_sb[0][64:128, cs], func=AF.Copy), reads=[("accsb", 0)], multi=["rd"])
                    P.op("act", lambda e, cs=cs: e.activation(out=rd[64:128, :], in_=acc_sb[1][0:64, cs], func=AF.Copy), reads=[("accsb", 1)], multi=["rd"])
                    P.op("dve", lambda e: e.reciprocal(out=rd, in_=rd), reads=["rd"], writes=["rd"])
                    P.op("dve", lambda e, sz=sz: e.tensor_tensor(out=tt_, in0=rd, in1=sz, op=ALU.mult), reads=["rd", sztok], writes=["tt_"])
                    P.op("dve", lambda e, yb=yb, cs=cs: e.tensor_tensor(out=yb[0:64, :], in0=acc_sb[0][0:64, cs], in1=tt_[0:64, :], op=ALU.mult), reads=[("accsb", 0), "tt_"], multi=[ybtok])
                    P.op("dve", lambda e, yb=yb, cs=cs: e.tensor_tensor(out=yb[64:128, :], in0=acc_sb[1][64:128, cs], in1=tt_[64:128, :], op=ALU.mult), reads=[("accsb", 1), "tt_"], multi=[ybtok])
                    P.op("sp", lambda e, yb=yb, hp=hp, pc=pc, t0=t0: e.dma_start(out=yBT[hp * 128:(hp + 1) * 128, t0 + pc * 1024:t0 + (pc + 1) * 1024], in_=yb), reads=[ybtok], multi=["yBT"], dma=ybkey)
        P.barrier()
        M.reset(ph2_mark)
        if stop_after <= 4:
            P.emit()
            return nc

        wpm_sb = M.bf16([128, 8, 1024])
        wpa_sb = M.bf16([128, 4, 1024])
        wout_sb = M.bf16([128, 8, 1024])
        postw_bc = M.f32([128, 1024])
        P.op("sp", lambda e: e.dma_start(out=postw_bc, in_=postw_h.ap().partition_broadcast(128)), writes=["postw"], dma="c0")
        wst5 = Ring("w5", [M.f32([128, 1024]) for _ in range(2)])
        n = 0
        for (wsrc, wdst, nk, nm) in ((w_pm, wpm_sb, 8, "wpm"), (w_pa, wpa_sb, 4, "wpa"), (w_out, wout_sb, 8, "wout")):
            for k in range(nk):
                s_t, s_tok, s_key = wst5.next()
                P.op("sp", lambda e, s_t=s_t, wsrc=wsrc, k=k: e.dma_start(out=s_t, in_=wsrc[k * 128:(k + 1) * 128, :]), writes=[s_tok], dma=s_key)
                if n % 2 == 0:
                    P.op("dve", lambda e, s_t=s_t, wdst=wdst, k=k: e.tensor_copy(out=wdst[:, k, :], in_=s_t), reads=[s_tok], multi=[nm])
                else:
                    P.op("act", lambda e, s_t=s_t, wdst=wdst, k=k: e.activation(out=wdst[:, k, :], in_=s_t, func=AF.Copy), reads=[s_tok], multi=[nm])
                n += 1
        ya_r = Ring("ya", [M.bf16([128, 8, 512]) for _ in range(2)])
        ybb_r = Ring("ybb", [M.bf16([128, 4, 512]) for _ in range(2)])
        gt_r = Ring("gt", [M.bf16([128, 16, 512]) for _ in range(2)])
        x5_r = Ring("x5", [M.f32([128, 1024]) for _ in range(8)])
        mT_r = Ring("mT", [M.bf16([128, 8, 512]) for _ in range(2)])
        t1_r = Ring("t1", [M.f32([128, 512]) for _ in range(2)])
        t2_r = Ring("t2", [M.f32([128, 512]) for _ in range(2)])
        yn_r = Ring("yn", [M.f32([128, 1024]) for _ in range(2)])
        yo_r = Ring("yo", [M.f32([128, 1024]) for _ in range(2)])
        s5_r = Ring("s5", [M.f32([128, 4]) for _ in range(2)])
        sq5 = M.bf16([128, 1024])
        pm_r = Ring("pm", [bank(0), bank(1)])
        pa_r = Ring("pa", [bank(2), bank(3)])
        o_r = Ring("ops", [psum[:, 4 * 512:6 * 512], psum[:, 6 * 512:8 * 512]])
        yBT_v = yBT.rearrange("(a p) t -> p a t", p=128)
        gT_v = gT.rearrange("(a p) t -> p a t", p=128)
        def load_x5(tt_):
            res_ = []
            for j in range(4):
                x5, x5tok, x5key = x5_r.next()
                r0 = tt_ * 512 + j * 128
                P.op("sp", lambda e, x5=x5, r0=r0: e.dma_start(out=x5, in_=x[r0:r0 + 128, :]), writes=[x5tok], dma=x5key)
                res_.append((x5, x5tok))
            return res_

        for tt in range(T // 512):
            ts0 = tt * 512
            ya, yatok, yakey = ya_r.next()
            ybb, ybbtok, ybbkey = ybb_r.next()
            gt, gttok, gtkey = gt_r.next()
            mT, mTtok, _ = mT_r.next()
            P.op("sp", lambda e, ya=ya, ts0=ts0: e.dma_start(out=ya, in_=yAT_v[:, :, ts0:ts0 + 512]), reads=["yAT"], writes=[yatok], dma=yakey)
            P.op("sp", lambda e, ybb=ybb, ts0=ts0: e.dma_start(out=ybb, in_=yBT_v[:, :, ts0:ts0 + 512]), reads=["yBT"], writes=[ybbtok], dma=ybbkey)
            P.op("sp", lambda e, gt=gt, ts0=ts0: e.dma_start(out=gt, in_=gT_v[:, :, ts0:ts0 + 512]), reads=["gT"], writes=[gttok], dma=gtkey)
            for dc in range(8):
                pm, pmtok, _ = pm_r.next()
                pa, patok, _ = pa_r.next()
                t1, t1tok, _ = t1_r.next()
                t2, t2tok, _ = t2_r.next()
                for kc in range(8):
                    P.op("pe", lambda e, pm=pm, kc=kc, dc=dc, ya=ya: e.matmul(pm, lhsT=wpm_sb[:, kc, dc * 128:(dc + 1) * 128], rhs=ya[:, kc, :], start=(kc == 0), stop=(kc == 7)), reads=["wpm", yatok], writes=[pmtok])
                for kc in range(4):
                    P.op("pe", lambda e, pa=pa, kc=kc, dc=dc, ybb=ybb: e.matmul(pa, lhsT=wpa_sb[:, kc, dc * 128:(dc + 1) * 128], rhs=ybb[:, kc, :], start=(kc == 0), stop=(kc == 3)), reads=["wpa", ybbtok], writes=[patok])
                P.op("dve", lambda e, t1=t1, pm=pm, gt=gt, dc=dc: e.tensor_tensor(out=t1, in0=pm, in1=gt[:, dc, :], op=ALU.mult), reads=[pmtok, gttok], writes=[t1tok])
                P.op("dve", lambda e, t2=t2, pa=pa, gt=gt, dc=dc: e.tensor_tensor(out=t2, in0=pa, in1=gt[:, 8 + dc, :], op=ALU.mult), reads=[patok, gttok], writes=[t2tok])
                P.op("dve", lambda e, t1=t1, t2=t2, mT=mT, dc=dc: e.tensor_tensor(out=mT[:, dc, :], in0=t1, in1=t2, op=ALU.add), reads=[t1tok, t2tok], multi=[mTtok])
            if tt == 0:
                x5next = load_x5(0)
            x5s = x5next
            if tt + 1 < T // 512:
                x5next = load_x5(tt + 1)
            for j in range(4):
                o_, otok, _ = o_r.next()
                x5, x5tok = x5s[j]
                s5, s5tok, _ = s5_r.next()
                yn, yntok, _ = yn_r.next()
                yo, yotok, yokey = yo_r.next()
                r0 = ts0 + j * 128
                for half in range(2):
                    for dc in range(8):
                        P.op("pe", lambda e, o_=o_, half=half, dc=dc, j=j, mT=mT: e.matmul(o_[:, half * 512:(half + 1) * 512], lhsT=mT[:, dc, j * 128:(j + 1) * 128], rhs=wout_sb[:, dc, half * 512:(half + 1) * 512], start=(dc == 0), stop=(dc == 7)), reads=[mTtok, "wout"], writes=[otok])
                P.op("act", lambda e, o_=o_, s5=s5: e.activation(out=sq5, in_=o_, func=AF.Square, accum_out=s5[:, 0:1]), reads=[otok], writes=[s5tok, "sq5"])
                P.op("act", lambda e, s5=s5: e.activation(out=s5[:, 1:2], in_=s5[:, 0:1], func=AF.Ln, scale=1.0 / D, bias=ceps[:, 0:1]), reads=[s5tok, "ceps"], writes=[s5tok])
                P.op("act", lambda e, s5=s5: e.activation(out=s5[:, 2:3], in_=s5[:, 1:2], func=AF.Exp, scale=-0.5), reads=[s5tok], writes=[s5tok])
                P.op("dve", lambda e, yn=yn, o_=o_, s5=s5: e.scalar_tensor_tensor(out=yn, in0=o_, scalar=s5[:, 2:3], in1=postw_bc, op0=ALU.mult, op1=ALU.mult), reads=[otok, s5tok, "postw"], writes=[yntok])
                P.op("dve", lambda e, yo=yo, yn=yn, x5=x5: e.tensor_tensor(out=yo, in0=yn, in1=x5, op=ALU.add), reads=[yntok, x5tok], writes=[yotok])
                P.op("sp", lambda e, yo=yo, r0=r0: e.dma_start(out=y[r0:r0 + 128, :], in_=yo), reads=[yotok], dma=yokey)
        P.emit()
        return nc


def host_inputs(x_prompt, x_sample, pre_norm_w, w_in, m_conv_w, m_conv_b, m_igate_b, m_fgate_b, m_head_norm_w,
                w_proj_m, w_proj_a, w_out, post_norm_w, rel_bias_table):
    f = np.float32
    common = {
        "w_in": np.ascontiguousarray(w_in[0], f),
        "prew": np.ascontiguousarray(pre_norm_w[0].reshape(8, 128).T, f),
        "convw": np.ascontiguousarray(m_conv_w[0].reshape(3, 16, 128).transpose(2, 1, 0), f),
        "convb": np.ascontiguousarray(m_conv_b[0].reshape(16, 128).T, f),
        "gbias": np.concatenate([m_igate_b[0].reshape(-1), m_fgate_b[0].reshape(-1)]).astype(f).reshape(1, 16),
        "hnw": np.ascontiguousarray(m_head_norm_w[0].reshape(1, 1024), f),
        "w_pm": np.ascontiguousarray(w_proj_m[0], f),
        "w_pa": np.ascontiguousarray(w_proj_a[0], f),
        "w_out": np.ascontiguousarray(w_out[0], f),
        "postw": np.ascontiguousarray(post_norm_w[0].reshape(1, 1024), f),
        "relx": np.concatenate([rel_bias_table.reshape(32, 24), np.ones((1, 24), f)], 0).astype(f),
        "ident": np.eye(128, dtype=f),
        "tri": np.triu(np.ones((128, 128), f)),
        "oh": bias_onehots(),
    }
    maps = []
    for c in range(8):
        if c < 4:
            xc = np.concatenate([x_prompt[c], x_sample[c]], 0)
            j = 1.0
        else:
            b = 4 + 3 * (c - 4)
            xc = np.concatenate([x_sample[b], x_sample[b + 1], x_sample[b + 2]], 0)
            j = 0.0
        m = dict(common)
        m["x"] = np.ascontiguousarray(xc, f)
        m["jflag"] = np.full((128, 1), j, f)
        maps.append(m)
    return maps


_CACHE = {}


def kernel(**inputs):
    inputs = {k: np.asarray(v) for k, v in inputs.items()}
    maps = host_inputs(**inputs)
    if "nc" not in _CACHE:
        _CACHE["nc"] = build_program()
    res = run_bass_kernel_spmd(_CACHE["nc"], maps, core_ids=list(range(8)))
    ys = [r["y"] for r in res.results]
    y_prompt = np.stack([ys[c][0:8192] for c in range(4)], 0)
    samples = [None] * 16
    for c in range(4):
        samples[c] = ys[c][8192:12288]
    for c in range(4, 8):
        b = 4 + 3 * (c - 4)
        for i in range(3):
            samples[b + i] = ys[c][i * 4096:(i + 1) * 4096]
    y_sample = np.stack(samples, 0)
    return (y_prompt.astype(np.float32), y_sample.astype(np.float32))
```

```python
import contextlib
import math
import numpy as np
import ml_dtypes
import concourse.bass as bass
import concourse.mybir as mybir
from concourse.bass_utils import run_bass_kernel_spmd

F32 = mybir.dt.float32
BF16 = mybir.dt.bfloat16
ALU = mybir.AluOpType
AF = mybir.ActivationFunctionType
AX = mybir.AxisListType

COMPUTE = ("pe", "act", "dve", "pool")

T = 12288
SEG = 4096
NSEG = 3
D = 1024
INC = 12304
NCH = 96
EPS = 1e-6
DILS = (1, 4, 16)
NEG = -30000.0
SBUF_WORDS = 207 * 256

C_MQK, C_MV, C_MO, C_MZ, C_MI, C_MF, C_AQKV, C_AZ, C_GATE = 0, 2048, 3072, 4096, 5120, 5128, 5136, 9744, 10256


class Prog:
    def __init__(self, nc):
        self.nc = nc
        self.ops = []
        self.barriers = []
        self.bankmap = {}

    def op(self, eng, fn, reads=(), writes=(), dma=None, multi=()):
        self.ops.append(dict(eng=eng, fn=fn, reads=tuple(reads), writes=tuple(writes), dma=dma,
                             multi=tuple(multi), deps=set(), signal=False))
        return len(self.ops) - 1

    def barrier(self):
        self.barriers.append(len(self.ops))

    def _analyze(self):
        last_w = {}
        readers = {}
        last_bank = {}
        bar_deps = set()
        bi = 0
        last_by_stream = {}
        for i, o in enumerate(self.ops):
            while bi < len(self.barriers) and self.barriers[bi] <= i:
                bar_deps = set(last_by_stream.values())
                bi += 1
            deps = set(bar_deps)
            for t in o["reads"]:
                deps.update(last_w.get(t, ()))
            for t in o["writes"]:
                deps.update(last_w.get(t, ()))
                deps.update(readers.get(t, ()))
            for t in o["multi"]:
                deps.update(readers.get(t, ()))
            obanks = set()
            for t in o["reads"] + o["writes"] + o["multi"]:
                obanks.update(self.bankmap.get(t, ()))
            for b in obanks:
                lb = last_bank.get(b)
                if lb is not None and self.ops[lb]["eng"] != o["eng"]:
                    deps.add(lb)
                last_bank[b] = i
            deps.discard(i)
            keep = set()
            for d in deps:
                od = self.ops[d]
                if od["dma"] is None and o["dma"] is None and od["eng"] == o["eng"] and o["eng"] == "pe":
                    continue
                keep.add(d)
            o["deps"] = keep
            for d in keep:
                self.ops[d]["signal"] = True
            for t in o["reads"]:
                readers.setdefault(t, []).append(i)
            for t in o["writes"]:
                last_w[t] = [i]
                readers[t] = []
            for t in o["multi"]:
                last_w.setdefault(t, []).append(i)
            key = ("d", o["dma"]) if o["dma"] is not None else ("e", o["eng"])
            last_by_stream[key] = i

    def emit(self):
        nc = self.nc
        self._analyze()
        dma_keys = []
        for o in self.ops:
            if o["dma"] is not None and o["dma"] not in dma_keys:
                dma_keys.append(o["dma"])
        engs_used = []
        for o in self.ops:
            if o["eng"] not in engs_used:
                engs_used.append(o["eng"])
        with contextlib.ExitStack() as st:
            esem = {e: st.enter_context(nc.semaphore("s_" + e)) for e in COMPUTE}
            first, last = {}, {}
            for i, o in enumerate(self.ops):
                if o["dma"] is not None:
                    first.setdefault(o["dma"], i)
                    last[o["dma"]] = i
            phys_last = []
            kmap = {}
            for k in sorted(dma_keys, key=lambda k: first[k]):
                chosen = None
                for pi, pl in enumerate(phys_last):
                    if any(pl < b <= first[k] for b in self.barriers):
                        chosen = pi
                        break
                if chosen is None:
                    phys_last.append(last[k])
                    chosen = len(phys_last) - 1
                else:
                    phys_last[chosen] = last[k]
                kmap[k] = "p%d" % chosen
            for o in self.ops:
                if o["dma"] is not None:
                    o["lkey"] = o["dma"]
                    o["dma"] = kmap[o["dma"]]

            def same_batch(a, b):
                return (self.ops[a]["lkey"] == self.ops[b]["lkey"]
                        and not any(a < x <= b for x in self.barriers))
            dma_keys = ["p%d" % i for i in range(len(phys_last))]
            dsem = {k: st.enter_context(nc.semaphore("d_" + k)) for k in dma_keys}
            cnt = {e: 0 for e in COMPUTE}
            dcnt = {k: 0 for k in dma_keys}
            for o in self.ops:
                if o["dma"] is not None:
                    dcnt[o["dma"]] += 16
                    o["sem"], o["val"], o["semname"] = dsem[o["dma"]], dcnt[o["dma"]], "d_" + o["dma"]
                elif o["signal"]:
                    cnt[o["eng"]] += 1
                    o["sem"], o["val"], o["semname"] = esem[o["eng"]], cnt[o["eng"]], "s_" + o["eng"]
            prev_batch_last = {}
            by_eng = {}
            for i, o in enumerate(self.ops):
                by_eng.setdefault(o["eng"], []).append(i)
            for ename, idxs in by_eng.items():
                batch = []
                def close(batch):
                    if not batch:
                        return
                    lastv = self.ops[batch[-1]]["val"]
                    for bi_ in batch:
                        self.ops[bi_]["val"] = lastv
                for i in idxs:
                    o = self.ops[i]
                    if o["dma"] is None:
                        close(batch)
                        batch = []
                        continue
                    if batch and same_batch(batch[-1], i):
                        batch.append(i)
                    else:
                        close(batch)
                        batch = [i]
                close(batch)
            lastop = {}
            prev_in_eng = {}
            for i, o in enumerate(self.ops):
                if o["dma"] is None:
                    prev_in_eng[o["eng"]] = None
                    continue
                k = o["dma"]
                pi_ = prev_in_eng.get(o["eng"])
                sb_ = pi_ is not None and same_batch(pi_, i)
                if not sb_ and k in lastop:
                    o["deps"].add(lastop[k])
                lastop[k] = i
                prev_in_eng[o["eng"]] = i
            final = dict(dcnt)
            self._simulate()
            block = st.enter_context(nc.Block())
            hooks = {"pe": block.tensor, "act": block.scalar, "dve": block.vector,
                     "pool": block.gpsimd, "sp": block.sync}

            def make(ename):
                def body(e):
                    seen = {}
                    for o in self.ops:
                        if o["eng"] != ename:
                            continue
                        need = {}
                        for d in o["deps"]:
                            od = self.ops[d]
                            key = od["semname"]
                            if od["val"] > need.get(key, (None, 0))[1]:
                                need[key] = (od["sem"], od["val"])
                        for key, (sem, val) in need.items():
                            if seen.get(key, 0) >= val:
                                continue
                            e.wait_ge(sem, val)
                            seen[key] = val
                        ins = o["fn"](e)
                        if o["dma"] is not None:
                            ins.then_inc(o["sem"], 16)
                        elif o["signal"]:
                            ins.then_inc(o["sem"], 1)
                    if ename == "sp":
                        for k in dma_keys:
                            if seen.get("d_" + k, 0) < final[k]:
                                e.wait_ge(dsem[k], final[k])
                return body

            names = list(engs_used)
            if "sp" not in names:
                names.append("sp")
            for ename in names:
                hooks[ename](make(ename))


def _prog_simulate(self):
    streams = {}
    for i, o in enumerate(self.ops):
        streams.setdefault(o["eng"], []).append(i)
    pos = {e: 0 for e in streams}
    sem = {}
    progress = True
    ndone = 0
    while progress:
        progress = False
        for e, idxs in streams.items():
            while pos[e] < len(idxs):
                o = self.ops[idxs[pos[e]]]
                ok = True
                for d in o["deps"]:
                    od = self.ops[d]
                    if sem.get(od["semname"], 0) < od["val"]:
                        ok = False
                        break
                if not ok:
                    break
                if o["dma"] is not None:
                    sem[o["semname"]] = sem.get(o["semname"], 0) + 16
                elif o["signal"]:
                    sem[o["semname"]] = sem.get(o["semname"], 0) + 1
                pos[e] += 1
                ndone += 1
                progress = True
    stuck = {e: idxs[pos[e]] for e, idxs in streams.items() if pos[e] < len(idxs)}
    if stuck:
        msg = []
        for e, i in stuck.items():
            o = self.ops[i]
            need = [(self.ops[d]["semname"], self.ops[d]["val"], sem.get(self.ops[d]["semname"], 0), d) for d in o["deps"] if sem.get(self.ops[d]["semname"], 0) < self.ops[d]["val"]]
            msg.append((e, i, o.get("lkey"), need[:4]))
        raise RuntimeError("semaphore program deadlocks: %r" % (msg,))


Prog._simulate = _prog_simulate


class Mem:
    def __init__(self, big):
        self.big = big
        self.off = 0

    def mark(self):
        return self.off

    def reset(self, m):
        self.off = m

    def _shape(self, v, shape):
        if len(shape) == 2:
            return v
        if len(shape) == 3:
            return v.rearrange("p (a b) -> p a b", a=shape[1])
        if len(shape) == 4:
            return v.rearrange("p (a b c) -> p a b c", a=shape[1], b=shape[2])
        raise ValueError

    def f32(self, shape):
        n = int(np.prod(shape[1:]))
        a = self.off
        self.off += n + (n & 1)
        assert self.off <= SBUF_WORDS, ("SBUF overflow", self.off)
        return self._shape(self.big[0:shape[0], a:a + n], shape)

    def bf16(self, shape):
        n = int(np.prod(shape[1:]))
        nw = (n + 1) // 2
        nw += nw & 1
        a = self.off
        self.off += nw
        assert self.off <= SBUF_WORDS, ("SBUF overflow", self.off)
        v = self.big[0:shape[0], a:a + nw].bitcast(BF16)[:, 0:n]
        return self._shape(v, shape)


class Ring:
    def __init__(self, name, tiles):
        self.name, self.tiles, self.i = name, tiles, 0

    def next(self):
        k = self.i % len(self.tiles)
        self.i += 1
        return self.tiles[k], (self.name, k), "%s%d" % (self.name, k)


def t5_bucket(rel):
    nb = 16
    exact = 8
    n = np.abs(rel)
    large = exact + (np.log(np.maximum(n, 1) / exact) / math.log(1024 / exact) * (nb - exact)).astype(np.int32)
    large = np.minimum(large, nb - 1)
    return (rel > 0).astype(np.int32) * nb + np.where(n < exact, n, large)


def bias_onehots():
    oh = np.zeros((3, 2, 33, 256), np.float32)
    for g, d in enumerate(DILS):
        for j in range(255):
            if j <= 127:
                m = 63 - j
                oh[g, 0, t5_bucket(np.array(m * d)), j] = 1.0
            else:
                oh[g, 0, 32, j] = NEG
            if j >= 127:
                m = 191 - j
                oh[g, 1, t5_bucket(np.array(m * d)), j] = 1.0
            else:
                oh[g, 1, 32, j] = NEG
        oh[g, :, 32, 255] = NEG
    return oh


def build_program(debug=False, stop_after=99):
    nc = bass.Bass("TRN2", target_bir_lowering=False)
    din = lambda n, s, dt=F32: nc.dram_tensor(n, list(s), dt, kind="ExternalInput")
    x_h = din("x", [T, D])
    x = x_h.ap()
    jflag = din("jflag", [128, 1]).ap()
    w_in = din("w_in", [D, INC]).ap()
    prew = din("prew", [128, 8]).ap()
    convw = din("convw", [128, 16, 3]).ap()
    convb = din("convb", [128, 16]).ap()
    gbias_h = din("gbias", [1, 16])
    hnw_h = din("hnw", [1, 1024])
    w_pm = din("w_pm", [1024, 1024]).ap()
    w_pa = din("w_pa", [512, 1024]).ap()
    w_out = din("w_out", [1024, 1024]).ap()
    postw_h = din("postw", [1, 1024])
    relx = din("relx", [33, 24]).ap()
    ident_in = din("ident", [128, 128]).ap()
    tri_in = din("tri", [128, 128]).ap()
    oh_in = din("oh", [3, 2, 33, 256]).ap()
    y = nc.dram_tensor("y", [T, D], F32, kind="ExternalOutput").ap()

    skind = "ExternalOutput" if debug else "Internal"
    scr = lambda n, s, dt=BF16: nc.dram_tensor(n, list(s), dt, kind=skind)
    wb = scr("wb", [D, INC]).ap()
    qkT = scr("qkT", [2048, T]).ap()
    mv = scr("mv", [T, 1024]).ap()
    goz = scr("goz", [T, 1024]).ap()
    aqk = scr("aqk", [6, 512, T]).ap()
    av_ntile = []
    for g, d in enumerate(DILS):
        av_ntile.append(d * (8192 // d // 128 + 1) + d * (4096 // d // 128 + 1))
    av = [scr("av%d" % g, [av_ntile[g], 128, 512]).ap() for g in range(3)]
    azT = scr("azT", [512, T]).ap()
    gT = scr("gT", [2048, T]).ap()
    yAT = scr("yAT", [1024, T]).ap()
    yBT = scr("yBT", [512, T]).ap()
    gs_h = scr("gs", [48, 128, 256], F32)
    gates_dbg = scr("gates_dbg", [128, 96 * 16], F32).ap() if debug else None

    def av_tile_index(g, seg, r, jloc):
        d = DILS[g]
        L = SEG // d
        if seg < 2:
            return r * (2 * L // 128 + 1) + seg * (L // 128) + jloc
        return d * (2 * L // 128 + 1) + r * (L // 128 + 1) + jloc

    with contextlib.ExitStack() as st:
        big = st.enter_context(nc.sbuf_tensor("big", [128, SBUF_WORDS], F32))
        psum = st.enter_context(nc.psum_tensor("psum", [128, 8 * 512], F32))
        M = Mem(big)
        P = Prog(nc)
        BM = P.bankmap
        BM["pb0"] = (0,)
        for k_ in range(4):
            BM[("pacc", k_)] = (k_,)
            BM["g%d" % k_] = (k_,)
        for k_ in range(2):
            BM[("ptr", k_)] = (4 + k_,)
            BM[("bA", k_)] = (k_,)
            BM[("bY", k_)] = (k_,)
            BM[("bN", k_)] = (2 + k_,)
            BM[("bKV", k_)] = (4 + 2 * k_, 5 + 2 * k_)
            BM[("Sps0", k_)] = (k_,)
            BM[("Sps1", k_)] = (6 + k_,)
            BM[("accps0", k_)] = (2 + k_,)
            BM[("accps1", k_)] = (4 + k_,)
            BM[("pm", k_)] = (k_,)
            BM[("pa", k_)] = (2 + k_,)
            BM[("ops", k_)] = (4 + 2 * k_, 5 + 2 * k_)
        BM[("pacc2", 0)] = (4, 5)
        BM[("pacc2", 1)] = (6, 7)
        bank = lambda b: psum[:, b * 512:(b + 1) * 512]
        bank_bf = lambda b: psum[:, b * 512:(b + 1) * 512].bitcast(BF16)

        ident_f = M.f32([128, 128])
        ident_b = M.bf16([128, 128])
        tri_f = M.f32([128, 128])
        triT_f = M.f32([128, 128])
        mask_f = M.bf16([128, 128])
        mask_b = M.bf16([128, 128])
        ones_f = M.f32([128, 128])
        jcol = M.f32([128, 1])
        gates = M.f32([128, NCH, 16])
        wk = M.f32([128, NCH, 8])
        flo = M.f32([128, NCH, 8])
        dec = M.f32([128, NCH, 8])
        decj = M.f32([128, 8])
        hhalo = M.bf16([128, 8, 2])
        P.op("sp", lambda e: e.dma_start(out=ident_f, in_=ident_in), writes=["ident_f"], dma="c0")
        P.op("sp", lambda e: e.dma_start(out=tri_f, in_=tri_in), writes=["tri_f"], dma="c1")
        P.op("sp", lambda e: e.dma_start(out=jcol, in_=jflag), writes=["jcol"], dma="c2")
        P.op("dve", lambda e: e.tensor_copy(out=ident_b, in_=ident_f), reads=["ident_f"], writes=["ident_b"])
        P.op("dve", lambda e: e.tensor_copy(out=mask_f, in_=tri_f), reads=["tri_f"], writes=["mask_f"])
        P.op("dve", lambda e: e.memset(ones_f, 1.0), writes=["ones_f"])
        P.op("pe", lambda e: e.transpose(bank(0)[:, 0:128], tri_f, ident_f), reads=["tri_f", "ident_f"], writes=["pb0"])
        P.op("dve", lambda e: e.tensor_copy(out=triT_f, in_=bank(0)[:, 0:128]), reads=["pb0"], writes=["triT_f"])
        P.op("dve", lambda e: e.tensor_copy(out=mask_b, in_=triT_f), reads=["triT_f"], writes=["mask_b"])
        const_mark = M.mark()

        prew_sb = M.f32([128, 8])
        P.op("sp", lambda e: e.dma_start(out=prew_sb, in_=prew), writes=["prew"], dma="c0")
        wst = Ring("wst", [M.f32([128, 2048]) for _ in range(2)])
        wbs = Ring("wbs", [M.bf16([128, 2048]) for _ in range(2)])
        n = 0
        for k in range(8):
            for c0 in range(0, INC, 2048):
                w = min(2048, INC - c0)
                s_t, s_tok, s_key = wst.next()
                b_t, b_tok, b_key = wbs.next()
                P.op("sp", lambda e, s_t=s_t, k=k, c0=c0, w=w: e.dma_start(out=s_t[:, 0:w], in_=w_in[k * 128:(k + 1) * 128, c0:c0 + w]),
                     writes=[s_tok], dma=s_key)
                if n % 2 == 0:
                    P.op("dve", lambda e, s_t=s_t, b_t=b_t, k=k, w=w: e.tensor_scalar(out=b_t[:, 0:w], in0=s_t[:, 0:w], scalar1=prew_sb[:, k:k + 1], scalar2=None, op0=ALU.mult),
                         reads=[s_tok, "prew"], writes=[b_tok])
                else:
                    P.op("act", lambda e, s_t=s_t, b_t=b_t, k=k, w=w: e.activation(out=b_t[:, 0:w], in_=s_t[:, 0:w], func=AF.Copy, scale=prew_sb[:, k:k + 1]),
                         reads=[s_tok, "prew"], writes=[b_tok])
                P.op("sp", lambda e, b_t=b_t, k=k, c0=c0, w=w: e.dma_start(out=wb[k * 128:(k + 1) * 128, c0:c0 + w], in_=b_t[:, 0:w]),
                     reads=[b_tok], multi=["wb"], dma=b_key)
                n += 1
        P.barrier()
        M.reset(const_mark)
        if stop_after <= 0:
            P.emit()
            return nc

        cw_sb = M.f32([128, 16, 3])
        cb_sb = M.f32([128, 16])
        hnw_bc = M.f32([128, 1024])
        P.op("sp", lambda e: e.dma_start(out=cw_sb, in_=convw), writes=["cw"], dma="c0")
        P.op("sp", lambda e: e.dma_start(out=cb_sb, in_=convb), writes=["cb"], dma="c1")
        P.op("sp", lambda e: e.dma_start(out=hnw_bc, in_=hnw_h.ap().partition_broadcast(128)), writes=["hnw"], dma="c2")
        hT = M.bf16([128, 8, SEG + 2])
        xring = Ring("xt", [M.f32([128, 1024]) for _ in range(2)])
        xsring = Ring("xs", [M.bf16([128, 1024]) for _ in range(2)])
        ssr = Ring("ss", [M.f32([128, 2]) for _ in range(2)])
        wblk = Ring("wblk", [M.bf16([128, 8, 512]) for _ in range(3)])
        pre_r = Ring("pre", [M.f32([128, SEG + 2]) for _ in range(2)])
        cv1 = M.f32([128, SEG])
        fmst = Ring("fmst", [M.bf16([128, SEG]) for _ in range(2)])
        tmst = Ring("tmst", [M.bf16([128, 512]) for _ in range(3)])
        sg1 = M.f32([128, 1024])
        sqj = sg1
        sg2 = M.f32([128, 512])
        sg3 = M.f32([128, 512])
        pacc = Ring("pacc", [bank(b) for b in (0, 1, 2, 3)])
        ptr = Ring("ptr", [bank_bf(b) for b in (4, 5)])
        pacc2 = Ring("pacc2", [psum[:, 4 * 512:6 * 512], psum[:, 6 * 512:8 * 512]])

        def rmsnorm_to_hT(rows, nrow, dst_fn, tag):
            xt, xtok, xkey = xring.next()
            xs, xstok, _ = xsring.next()
            ss, sstok, _ = ssr.next()
            pt, pttok, _ = ptr.next()
            P.op("sp", lambda e: e.dma_start(out=xt[0:nrow, :], in_=rows), writes=[xtok], dma=xkey)
            P.op("act", lambda e: e.activation(out=sqj[0:nrow, :], in_=xt[0:nrow, :], func=AF.Square, accum_out=ss[0:nrow, 0:1]),
                 reads=[xtok], writes=[sstok, "sg1"])
            P.op("act", lambda e: e.activation(out=ss[0:nrow, 1:2], in_=ss[0:nrow, 0:1], func=AF.Ln, scale=1.0 / D, bias=EPS),
                 reads=[sstok], writes=[sstok])
            P.op("act", lambda e: e.activation(out=ss[0:nrow, 0:1], in_=ss[0:nrow, 1:2], func=AF.Exp, scale=-0.5),
                 reads=[sstok], writes=[sstok])
            P.op("dve", lambda e: e.tensor_scalar(out=xs[0:nrow, :], in0=xt[0:nrow, :], scalar1=ss[0:nrow, 0:1], scalar2=None, op0=ALU.mult),
                 reads=[xtok, sstok], writes=[xstok])
            for k in range(8):
                P.op("pe", lambda e, k=k: e.transpose(pt[:, k * 128:k * 128 + nrow], xs[0:nrow, k * 128:(k + 1) * 128], ident_b[0:nrow, 0:nrow]),
                     reads=[xstok, "ident_b"], writes=[pttok])
            return pt, pttok

        pt, pttok = rmsnorm_to_hT(x[SEG - 1:SEG + 1, :], 2, None, "halo")
        P.op("dve", lambda e: e.tensor_scalar(out=hhalo, in0=pt.rearrange("p (k t) -> p k t", k=8)[:, :, 0:2], scalar1=jcol[:, 0:1], scalar2=None, op0=ALU.mult),
             reads=[pttok, "jcol"], writes=["hhalo"])

        evac_rr = [0]

        def evac_engine():
            evac_rr[0] += 1
            return "act" if evac_rr[0] % 2 == 0 else "dve"

        wsched = []
        for c0_ in range(C_MQK, C_MV, 512):
            wsched.append((c0_, 512))
        for g_ in range(3):
            wsched.append((C_AQKV + g_ * 512, 512))
            wsched.append((C_AQKV + (3 + g_) * 512, 512))
        wsched.append((C_AZ, 512))
        for c0_ in range(C_GATE, INC, 512):
            wsched.append((c0_, 512))
        for j_ in range(2):
            wsched.append((C_MV + j_ * 512, 512))
        for j_ in range(2):
            wsched.append((C_MO + j_ * 512, 512))
            wsched.append((C_MZ + j_ * 512, 512))
        wsched.append((C_MI, 16))
        for g_ in range(3):
            wsched.append((C_AQKV + (6 + g_) * 512, 512))
        wsched = wsched * NSEG
        wstate = {"next": 0, "issued": []}

        def _issue_next():
            if wstate["next"] >= len(wsched):
                return
            c0, w = wsched[wstate["next"]]
            wstate["next"] += 1
            wt, wtok, wkey = wblk.next()
            P.op("sp", lambda e: e.dma_start(out=wt[:, :, 0:w], in_=wb.rearrange("(k p) c -> p k c", p=128)[:, :, c0:c0 + w]),
                 reads=["wb"], writes=[wtok], dma=wkey)
            wstate["issued"].append((c0, w, wt, wtok))

        def load_wblk(c0, w):
            if not wstate["issued"]:
                _issue_next()
            c0i, wi, wt, wtok = wstate["issued"].pop(0)
            assert (c0i, wi) == (c0, w), ((c0i, wi), (c0, w))
            if not wstate["issued"]:
                _issue_next()
            return wt, wtok

        for seg in range(NSEG):
            t0 = seg * SEG
            for i in range(SEG // 128):
                pt, pttok = rmsnorm_to_hT(x[t0 + i * 128:t0 + (i + 1) * 128, :], 128, None, "h")
                eng = evac_engine()
                dst = hT[:, :, 1 + i * 128:1 + (i + 1) * 128]
                src = pt.rearrange("p (k t) -> p k t", k=8)
                if eng == "act":
                    P.op("act", lambda e, dst=dst, src=src: e.activation(out=dst, in_=src, func=AF.Copy), reads=[pttok], multi=["hT"])
                else:
                    P.op("dve", lambda e, dst=dst, src=src: e.tensor_copy(out=dst, in_=src), reads=[pttok], multi=["hT"])
            if seg == 0:
                P.op("pool", lambda e: e.memset(hT[:, :, 0:1], 0.0), multi=["hT"])
                P.op("pool", lambda e: e.tensor_copy(out=hT[:, :, SEG + 1:SEG + 2], in_=hhalo[:, :, 1:2]), reads=["hhalo"], multi=["hT"])
            elif seg == 1:
                P.op("pool", lambda e: e.tensor_copy(out=hT[:, :, 0:1], in_=hhalo[:, :, 0:1]), reads=["hhalo"], multi=["hT"])
                P.op("pool", lambda e: e.memset(hT[:, :, SEG + 1:SEG + 2], 0.0), multi=["hT"])
            else:
                P.op("pool", lambda e: e.memset(hT[:, :, 0:1], 0.0), multi=["hT"])
                P.op("pool", lambda e: e.memset(hT[:, :, SEG + 1:SEG + 2], 0.0), multi=["hT"])

            def fm_block(c0, nchunk, kind, arg=None):
                wt, wtok = load_wblk(c0, nchunk * 128)
                for cc in range(nchunk):
                    col = c0 + cc * 128
                    if kind == "mqk":
                        chunk = (col - C_MQK) // 128
                        pre, pretok, _ = pre_r.next()
                        for tt in range(8):
                            pa, patok, _ = pacc.next()
                            for k in range(8):
                                P.op("pe", lambda e, pa=pa, k=k, cc=cc, tt=tt: e.matmul(pa, lhsT=wt[:, k, cc * 128:(cc + 1) * 128], rhs=hT[:, k, 1 + tt * 512:1 + (tt + 1) * 512], start=(k == 0), stop=(k == 7)),
                                     reads=[wtok, "hT"], writes=[patok])
                            eng = "act"
                            dst = pre[:, 1 + tt * 512:1 + (tt + 1) * 512]
                            if eng == "act":
                                P.op("act", lambda e, pa=pa, dst=dst: e.activation(out=dst, in_=pa, func=AF.Copy), reads=[patok], multi=[pretok])
                            else:
                                P.op("dve", lambda e, pa=pa, dst=dst: e.tensor_copy(out=dst, in_=pa), reads=[patok], multi=[pretok])
                        pa, patok, _ = pacc.next()
                        for k in range(8):
                            P.op("pe", lambda e, pa=pa, k=k, cc=cc: e.matmul(pa[:, 0:2], lhsT=wt[:, k, cc * 128:(cc + 1) * 128], rhs=hT[:, k, 0:SEG + 2:SEG + 1], start=(k == 0), stop=(k == 7)),
                                 reads=[wtok, "hT"], writes=[patok])
                        P.op("dve", lambda e, pa=pa, pre=pre: e.tensor_copy(out=pre[:, 0:SEG + 2:SEG + 1], in_=pa[:, 0:2]), reads=[patok], multi=[pretok])
                        P.op("dve", lambda e, ch=chunk, pre=pre: e.tensor_scalar(out=cv1, in0=pre[:, 1:SEG + 1], scalar1=cw_sb[:, ch, 1:2], scalar2=cb_sb[:, ch:ch + 1], op0=ALU.mult, op1=ALU.add),
                             reads=[pretok, "cw", "cb"], writes=["cv1"])
                        P.op("dve", lambda e, ch=chunk, pre=pre: e.scalar_tensor_tensor(out=cv1, in0=pre[:, 0:SEG], scalar=cw_sb[:, ch, 0:1], in1=cv1, op0=ALU.mult, op1=ALU.add),
                             reads=[pretok, "cw", "cv1"], writes=["cv1"])
                        P.op("dve", lambda e, ch=chunk, pre=pre: e.scalar_tensor_tensor(out=cv1, in0=pre[:, 2:SEG + 2], scalar=cw_sb[:, ch, 2:3], in1=cv1, op0=ALU.mult, op1=ALU.add),
                             reads=[pretok, "cw", "cv1"], writes=["cv1"])
                        fs, fstok, fskey = fmst.next()
                        P.op("act", lambda e, fs=fs: e.activation(out=fs, in_=cv1, func=AF.Silu), reads=["cv1"], writes=[fstok])
                        P.op("sp", lambda e, fs=fs, col=col, t0=t0: e.dma_start(out=qkT[col:col + 128, t0:t0 + SEG], in_=fs),
                             reads=[fstok], multi=["qkT"], dma=fskey)
                    else:
                        fs, fstok, fskey = fmst.next()
                        for tt in range(8):
                            pa, patok, _ = pacc.next()
                            for k in range(8):
                                P.op("pe", lambda e, pa=pa, k=k, cc=cc, tt=tt: e.matmul(pa, lhsT=wt[:, k, cc * 128:(cc + 1) * 128], rhs=hT[:, k, 1 + tt * 512:1 + (tt + 1) * 512], start=(k == 0), stop=(k == 7)),
                                     reads=[wtok, "hT"], writes=[patok])
                            if kind == "aqk":
                                g, scale = arg
                                d = DILS[g]
                                if d == 1:
                                    dst = fs[:, tt * 512:(tt + 1) * 512]
                                    src = pa
                                else:
                                    na = 512 // d
                                    dst = fs.rearrange("p (r a) -> p a r", r=d)[:, tt * na:(tt + 1) * na, :]
                                    src = pa.rearrange("p (a r) -> p a r", r=d)
                                eng = evac_engine()
                                if eng == "act":
                                    P.op("act", lambda e, dst=dst, src=src, scale=scale: e.activation(out=dst, in_=src, func=AF.Copy, scale=scale), reads=[patok], multi=[fstok])
                                else:
                                    P.op("dve", lambda e, dst=dst, src=src, scale=scale: e.tensor_scalar(out=dst, in0=src, scalar1=scale, scalar2=None, op0=ALU.mult), reads=[patok], multi=[fstok])
                            else:
                                func = AF.Silu if kind == "az" else AF.Sigmoid
                                dst = fs[:, tt * 512:(tt + 1) * 512]
                                P.op("act", lambda e, dst=dst, pa=pa, func=func: e.activation(out=dst, in_=pa, func=func), reads=[patok], multi=[fstok])
                        if kind == "aqk":
                            idx = (col - C_AQKV) // 512
                            row = (col - C_AQKV) % 512
                            dstd = aqk[idx, row:row + 128, t0:t0 + SEG]
                            mt = "aqk"
                        elif kind == "az":
                            dstd = azT[col - C_AZ:col - C_AZ + 128, t0:t0 + SEG]
                            mt = "azT"
                        else:
                            dstd = gT[col - C_GATE:col - C_GATE + 128, t0:t0 + SEG]
                            mt = "gT"
                        P.op("sp", lambda e, fs=fs, dstd=dstd: e.dma_start(out=dstd, in_=fs), reads=[fstok], multi=[mt], dma=fskey)

            for c0 in range(C_MQK, C_MV, 512):
                fm_block(c0, 4, "mqk")
            for g in range(3):
                fm_block(C_AQKV + g * 512, 4, "aqk", (g, 0.125))
                fm_block(C_AQKV + (3 + g) * 512, 4, "aqk", (g, 1.0))
            fm_block(C_AZ, 4, "az")
            for c0 in range(C_GATE, INC, 512):
                fm_block(c0, 4, "gate")

            for j in range(2):
                wt, wtok = load_wblk(C_MV + j * 512, 512)
                for i in range(32):
                    pa, patok, _ = pacc.next()
                    for k in range(8):
                        P.op("pe", lambda e, pa=pa, k=k, i=i, wt=wt: e.matmul(pa, lhsT=hT[:, k, 1 + i * 128:1 + (i + 1) * 128], rhs=wt[:, k, :], start=(k == 0), stop=(k == 7)),
                             reads=[wtok, "hT"], writes=[patok])
                    ts_, tstok, tskey = tmst.next()
                    eng = evac_engine()
                    if eng == "act":
                        P.op("act", lambda e, ts_=ts_, pa=pa: e.activation(out=ts_, in_=pa, func=AF.Copy), reads=[patok], writes=[tstok])
                    else:
                        P.op("dve", lambda e, ts_=ts_, pa=pa: e.tensor_copy(out=ts_, in_=pa), reads=[patok], writes=[tstok])
                    P.op("sp", lambda e, ts_=ts_, i=i, j=j, t0=t0: e.dma_start(out=mv[t0 + i * 128:t0 + (i + 1) * 128, j * 512:(j + 1) * 512], in_=ts_),
                         reads=[tstok], multi=["mv"], dma=tskey)
            for j in range(2):
                wo, wotok = load_wblk(C_MO + j * 512, 512)
                wz, wztok = load_wblk(C_MZ + j * 512, 512)
                for i in range(32):
                    pa, patok, _ = pacc2.next()
                    for half, (wt_, wtk_) in enumerate(((wo, wotok), (wz, wztok))):
                        for k in range(8):
                            P.op("pe", lambda e, pa=pa, k=k, i=i, wt_=wt_, half=half: e.matmul(pa[:, half * 512:(half + 1) * 512], lhsT=hT[:, k, 1 + i * 128:1 + (i + 1) * 128], rhs=wt_[:, k, :], start=(k == 0), stop=(k == 7)),
                                 reads=[wtk_, "hT"], writes=[patok])
                    P.op("act", lambda e, pa=pa: e.activation(out=sg1, in_=pa, func=AF.Sigmoid), reads=[patok], writes=["sg1"])
                    P.op("dve", lambda e: e.tensor_tensor(out=sg2, in0=sg1[:, 0:512], in1=sg1[:, 512:1024], op=ALU.mult), reads=["sg1"], writes=["sg2"])
                    P.op("dve", lambda e, j=j: e.tensor_tensor(out=sg3, in0=sg2, in1=hnw_bc[:, j * 512:(j + 1) * 512], op=ALU.mult), reads=["sg2", "hnw"], writes=["sg3"])
                    ts_, tstok, tskey = tmst.next()
                    P.op("dve", lambda e, pa=pa, ts_=ts_: e.tensor_tensor(out=ts_, in0=pa[:, 512:1024], in1=sg3, op=ALU.mult), reads=[patok, "sg3"], writes=[tstok])
                    P.op("sp", lambda e, ts_=ts_, i=i, j=j, t0=t0: e.dma_start(out=goz[t0 + i * 128:t0 + (i + 1) * 128, j * 512:(j + 1) * 512], in_=ts_),
                         reads=[tstok], multi=["goz"], dma=tskey)
            wt, wtok = load_wblk(C_MI, 16)
            for i in range(32):
                pa, patok, _ = pacc.next()
                for k in range(8):
                    P.op("pe", lambda e, pa=pa, k=k, i=i, wt=wt: e.matmul(pa[:, 0:16], lhsT=hT[:, k, 1 + i * 128:1 + (i + 1) * 128], rhs=wt[:, k, 0:16], start=(k == 0), stop=(k == 7)),
                         reads=[wtok, "hT"], writes=[patok])
                P.op("dve", lambda e, pa=pa, i=i, seg=seg: e.tensor_copy(out=gates[:, seg * 32 + i, :], in_=pa[:, 0:16]), reads=[patok], multi=["gates"])
            for g, d in enumerate(DILS):
                wt, wtok = load_wblk(C_AQKV + (6 + g) * 512, 512)
                L = SEG // d
                nj = L // 128
                for r in range(d):
                    for jl in range(nj + 1):
                        if jl == 0:
                            p0, p1, a0 = 64, 128, 0
                        elif jl == nj:
                            p0, p1, a0 = 0, 64, L - 64
                        else:
                            p0, p1, a0 = 0, 128, 64 + 128 * (jl - 1)
                        cnt = p1 - p0
                        c_start = 1 + a0 * d + r
                        pa, patok, _ = pacc.next()
                        for k in range(8):
                            P.op("pe", lambda e, pa=pa, k=k, wt=wt, p0=p0, p1=p1, c_start=c_start, cnt=cnt, d=d: e.matmul(pa[p0:p1, :], lhsT=hT[:, k, c_start:c_start + (cnt - 1) * d + 1:d], rhs=wt[:, k, :], start=(k == 0), stop=(k == 7)),
                                 reads=[wtok, "hT"], writes=[patok])
                        ts_, tstok, tskey = tmst.next()
                        eng = evac_engine()
                        if eng == "act":
                            P.op("act", lambda e, ts_=ts_, pa=pa, p0=p0, p1=p1: e.activation(out=ts_[p0:p1, :], in_=pa[p0:p1, :], func=AF.Copy), reads=[patok], writes=[tstok])
                        else:
                            P.op("dve", lambda e, ts_=ts_, pa=pa, p0=p0, p1=p1: e.tensor_copy(out=ts_[p0:p1, :], in_=pa[p0:p1, :]), reads=[patok], writes=[tstok])
                        ti = av_tile_index(g, seg, r, jl)
                        P.op("sp", lambda e, ts_=ts_, g=g, ti=ti, p0=p0, p1=p1: e.dma_start(out=av[g][ti, p0:p1, :], in_=ts_[p0:p1, :]),
                             reads=[tstok], multi=["av%d" % g], dma=tskey)
            P.barrier()
        if debug:
            P.op("sp", lambda e: e.dma_start(out=gates_dbg, in_=gates.rearrange("p c g -> p (c g)")), reads=["gates"], dma="c0")
        M.reset(const_mark)
        if stop_after <= 1:
            P.emit()
            return nc
        LN16 = math.log(16.0)
        cneg = M.f32([128, 1])
        cone = M.f32([128, 1])
        ceps = M.f32([128, 1])
        P.op("pool", lambda e: e.memset(cneg, -LN16), writes=["cneg"])
        P.op("pool", lambda e: e.memset(cone, 1.0), writes=["cone"])
        P.op("pool", lambda e: e.memset(ceps, EPS), writes=["ceps"])
        ph2_mark = M.mark()
        gb = M.f32([128, NCH, 16])
        gsum = M.f32([128, NCH, 16])
        nl = M.f32([128, NCH, 8])
        uu = M.f32([128, NCH, 8])
        P.op("sp", lambda e: e.dma_start(out=gb, in_=bass.AP(gbias_h, 0, [[0, 128], [0, NCH], [1, 16]])), writes=["gb"], dma="c0")
        P.op("dve", lambda e: e.tensor_tensor(out=gsum, in0=gates, in1=gb, op=ALU.add), reads=["gates", "gb"], writes=["gsum"])
        P.op("act", lambda e: e.activation(out=nl, in_=gsum[:, :, 8:16], func=AF.Exp, scale=-1.0), reads=["gsum"], writes=["nl"])
        P.op("act", lambda e: e.activation(out=nl, in_=nl, func=AF.Ln, bias=cone[:, 0:1]), reads=["nl", "cone"], writes=["nl"])
        b3 = lambda b: bank(b)[:, 0:384].rearrange("p (c g) -> p c g", g=4)
        P.op("pe", lambda e: e.matmul(bank(0)[:, 0:384], lhsT=tri_f, rhs=nl[:, :, 0:4], start=True, stop=True), reads=["tri_f", "nl"], writes=["g0"])
        P.op("pe", lambda e: e.matmul(bank(1)[:, 0:384], lhsT=triT_f, rhs=nl[:, :, 4:8], start=True, stop=True), reads=["triT_f", "nl"], writes=["g1"])
        P.op("pe", lambda e: e.matmul(bank(2)[:, 0:384], lhsT=ones_f, rhs=nl[:, :, 0:4], start=True, stop=True), reads=["ones_f", "nl"], writes=["g2"])
        P.op("pe", lambda e: e.matmul(bank(3)[:, 0:384], lhsT=ones_f, rhs=nl[:, :, 4:8], start=True, stop=True), reads=["ones_f", "nl"], writes=["g3"])
        for dr in range(2):
            sl = slice(dr * 4, dr * 4 + 4)
            P.op("dve", lambda e, dr=dr, sl=sl: e.tensor_tensor(out=uu[:, :, sl], in0=gsum[:, :, sl], in1=b3(dr), op=ALU.add), reads=["gsum", "g%d" % dr], writes=["uu%d" % dr])
            P.op("act", lambda e, sl=sl: e.activation(out=wk[:, :, sl], in_=uu[:, :, sl], func=AF.Exp, bias=cneg[:, 0:1]), reads=["uu%d" % dr, "cneg"], writes=["wk%d" % dr])
            P.op("act", lambda e, dr=dr, sl=sl: e.activation(out=flo[:, :, sl], in_=b3(dr), func=AF.Exp), reads=["g%d" % dr], writes=["flo%d" % dr])
            P.op("act", lambda e, dr=dr, sl=sl: e.activation(out=dec[:, :, sl], in_=b3(2 + dr), func=AF.Exp, scale=-1.0), reads=["g%d" % (2 + dr)], writes=["dec%d" % dr])
        P.op("dve", lambda e: e.tensor_scalar(out=decj[:, 0:4], in0=dec[:, 31, 0:4], scalar1=jcol[:, 0:1], scalar2=None, op0=ALU.mult), reads=["dec0", "jcol"], writes=["decj0"])
        P.op("dve", lambda e: e.tensor_scalar(out=decj[:, 4:8], in0=dec[:, 32, 4:8], scalar1=jcol[:, 0:1], scalar2=None, op0=ALU.mult), reads=["dec1", "jcol"], writes=["decj1"])
        gtoks = ["wk0", "wk1", "flo0", "flo1", "dec0", "dec1", "decj0", "decj1"]
        P.barrier()
        M.reset(ph2_mark)
        if stop_after <= 2:
            if debug:
                for nm, tl in (("wk", wk), ("flo", flo), ("dec", dec)):
                    dd = nc.dram_tensor("dbg_" + nm, [128, NCH * 8], F32, kind="ExternalOutput").ap()
                    P.op("sp", lambda e, dd=dd, tl=tl: e.dma_start(out=dd, in_=tl.rearrange("p c g -> p (c g)")), reads=gtoks, dma="c0")
            P.emit()
            return nc

        hpart = M.f32([128, 64, 256])
        NS = 4
        qk_r = [Ring("qk%d" % dr, [M.bf16([128, 4, 128]) for _ in range(NS)]) for dr in range(2)]
        vx_r = [Ring("vx%d" % dr, [M.bf16([128, 258]) for _ in range(NS)]) for dr in range(2)]
        gz_r = [Ring("gz%d" % dr, [M.bf16([128, 256]) for _ in range(NS)]) for dr in range(2)]
        kw_r = [Ring("kw%d" % dr, [M.bf16([128, 256]) for _ in range(3)]) for dr in range(2)]
        sc_r = [Ring("sc%d" % dr, [M.bf16([128, 128]) for _ in range(3)]) for dr in range(2)]
        Tst = [M.f32([128, 2, 257]) for _ in range(2)]
        Cb = [M.bf16([128, 2, 258]) for _ in range(2)]
        sm_r = [Ring("sm%d" % dr, [M.f32([128, 4]) for _ in range(3)]) for dr in range(2)]
        hh_r = Ring("hh", [M.f32([128, 256]) for _ in range(4)])
        ns_r = [Ring("ns%d" % dr, [M.f32([128, 258]) for _ in range(3)]) for dr in range(2)]
        yv_r = Ring("yv", [M.bf16([128, 256]) for _ in range(6)])
        yst_r = Ring("yst", [M.bf16([128, 2, 128]) for _ in range(4)])
        sq3 = M.bf16([128, 256])
        for dr in range(2):
            for k2 in range(NS):
                vt_, _, _ = vx_r[dr].next()
                P.op("pool", lambda e, vt_=vt_: e.memset(vt_[:, 256:257], 1.0), multi=[("vx%d" % dr, k2)])
        qkT_v = qkT.rearrange("(a p) t -> p a t", p=128)
        yAT_v = yAT.rearrange("(a p) t -> p a t", p=128)
        maskd = [mask_f, mask_b]
        def A_regions(dr, par):
            base = 0
            return (bank_bf(dr)[:, 2 * base:2 * base + 256], bank(dr)[:, base + 128:base + 256], ("bA", dr))

        for h in range(4):
            for (cbase, nchk) in ((0, 64), (64, 32)):
                for dr in range(2):
                    P.op("pool", lambda e, dr=dr: e.memset(Tst[dr], 0.0), writes=[("T", dr)])
                    P.op("pool", lambda e, dr=dr: e.memset(Cb[dr], 0.0), writes=[("Cb", dr)])
                U = {}

                def chunk_of(i, dr):
                    return cbase + i if dr == 0 else cbase + nchk - 1 - i

                def loads(i):
                    fin = i >= nchk // 2
                    for dr in range(2):
                        c = chunk_of(i, dr)
                        qk_t, qktok, qkkey = qk_r[dr].next()
                        vx_t, vxtok, vxkey = vx_r[dr].next()
                        u = dict(c=c, qk=qk_t, qktok=qktok, vx=vx_t, vxtok=vxtok, fin=fin)
                        P.op("sp", lambda e, qk_t=qk_t, c=c, h=h: e.dma_start(out=qk_t[:, 0:2, :], in_=qkT_v[:, 2 * h:2 * h + 2, c * 128:(c + 1) * 128]), reads=["qkT"], multi=[qktok], dma=qkkey)
                        P.op("sp", lambda e, qk_t=qk_t, c=c, h=h: e.dma_start(out=qk_t[:, 2:4, :], in_=qkT_v[:, 8 + 2 * h:8 + 2 * h + 2, c * 128:(c + 1) * 128]), reads=["qkT"], multi=[qktok], dma=qkkey)
                        P.op("sp", lambda e, vx_t=vx_t, c=c, h=h: e.dma_start(out=vx_t[:, 0:256], in_=mv[c * 128:(c + 1) * 128, h * 256:(h + 1) * 256]), reads=["mv"], multi=[vxtok], dma=vxkey)
                        if fin:
                            gz_t, gztok, gzkey = gz_r[dr].next()
                            u.update(gz=gz_t, gztok=gztok)
                            P.op("sp", lambda e, gz_t=gz_t, c=c, h=h: e.dma_start(out=gz_t, in_=goz[c * 128:(c + 1) * 128, h * 256:(h + 1) * 256]), reads=["goz"], writes=[gztok], dma=gzkey)
                        U[(i, dr)] = u

                def stage12(i):
                    for dr in range(2):
                        u = U[(i, dr)]
                        c, qk_t, qktok = u["c"], u["qk"], u["qktok"]
                        lane = dr * 4 + h
                        ktp, stp, tA = A_regions(dr, i % 2)
                        for hf in range(2):
                            P.op("pe", lambda e, hf=hf, qk_t=qk_t, ktp=ktp: e.transpose(ktp[:, hf * 128:(hf + 1) * 128], qk_t[:, 2 + hf, :], ident_b), reads=[qktok, "ident_b"], writes=[tA])
                        for hf in range(2):
                            P.op("pe", lambda e, hf=hf, qk_t=qk_t, stp=stp: e.matmul(stp, lhsT=qk_t[:, 2 + hf, :], rhs=qk_t[:, hf, :], start=(hf == 0), stop=(hf == 1)), reads=[qktok], writes=[tA])
                    for dr in range(2):
                        u = U[(i, dr)]
                        c = u["c"]
                        lane = dr * 4 + h
                        ktp, stp, tA = A_regions(dr, i % 2)
                        kw_t, kwtok, _ = kw_r[dr].next()
                        sc_t, sctok, _ = sc_r[dr].next()
                        u.update(kw=kw_t, kwtok=kwtok, sc=sc_t, sctok=sctok)
                        wcol = wk[:, c, lane:lane + 1]
                        P.op("dve", lambda e, kw_t=kw_t, ktp=ktp, wcol=wcol: e.tensor_scalar(out=kw_t, in0=ktp, scalar1=wcol, scalar2=None, op0=ALU.mult), reads=[tA, "wk%d" % dr], writes=[kwtok])
                        P.op("dve", lambda e, sc_t=sc_t, stp=stp, wcol=wcol, dr=dr: e.scalar_tensor_tensor(out=sc_t, in0=stp, scalar=wcol, in1=maskd[dr], op0=ALU.mult, op1=ALU.mult), reads=[tA, "wk%d" % dr, "mask_f", "mask_b"], writes=[sctok])

                def stage3(i):
                    for dr in range(2):
                        u = U[(i, dr)]
                        kvp = psum[:, (4 + 2 * dr) * 512:(6 + 2 * dr) * 512]
                        for hf in range(2):
                            P.op("pe", lambda e, hf=hf, kw_t=u["kw"], vx_t=u["vx"], kvp=kvp: e.matmul(kvp[:, hf * 512:hf * 512 + 257], lhsT=kw_t[:, hf * 128:(hf + 1) * 128], rhs=vx_t[:, 0:257], start=True, stop=True), reads=[u["kwtok"], u["vxtok"]], writes=[("bKV", dr)])
                    for dr in range(2):
                        u = U[(i, dr)]
                        bN = bank(2 + dr)
                        P.op("pe", lambda e, sc_t=u["sc"], vx_t=u["vx"], bN=bN: e.matmul(bN[:, 0:257], lhsT=sc_t, rhs=vx_t[:, 0:257], start=True, stop=False), reads=[u["sctok"], u["vxtok"]], writes=[("bN", dr)])
                        for hf in range(2):
                            P.op("pe", lambda e, hf=hf, qk_t=u["qk"], bN=bN, dr=dr: e.matmul(bN[:, 0:257], lhsT=qk_t[:, hf, :], rhs=Cb[dr][:, hf, 0:257], start=False, stop=(hf == 1)), reads=[u["qktok"], ("Cb", dr)], writes=[("bN", dr)])

                def stage4(i):
                    for dr in range(2):
                        u = U[(i, dr)]
                        c = u["c"]
                        lane = dr * 4 + h
                        cprev = c - 1 if dr == 0 else c + 1
                        kvp = psum[:, (4 + 2 * dr) * 512:(6 + 2 * dr) * 512]
                        joined_prev = (cbase == 0) and ((dr == 0 and c == 32) or (dr == 1 and c == 31))
                        joined_next = (cbase == 0) and ((dr == 0 and c == 31) or (dr == 1 and c == 32))
                        dprev = decj[:, lane:lane + 1] if joined_prev else dec[:, (c if i == 0 else cprev), lane:lane + 1]
                        dnext = decj[:, lane:lane + 1] if joined_next else dec[:, c, lane:lane + 1]
                        kv3 = kvp.rearrange("p (a b) -> p a b", a=2)[:, :, 0:257]
                        P.op("dve", lambda e, dr=dr, dprev=dprev, kv3=kv3: e.scalar_tensor_tensor(out=Tst[dr], in0=Tst[dr], scalar=dprev, in1=kv3, op0=ALU.mult, op1=ALU.add), reads=[("T", dr), ("bKV", dr), "dec%d" % dr, "decj%d" % dr], writes=[("T", dr)])
                        if i < nchk - 1:
                            P.op("act", lambda e, dr=dr, dnext=dnext: e.activation(out=Cb[dr][:, :, 0:257], in_=Tst[dr], func=AF.Copy, scale=dnext), reads=[("T", dr), "dec%d" % dr, "decj%d" % dr], writes=[("Cb", dr)])

                def stage5(i):
                    for dr in range(2):
                        u = U[(i, dr)]
                        c, fin = u["c"], u["fin"]
                        lane = dr * 4 + h
                        bN = bank(2 + dr)
                        tN = ("bN", dr)
                        sm, smtok, _ = sm_r[dr].next()
                        ns, nstok, _ = ns_r[dr].next()
                        P.op("act", lambda e, ns=ns, bN=bN: e.activation(out=ns[:, 0:257], in_=bN[:, 0:257], func=AF.Copy), reads=[tN], writes=[nstok])
                        P.op("dve", lambda e, sm=sm, ns=ns: e.scalar_tensor_tensor(out=sm[:, 0:1], in0=ns[:, 256:257], scalar=-1.0, in1=ns[:, 256:257], op0=ALU.mult, op1=ALU.max), reads=[nstok], writes=[smtok])
                        P.op("dve", lambda e, sm=sm, c=c, lane=lane: e.tensor_tensor(out=sm[:, 1:2], in0=sm[:, 0:1], in1=flo[:, c, lane:lane + 1], op=ALU.max), reads=[smtok, "flo%d" % dr], writes=[smtok])
                        P.op("dve", lambda e, sm=sm: e.reciprocal(out=sm[:, 2:3], in_=sm[:, 1:2]), reads=[smtok], writes=[smtok])
                        hp_ = hpart[:, c - cbase, :]
                        hptok = ("hpart", c - cbase)
                        if not fin:
                            P.op("act", lambda e, hp_=hp_, ns=ns, sm=sm: e.activation(out=hp_, in_=ns[:, 0:256], func=AF.Copy, scale=sm[:, 2:3]), reads=[nstok, smtok], writes=[hptok])
                        else:
                            hh, hhtok, _ = hh_r.next()
                            yv, yvtok, _ = yv_r.next()
                            gz_t, gztok = u["gz"], u["gztok"]
                            P.op("dve", lambda e, hh=hh, ns=ns, sm=sm, hp_=hp_: e.scalar_tensor_tensor(out=hh, in0=ns[:, 0:256], scalar=sm[:, 2:3], in1=hp_, op0=ALU.mult, op1=ALU.add), reads=[nstok, smtok, hptok], writes=[hhtok])
                            P.op("act", lambda e, hh=hh, sm=sm: e.activation(out=sq3, in_=hh, func=AF.Square, accum_out=sm[:, 3:4]), reads=[hhtok, smtok], writes=[smtok, "sq3"])
                            P.op("act", lambda e, sm=sm: e.activation(out=sm[:, 0:1], in_=sm[:, 3:4], func=AF.Ln, scale=1.0 / 256, bias=ceps[:, 0:1]), reads=[smtok, "ceps"], writes=[smtok])
                            P.op("act", lambda e, sm=sm: e.activation(out=sm[:, 1:2], in_=sm[:, 0:1], func=AF.Exp, scale=-0.5), reads=[smtok], writes=[smtok])
                            P.op("dve", lambda e, yv=yv, hh=hh, sm=sm, gz_t=gz_t: e.scalar_tensor_tensor(out=yv, in0=hh, scalar=sm[:, 1:2], in1=gz_t, op0=ALU.mult, op1=ALU.mult), reads=[hhtok, smtok, gztok], writes=[yvtok])
                            u.update(yv=yv, yvtok=yvtok)

                def stage6(i):
                    for dr in range(2):
                        u = U[(i, dr)]
                        if not u["fin"]:
                            continue
                        c = u["c"]
                        yv, yvtok = u["yv"], u["yvtok"]
                        yTp = bank_bf(dr)[:, 512:768]
                        yst, ysttok, ystkey = yst_r.next()
                        for hf in range(2):
                            P.op("pe", lambda e, hf=hf, yv=yv, yTp=yTp: e.transpose(yTp[:, hf * 128:(hf + 1) * 128], yv[:, hf * 128:(hf + 1) * 128], ident_b), reads=[yvtok, "ident_b"], writes=[("bY", dr)])
                        P.op("act", lambda e, yst=yst, yTp=yTp: e.activation(out=yst, in_=yTp.rearrange("p (a b) -> p a b", a=2), func=AF.Copy), reads=[("bY", dr)], writes=[ysttok])
                        P.op("act", lambda e, yst=yst, c=c, h=h: e.dma_start(out=yAT_v[:, 2 * h:2 * h + 2, c * 128:(c + 1) * 128], in_=yst), reads=[ysttok], multi=["yAT"], dma=ystkey)

                PF = 2
                for i in range(min(PF, nchk)):
                    loads(i)
                stage12(0)
                for i in range(nchk):
                    if i + PF < nchk:
                        loads(i + PF)
                    if i + 1 < nchk:
                        stage12(i + 1)
                    if i >= 1:
                        stage6(i - 1)
                    stage3(i)
                    stage4(i)
                    stage5(i)
                stage6(nchk - 1)
        P.barrier()
        M.reset(ph2_mark)
        if stop_after <= 3:
            P.emit()
            return nc
        bias_sb = M.f32([128, 12, 4, 128])
        rel_sb = M.f32([33, 24])
        oh_sb = M.f32([33, 6, 256])
        grep_r = Ring("grep", [M.f32([128, 256]) for _ in range(2)])
        P.op("sp", lambda e: e.dma_start(out=rel_sb, in_=relx), writes=["rel_sb"], dma="c0")
        P.op("sp", lambda e: e.dma_start(out=oh_sb, in_=oh_in.rearrange("g ab b j -> b (g ab) j")), writes=["oh_sb"], dma="c1")
        gs = gs_h.ap()
        for g in range(3):
            for ab in range(2):
                idx = g * 2 + ab
                for h8 in range(8):
                    row = idx * 8 + h8
                    gr_t, grtok, grkey = grep_r.next()
                    P.op("pe", lambda e, g=g, idx=idx, h8=h8: e.matmul(bank(0)[:, 0:256], lhsT=rel_sb[:, g * 8 + h8:g * 8 + h8 + 1].broadcast_to([33, 128]), rhs=oh_sb[:, idx, :], start=True, stop=True), reads=["rel_sb", "oh_sb"], writes=["pb0"])
                    P.op("dve", lambda e, gr_t=gr_t: e.tensor_copy(out=gr_t, in_=bank(0)[:, 0:256]), reads=["pb0"], writes=[grtok])
                    P.op("sp", lambda e, gr_t=gr_t, row=row: e.dma_start(out=gs[row], in_=gr_t), reads=[grtok], multi=["gs"], dma=grkey)
        for g in range(3):
            for hp in range(4):
                for hh in range(2):
                    for ab in range(2):
                        row = (g * 2 + ab) * 8 + hp * 2 + hh
                        src = bass.AP(gs_h, row * 128 * 256 + 127, [[255, 128], [1, 128]])
                        P.op("sp", lambda e, src=src, g=g, hp=hp, hh=hh, ab=ab: e.dma_start(out=bias_sb[:, g * 4 + hp, hh * 2 + ab, :], in_=src), reads=["gs"], multi=["bias_sb"], dma="c%d" % ((hh * 2 + ab) % 3))
        P.barrier()
        ph4_mark = M.mark()
        qT_r = Ring("aq", [M.bf16([128, SEG]) for _ in range(2)])
        kT_r = Ring("ak", [M.bf16([128, SEG + 16 * 128]) for _ in range(2)])
        vt_r = Ring("avt", [M.bf16([128, 48, 3, 64]) for _ in range(2)])
        acc_sb = [M.f32([128, SEG]) for _ in range(2)]
        tb_r = Ring("tb", [M.f32([128, 256]) for _ in range(10)])
        pT_r = Ring("pT", [M.bf16([128, 256]) for _ in range(10)])
        rd = M.f32([128, 1024])
        sz_r = Ring("sz", [M.bf16([128, 1024]) for _ in range(2)])
        tt_ = M.f32([128, 1024])
        yb_r = Ring("ybst", [M.bf16([128, 1024]) for _ in range(2)])
        S_r = [Ring("Sps0", [bank(0)[:, 0:256], bank(1)[:, 0:256]]),
               Ring("Sps1", [bank(6)[:, 0:256], bank(7)[:, 0:256]])]
        acc_r = [Ring("accps0", [bank(2), bank(3)]), Ring("accps1", [bank(4), bank(5)])]
        for seg in range(NSEG):
            t0 = seg * SEG
            ljoin, rjoin = (seg == 1), (seg == 0)
            for hp in range(4):
                for g, d in enumerate(DILS):
                    L = SEG // d
                    nj = L // 128
                    W = L + 128
                    qf, qtok, qkey = qT_r.next()
                    kf, ktok, kkey = kT_r.next()
                    vf, vtok, vkey = vt_r.next()
                    q3 = qf.rearrange("p (r a) -> p r a", r=d)
                    k3 = kf[:, 0:d * W].rearrange("p (r a) -> p r a", r=d)
                    rows = slice(hp * 128, (hp + 1) * 128)
                    P.op("sp", lambda e, qf=qf, g=g, rows=rows, t0=t0: e.dma_start(out=qf, in_=aqk[g, rows, t0:t0 + SEG]), reads=["aqk"], writes=[qtok], dma=qkey)
                    P.op("sp", lambda e, k3=k3, g=g, rows=rows, d=d, L=L, t0=t0: e.dma_start(out=k3[:, :, 64:64 + L], in_=aqk[3 + g, rows, t0:t0 + SEG].rearrange("p (r a) -> p r a", r=d)), reads=["aqk"], multi=[ktok], dma=kkey)
                    if ljoin:
                        P.op("sp", lambda e, k3=k3, g=g, rows=rows, d=d, L=L, t0=t0: e.dma_start(out=k3[:, :, 0:64], in_=aqk[3 + g, rows, t0 - SEG:t0].rearrange("p (r a) -> p r a", r=d)[:, :, L - 64:L]), reads=["aqk"], multi=[ktok], dma=kkey)
                    else:
                        P.op("pool", lambda e, k3=k3: e.memset(k3[:, :, 0:64], 0.0), multi=[ktok])
                    if rjoin:
                        P.op("sp", lambda e, k3=k3, g=g, rows=rows, d=d, L=L, t0=t0: e.dma_start(out=k3[:, :, 64 + L:128 + L], in_=aqk[3 + g, rows, t0 + SEG:t0 + 2 * SEG].rearrange("p (r a) -> p r a", r=d)[:, :, 0:64]), reads=["aqk"], multi=[ktok], dma=kkey)
                    else:
                        P.op("pool", lambda e, k3=k3, L=L: e.memset(k3[:, :, 64 + L:128 + L], 0.0), multi=[ktok])
                    for r in range(d):
                        ti0 = av_tile_index(g, seg, r, 0)
                        for hh in range(2):
                            P.op("sp", lambda e, vf=vf, g=g, r=r, ti0=ti0, nj=nj, hp=hp, hh=hh: e.dma_start(out=vf[:, r * (nj + 1):(r + 1) * (nj + 1), 2 * hh, :], in_=av[g][ti0:ti0 + nj + 1, :, hp * 128 + hh * 64:hp * 128 + (hh + 1) * 64].rearrange("t k e -> k t e")), reads=["av%d" % g], multi=[vtok], dma=vkey)
                    ntile = d * (nj + 1)
                    P.op("pool", lambda e, vf=vf, ntile=ntile: e.memset(vf[:, 0:ntile, 1, :], 1.0), multi=[vtok])
                    vfl = vf.rearrange("p t a e -> p t (a e)")
                    if ljoin:
                        P.op("dve", lambda e, vfl=vfl, nj=nj, ntile=ntile: e.tensor_scalar(out=vfl[0:64, 0:ntile:nj + 1, :], in0=vfl[0:64, 0:ntile:nj + 1, :], scalar1=jcol[0:64, 0:1], scalar2=None, op0=ALU.mult), reads=[vtok, "jcol"], writes=[vtok])
                    else:
                        P.op("pool", lambda e, vfl=vfl, nj=nj, ntile=ntile: e.memset(vfl[0:64, 0:ntile:nj + 1, :], 0.0), reads=[vtok], writes=[vtok])
                    if rjoin:
                        P.op("dve", lambda e, vfl=vfl, nj=nj, ntile=ntile: e.tensor_scalar(out=vfl[64:128, nj:ntile:nj + 1, :], in0=vfl[64:128, nj:ntile:nj + 1, :], scalar1=jcol[64:128, 0:1], scalar2=None, op0=ALU.mult), reads=[vtok, "jcol"], writes=[vtok])
                    else:
                        P.op("pool", lambda e, vfl=vfl, nj=nj, ntile=ntile: e.memset(vfl[64:128, nj:ntile:nj + 1, :], 0.0), reads=[vtok], writes=[vtok])
                    units = []
                    for r in range(d):
                        for i in range(nj):
                            for hh in range(2):
                                units.append((r, i, hh))
                    KSK = 3
                    UU = {}
                    accs = {}

                    def stA(un):
                        r, i, hh = un
                        S, Stok, _ = S_r[hh].next()
                        tb, tbtok, _ = tb_r.next()
                        pT, pTtok, _ = pT_r.next()
                        UU[un] = (pT, pTtok)
                        bsl = bias_sb[:, g * 4 + hp, 2 * hh:2 * hh + 2, :].rearrange("p a b -> p (a b)")
                        for ab in range(2):
                            j = i + ab
                            P.op("pe", lambda e, S=S, hh=hh, r=r, j=j, ab=ab, i=i, k3=k3, q3=q3: e.matmul(S[:, ab * 128:(ab + 1) * 128], lhsT=k3[hh * 64:(hh + 1) * 64, r, 128 * j:128 * j + 128], rhs=q3[hh * 64:(hh + 1) * 64, r, i * 128:(i + 1) * 128], start=True, stop=True), reads=[qtok, ktok], writes=[Stok])
                        P.op("dve", lambda e, tb=tb, S=S, bsl=bsl: e.tensor_tensor(out=tb, in0=S, in1=bsl, op=ALU.add), reads=[Stok, "bias_sb"], writes=[tbtok])
                        P.op("act", lambda e, tb=tb, pT=pT: e.activation(out=pT, in_=tb, func=AF.Exp), reads=[tbtok], writes=[pTtok])

                    def stD(un):
                        r, i, hh = un
                        pT, pTtok = UU.pop(un)
                        i0 = (i // 4) * 4
                        nq = min(4, nj - i0)
                        qi = i - i0
                        key = (r, i0, hh)
                        if key not in accs:
                            accs[key] = acc_r[hh].next()
                        ap_, aptok, _ = accs[key]
                        for ab in range(2):
                            ti = r * (nj + 1) + i + ab
                            P.op("pe", lambda e, hh=hh, ab=ab, ti=ti, qi=qi, pT=pT, ap_=ap_, vf=vf: e.matmul(ap_[:, qi * 128:(qi + 1) * 128], lhsT=vf[:, ti, hh:hh + 2, :].rearrange("p a e -> p (a e)"), rhs=pT[:, ab * 128:(ab + 1) * 128], start=(ab == 0), stop=(ab == 1)), reads=[vtok, pTtok], writes=[aptok])
                        if qi == nq - 1:
                            dst = acc_sb[hh].rearrange("p (a r) -> p r a", r=d)[:, r, i0 * 128:(i0 + nq) * 128]
                            srcp = ap_[:, 0:nq * 128]
                            if g == 0:
                                P.op("act", lambda e, dst=dst, srcp=srcp: e.activation(out=dst, in_=srcp, func=AF.Copy), reads=[aptok], multi=[("accsb", hh)])
                            else:
                                P.op("dve", lambda e, dst=dst, srcp=srcp: e.tensor_tensor(out=dst, in0=srcp, in1=dst, op=ALU.add), reads=[aptok, ("accsb", hh)], multi=[("accsb", hh)])
                            del accs[key]

                    for n_ in range(len(units) + KSK):
                        if n_ < len(units):
                            stA(units[n_])
                        if n_ >= KSK:
                            stD(units[n_ - KSK])
                for pc in range(4):
                    cs = slice(pc * 1024, (pc + 1) * 1024)
                    sz, sztok, szkey = sz_r.next()
                    yb, ybtok, ybkey = yb_r.next()
                    P.op("sp", lambda e, sz=sz, hp=hp, pc=pc, t0=t0: e.dma_start(out=sz, in_=azT[hp * 128:(hp + 1) * 128, t0 + pc * 1024:t0 + (pc + 1) * 1024]), reads=["azT"], writes=[sztok], dma=szkey)
                    P.op("act", lambda e, cs=cs: e.activation(out=rd[0:64, :], in_=acc_sb[0][64:128, cs], func=AF.Copy), reads=[("accsb", 0)], multi=["rd"])
                    P.op("act", lambda e, cs=cs: e.activation(out=rd[64:128, :], in_=acc_sb[1][0:64, cs], func=AF.Copy), reads=[("accsb", 1)], multi=["rd"])
                    P.op("dve", lambda e: e.reciprocal(out=rd, in_=rd), reads=["rd"], writes=["rd"])
                    P.op("dve", lambda e, sz=sz: e.tensor_tensor(out=tt_, in0=rd, in1=sz, op=ALU.mult), reads=["rd", sztok], writes=["tt_"])
                    P.op("dve", lambda e, yb=yb, cs=cs: e.tensor_tensor(out=yb[0:64, :], in0=acc_sb[0][0:64, cs], in1=tt_[0:64, :], op=ALU.mult), reads=[("accsb", 0), "tt_"], multi=[ybtok])
                    P.op("dve", lambda e, yb=yb, cs=cs: e.tensor_tensor(out=yb[64:128, :], in0=acc_sb[1][64:128, cs], in1=tt_[64:128, :], op=ALU.mult), reads=[("accsb", 1), "tt_"], multi=[ybtok])
                    P.op("sp", lambda e, yb=yb, hp=hp, pc=pc, t0=t0: e.dma_start(out=yBT[hp * 128:(hp + 1) * 128, t0 + pc * 1024:t0 + (pc + 1) * 1024], in_=yb), reads=[ybtok], multi=["yBT"], dma=ybkey)
        P.barrier()
        M.reset(ph2_mark)
        if stop_after <= 4:
            P.emit()
            return nc

        wpm_sb = M.bf16([128, 8, 1024])
        wpa_sb = M.bf16([128, 4, 1024])
        wout_sb = M.bf16([128, 8, 1024])
        postw_bc = M.f32([128, 1024])
        P.op("sp", lambda e: e.dma_start(out=postw_bc, in_=postw_h.ap().partition_broadcast(128)), writes=["postw"], dma="c0")
        wst5 = Ring("w5", [M.f32([128, 1024]) for _ in range(2)])
        n = 0
        for (wsrc, wdst, nk, nm) in ((w_pm, wpm_sb, 8, "wpm"), (w_pa, wpa_sb, 4, "wpa"), (w_out, wout_sb, 8, "wout")):
            for k in range(nk):
                s_t, s_tok, s_key = wst5.next()
                P.op("sp", lambda e, s_t=s_t, wsrc=wsrc, k=k: e.dma_start(out=s_t, in_=wsrc[k * 128:(k + 1) * 128, :]), writes=[s_tok], dma=s_key)
                if n % 2 == 0:
                    P.op("dve", lambda e, s_t=s_t, wdst=wdst, k=k: e.tensor_copy(out=wdst[:, k, :], in_=s_t), reads=[s_tok], multi=[nm])
                else:
                    P.op("act", lambda e, s_t=s_t, wdst=wdst, k=k: e.activation(out=wdst[:, k, :], in_=s_t, func=AF.Copy), reads=[s_tok], multi=[nm])
                n += 1
        ya_r = Ring("ya", [M.bf16([128, 8, 512]) for _ in range(2)])
        ybb_r = Ring("ybb", [M.bf16([128, 4, 512]) for _ in range(2)])
        gt_r = Ring("gt", [M.bf16([128, 16, 512]) for _ in range(2)])
        x5_r = Ring("x5", [M.f32([128, 1024]) for _ in range(8)])
        mT_r = Ring("mT", [M.bf16([128, 8, 512]) for _ in range(2)])
        t1_r = Ring("t1", [M.f32([128, 512]) for _ in range(2)])
        t2_r = Ring("t2", [M.f32([128, 512]) for _ in range(2)])
        yn_r = Ring("yn", [M.f32([128, 1024]) for _ in range(2)])
        yo_r = Ring("yo", [M.f32([128, 1024]) for _ in range(2)])
        s5_r = Ring("s5", [M.f32([128, 4]) for _ in range(2)])
        sq5 = M.bf16([128, 1024])
        pm_r = Ring("pm", [bank(0), bank(1)])
        pa_r = Ring("pa", [bank(2), bank(3)])
        o_r = Ring("ops", [psum[:, 4 * 512:6 * 512], psum[:, 6 * 512:8 * 512]])
        yBT_v = yBT.rearrange("(a p) t -> p a t", p=128)
        gT_v = gT.rearrange("(a p) t -> p a t", p=128)
        def load_x5(tt_):
            res_ = []
            for j in range(4):
                x5, x5tok, x5key = x5_r.next()
                r0 = tt_ * 512 + j * 128
                P.op("sp", lambda e, x5=x5, r0=r0: e.dma_start(out=x5, in_=x[r0:r0 + 128, :]), writes=[x5tok], dma=x5key)
                res_.append((x5, x5tok))
            return res_

        for tt in range(T // 512):
            ts0 = tt * 512
            ya, yatok, yakey = ya_r.next()
            ybb, ybbtok, ybbkey = ybb_r.next()
            gt, gttok, gtkey = gt_r.next()
            mT, mTtok, _ = mT_r.next()
            P.op("sp", lambda e, ya=ya, ts0=ts0: e.dma_start(out=ya, in_=yAT_v[:, :, ts0:ts0 + 512]), reads=["yAT"], writes=[yatok], dma=yakey)
            P.op("sp", lambda e, ybb=ybb, ts0=ts0: e.dma_start(out=ybb, in_=yBT_v[:, :, ts0:ts0 + 512]), reads=["yBT"], writes=[ybbtok], dma=ybbkey)
            P.op("sp", lambda e, gt=gt, ts0=ts0: e.dma_start(out=gt, in_=gT_v[:, :, ts0:ts0 + 512]), reads=["gT"], writes=[gttok], dma=gtkey)
            for dc in range(8):
                pm, pmtok, _ = pm_r.next()
                pa, patok, _ = pa_r.next()
                t1, t1tok, _ = t1_r.next()
                t2, t2tok, _ = t2_r.next()
                for kc in range(8):
                    P.op("pe", lambda e, pm=pm, kc=kc, dc=dc, ya=ya: e.matmul(pm, lhsT=wpm_sb[:, kc, dc * 128:(dc + 1) * 128], rhs=ya[:, kc, :], start=(kc == 0), stop=(kc == 7)), reads=["wpm", yatok], writes=[pmtok])
                for kc in range(4):
                    P.op("pe", lambda e, pa=pa, kc=kc, dc=dc, ybb=ybb: e.matmul(pa, lhsT=wpa_sb[:, kc, dc * 128:(dc + 1) * 128], rhs=ybb[:, kc, :], start=(kc == 0), stop=(kc == 3)), reads=["wpa", ybbtok], writes=[patok])
                P.op("dve", lambda e, t1=t1, pm=pm, gt=gt, dc=dc: e.tensor_tensor(out=t1, in0=pm, in1=gt[:, dc, :], op=ALU.mult), reads=[pmtok, gttok], writes=[t1tok])
                P.op("dve", lambda e, t2=t2, pa=pa, gt=gt, dc=dc: e.tensor_tensor(out=t2, in0=pa, in1=gt[:, 8 + dc, :], op=ALU.mult), reads=[patok, gttok], writes=[t2tok])
                P.op("dve", lambda e, t1=t1, t2=t2, mT=mT, dc=dc: e.tensor_tensor(out=mT[:, dc, :], in0=t1, in1=t2, op=ALU.add), reads=[t1tok, t2tok], multi=[mTtok])
            if tt == 0:
                x5next = load_x5(0)
            x5s = x5next
            if tt + 1 < T // 512:
                x5next = load_x5(tt + 1)
            for j in range(4):
                o_, otok, _ = o_r.next()
                x5, x5tok = x5s[j]
                s5, s5tok, _ = s5_r.next()
                yn, yntok, _ = yn_r.next()
                yo, yotok, yokey = yo_r.next()
                r0 = ts0 + j * 128
                for half in range(2):
                    for dc in range(8):
                        P.op("pe", lambda e, o_=o_, half=half, dc=dc, j=j, mT=mT: e.matmul(o_[:, half * 512:(half + 1) * 512], lhsT=mT[:, dc, j * 128:(j + 1) * 128], rhs=wout_sb[:, dc, half * 512:(half + 1) * 512], start=(dc == 0), stop=(dc == 7)), reads=[mTtok, "wout"], writes=[otok])
                P.op("act", lambda e, o_=o_, s5=s5: e.activation(out=sq5, in_=o_, func=AF.Square, accum_out=s5[:, 0:1]), reads=[otok], writes=[s5tok, "sq5"])
                P.op("act", lambda e, s5=s5: e.activation(out=s5[:, 1:2], in_=s5[:, 0:1], func=AF.Ln, scale=1.0 / D, bias=ceps[:, 0:1]), reads=[s5tok, "ceps"], writes=[s5tok])
                P.op("act", lambda e, s5=s5: e.activation(out=s5[:, 2:3], in_=s5[:, 1:2], func=AF.Exp, scale=-0.5), reads=[s5tok], writes=[s5tok])
                P.op("dve", lambda e, yn=yn, o_=o_, s5=s5: e.scalar_tensor_tensor(out=yn, in0=o_, scalar=s5[:, 2:3], in1=postw_bc, op0=ALU.mult, op1=ALU.mult), reads=[otok, s5tok, "postw"], writes=[yntok])
                P.op("dve", lambda e, yo=yo, yn=yn, x5=x5: e.tensor_tensor(out=yo, in0=yn, in1=x5, op=ALU.add), reads=[yntok, x5tok], writes=[yotok])
                P.op("sp", lambda e, yo=yo, r0=r0: e.dma_start(out=y[r0:r0 + 128, :], in_=yo), reads=[yotok], dma=yokey)
        P.emit()
        return nc


def host_inputs(x_prompt, x_sample, pre_norm_w, w_in, m_conv_w, m_conv_b, m_igate_b, m_fgate_b, m_head_norm_w,
                w_proj_m, w_proj_a, w_out, post_norm_w, rel_bias_table):
    f = np.float32
    common = {
        "w_in": np.ascontiguousarray(w_in[0], f),
        "prew": np.ascontiguousarray(pre_norm_w[0].reshape(8, 128).T, f),
        "convw": np.ascontiguousarray(m_conv_w[0].reshape(3, 16, 128).transpose(2, 1, 0), f),
        "convb": np.ascontiguousarray(m_conv_b[0].reshape(16, 128).T, f),
        "gbias": np.concatenate([m_igate_b[0].reshape(-1), m_fgate_b[0].reshape(-1)]).astype(f).reshape(1, 16),
        "hnw": np.ascontiguousarray(m_head_norm_w[0].reshape(1, 1024), f),
        "w_pm": np.ascontiguousarray(w_proj_m[0], f),
        "w_pa": np.ascontiguousarray(w_proj_a[0], f),
        "w_out": np.ascontiguousarray(w_out[0], f),
        "postw": np.ascontiguousarray(post_norm_w[0].reshape(1, 1024), f),
        "relx": np.concatenate([rel_bias_table.reshape(32, 24), np.ones((1, 24), f)], 0).astype(f),
        "ident": np.eye(128, dtype=f),
        "tri": np.triu(np.ones((128, 128), f)),
        "oh": bias_onehots(),
    }
    maps = []
    for c in range(8):
        if c < 4:
            xc = np.concatenate([x_prompt[c], x_sample[c]], 0)
            j = 1.0
        else:
            b = 4 + 3 * (c - 4)
            xc = np.concatenate([x_sample[b], x_sample[b + 1], x_sample[b + 2]], 0)
            j = 0.0
        m = dict(common)
        m["x"] = np.ascontiguousarray(xc, f)
        m["jflag"] = np.full((128, 1), j, f)
        maps.append(m)
    return maps


_CACHE = {}


def kernel(**inputs):
    inputs = {k: np.asarray(v) for k, v in inputs.items()}
    maps = host_inputs(**inputs)
    if "nc" not in _CACHE:
        _CACHE["nc"] = build_program()
    res = run_bass_kernel_spmd(_CACHE["nc"], maps, core_ids=list(range(8)))
    ys = [r["y"] for r in res.results]
    y_prompt = np.stack([ys[c][0:8192] for c in range(4)], 0)
    samples = [None] * 16
    for c in range(4):
        samples[c] = ys[c][8192:12288]
    for c in range(4, 8):
        b = 4 + 3 * (c - 4)
        for i in range(3):
            samples[b + i] = ys[c][i * 4096:(i + 1) * 4096]
    y_sample = np.stack(samples, 0)
    return (y_prompt.astype(np.float32), y_sample.astype(np.float32))
```

```python
import contextlib
import math
import numpy as np
import ml_dtypes
import concourse.bass as bass
import concourse.mybir as mybir
from concourse.bass_utils import run_bass_kernel_spmd

F32 = mybir.dt.float32
BF16 = mybir.dt.bfloat16
ALU = mybir.AluOpType
AF = mybir.ActivationFunctionType
AX = mybir.AxisListType

COMPUTE = ("pe", "act", "dve", "pool")

T = 12288
SEG = 4096
NSEG = 3
D = 1024
INC = 12304
NCH = 96
EPS = 1e-6
DILS = (1, 4, 16)
NEG = -30000.0
SBUF_WORDS = 207 * 256

C_MQK, C_MV, C_MO, C_MZ, C_MI, C_MF, C_AQKV, C_AZ, C_GATE = 0, 2048, 3072, 4096, 5120, 5128, 5136, 9744, 10256


class Prog:
    def __init__(self, nc):
        self.nc = nc
        self.ops = []
        self.barriers = []
        self.bankmap = {}

    def op(self, eng, fn, reads=(), writes=(), dma=None, multi=()):
        self.ops.append(dict(eng=eng, fn=fn, reads=tuple(reads), writes=tuple(writes), dma=dma,
                             multi=tuple(multi), deps=set(), signal=False))
        return len(self.ops) - 1

    def barrier(self):
        self.barriers.append(len(self.ops))

    def _analyze(self):
        last_w = {}
        readers = {}
        last_bank = {}
        bar_deps = set()
        bi = 0
        last_by_stream = {}
        for i, o in enumerate(self.ops):
            while bi < len(self.barriers) and self.barriers[bi] <= i:
                bar_deps = set(last_by_stream.values())
                bi += 1
            deps = set(bar_deps)
            for t in o["reads"]:
                deps.update(last_w.get(t, ()))
            for t in o["writes"]:
                deps.update(last_w.get(t, ()))
                deps.update(readers.get(t, ()))
            for t in o["multi"]:
                deps.update(readers.get(t, ()))
            obanks = set()
            for t in o["reads"] + o["writes"] + o["multi"]:
                obanks.update(self.bankmap.get(t, ()))
            for b in obanks:
                lb = last_bank.get(b)
                if lb is not None and self.ops[lb]["eng"] != o["eng"]:
                    deps.add(lb)
                last_bank[b] = i
            deps.discard(i)
            keep = set()
            for d in deps:
                od = self.ops[d]
                if od["dma"] is None and o["dma"] is None and od["eng"] == o["eng"] and o["eng"] == "pe":
                    continue
                keep.add(d)
            o["deps"] = keep
            for d in keep:
                self.ops[d]["signal"] = True
            for t in o["reads"]:
                readers.setdefault(t, []).append(i)
            for t in o["writes"]:
                last_w[t] = [i]
                readers[t] = []
            for t in o["multi"]:
                last_w.setdefault(t, []).append(i)
            key = ("d", o["dma"]) if o["dma"] is not None else ("e", o["eng"])
            last_by_stream[key] = i

    def emit(self):
        nc = self.nc
        self._analyze()
        dma_keys = []
        for o in self.ops:
            if o["dma"] is not None and o["dma"] not in dma_keys:
                dma_keys.append(o["dma"])
        engs_used = []
        for o in self.ops:
            if o["eng"] not in engs_used:
                engs_used.append(o["eng"])
        with contextlib.ExitStack() as st:
            esem = {e: st.enter_context(nc.semaphore("s_" + e)) for e in COMPUTE}
            first, last = {}, {}
            for i, o in enumerate(self.ops):
                if o["dma"] is not None:
                    first.setdefault(o["dma"], i)
                    last[o["dma"]] = i
            phys_last = []
            kmap = {}
            for k in sorted(dma_keys, key=lambda k: first[k]):
                chosen = None
                for pi, pl in enumerate(phys_last):
                    if any(pl < b <= first[k] for b in self.barriers):
                        chosen = pi
                        break
                if chosen is None:
                    phys_last.append(last[k])
                    chosen = len(phys_last) - 1
                else:
                    phys_last[chosen] = last[k]
                kmap[k] = "p%d" % chosen
            for o in self.ops:
                if o["dma"] is not None:
                    o["lkey"] = o["dma"]
                    o["dma"] = kmap[o["dma"]]

            def same_batch(a, b):
                return (self.ops[a]["lkey"] == self.ops[b]["lkey"]
                        and not any(a < x <= b for x in self.barriers))
            dma_keys = ["p%d" % i for i in range(len(phys_last))]
            dsem = {k: st.enter_context(nc.semaphore("d_" + k)) for k in dma_keys}
            cnt = {e: 0 for e in COMPUTE}
            dcnt = {k: 0 for k in dma_keys}
            for o in self.ops:
                if o["dma"] is not None:
                    dcnt[o["dma"]] += 16
                    o["sem"], o["val"], o["semname"] = dsem[o["dma"]], dcnt[o["dma"]], "d_" + o["dma"]
                elif o["signal"]:
                    cnt[o["eng"]] += 1
                    o["sem"], o["val"], o["semname"] = esem[o["eng"]], cnt[o["eng"]], "s_" + o["eng"]
            prev_batch_last = {}
            by_eng = {}
            for i, o in enumerate(self.ops):
                by_eng.setdefault(o["eng"], []).append(i)
            for ename, idxs in by_eng.items():
                batch = []
                def close(batch):
                    if not batch:
                        return
                    lastv = self.ops[batch[-1]]["val"]
                    for bi_ in batch:
                        self.ops[bi_]["val"] = lastv
                for i in idxs:
                    o = self.ops[i]
                    if o["dma"] is None:
                        close(batch)
                        batch = []
                        continue
                    if batch and same_batch(batch[-1], i):
                        batch.append(i)
                    else:
                        close(batch)
                        batch = [i]
                close(batch)
            lastop = {}
            prev_in_eng = {}
            for i, o in enumerate(self.ops):
                if o["dma"] is None:
                    prev_in_eng[o["eng"]] = None
                    continue
                k = o["dma"]
                pi_ = prev_in_eng.get(o["eng"])
                sb_ = pi_ is not None and same_batch(pi_, i)
                if not sb_ and k in lastop:
                    o["deps"].add(lastop[k])
                lastop[k] = i
                prev_in_eng[o["eng"]] = i
            final = dict(dcnt)
            self._simulate()
            block = st.enter_context(nc.Block())
            hooks = {"pe": block.tensor, "act": block.scalar, "dve": block.vector,
                     "pool": block.gpsimd, "sp": block.sync}

            def make(ename):
                def body(e):
                    seen = {}
                    for o in self.ops:
                        if o["eng"] != ename:
                            continue
                        need = {}
                        for d in o["deps"]:
                            od = self.ops[d]
                            key = od["semname"]
                            if od["val"] > need.get(key, (None, 0))[1]:
                                need[key] = (od["sem"], od["val"])
                        for key, (sem, val) in need.items():
                            if seen.get(key, 0) >= val:
                                continue
                            e.wait_ge(sem, val)
                            seen[key] = val
                        ins = o["fn"](e)
                        if o["dma"] is not None:
                            ins.then_inc(o["sem"], 16)
                        elif o["signal"]:
                            ins.then_inc(o["sem"], 1)
                    if ename == "sp":
                        for k in dma_keys:
                            if seen.get("d_" + k, 0) < final[k]:
                                e.wait_ge(dsem[k], final[k])
                return body

            names = list(engs_used)
            if "sp" not in names:
                names.append("sp")
            for ename in names:
                hooks[ename](make(ename))


def _prog_simulate(self):
    streams = {}
    for i, o in enumerate(self.ops):
        streams.setdefault(o["eng"], []).append(i)
    pos = {e: 0 for e in streams}
    sem = {}
    progress = True
    ndone = 0
    while progress:
        progress = False
        for e, idxs in streams.items():
            while pos[e] < len(idxs):
                o = self.ops[idxs[pos[e]]]
                ok = True
                for d in o["deps"]:
                    od = self.ops[d]
                    if sem.get(od["semname"], 0) < od["val"]:
                        ok = False
                        break
                if not ok:
                    break
                if o["dma"] is not None:
                    sem[o["semname"]] = sem.get(o["semname"], 0) + 16
                elif o["signal"]:
                    sem[o["semname"]] = sem.get(o["semname"], 0) + 1
                pos[e] += 1
                ndone += 1
                progress = True
    stuck = {e: idxs[pos[e]] for e, idxs in streams.items() if pos[e] < len(idxs)}
    if stuck:
        msg = []
        for e, i in stuck.items():
            o = self.ops[i]
            need = [(self.ops[d]["semname"], self.ops[d]["val"], sem.get(self.ops[d]["semname"], 0), d) for d in o["deps"] if sem.get(self.ops[d]["semname"], 0) < self.ops[d]["val"]]
            msg.append((e, i, o.get("lkey"), need[:4]))
        raise RuntimeError("semaphore program deadlocks: %r" % (msg,))


Prog._simulate = _prog_simulate


class Mem:
    def __init__(self, big):
        self.big = big
        self.off = 0

    def mark(self):
        return self.off

    def reset(self, m):
        self.off = m

    def _shape(self, v, shape):
        if len(shape) == 2:
            return v
        if len(shape) == 3:
            return v.rearrange("p (a b) -> p a b", a=shape[1])
        if len(shape) == 4:
            return v.rearrange("p (a b c) -> p a b c", a=shape[1], b=shape[2])
        raise ValueError

    def f32(self, shape):
        n = int(np.prod(shape[1:]))
        a = self.off
        self.off += n + (n & 1)
        assert self.off <= SBUF_WORDS, ("SBUF overflow", self.off)
        return self._shape(self.big[0:shape[0], a:a + n], shape)

    def bf16(self, shape):
        n = int(np.prod(shape[1:]))
        nw = (n + 1) // 2
        nw += nw & 1
        a = self.off
        self.off += nw
        assert self.off <= SBUF_WORDS, ("SBUF overflow", self.off)
        v = self.big[0:shape[0], a:a + nw].bitcast(BF16)[:, 0:n]
        return self._shape(v, shape)


class Ring:
    def __init__(self, name, tiles):
        self.name, self.tiles, self.i = name, tiles, 0

    def next(self):
        k = self.i % len(self.tiles)
        self.i += 1
        return self.tiles[k], (self.name, k), "%s%d" % (self.name, k)


def t5_bucket(rel):
    nb = 16
    exact = 8
    n = np.abs(rel)
    large = exact + (np.log(np.maximum(n, 1) / exact) / math.log(1024 / exact) * (nb - exact)).astype(np.int32)
    large = np.minimum(large, nb - 1)
    return (rel > 0).astype(np.int32) * nb + np.where(n < exact, n, large)


def bias_onehots():
    oh = np.zeros((3, 2, 33, 256), np.float32)
    for g, d in enumerate(DILS):
        for j in range(255):
            if j <= 127:
                m = 63 - j
                oh[g, 0, t5_bucket(np.array(m * d)), j] = 1.0
            else:
                oh[g, 0, 32, j] = NEG
            if j >= 127:
                m = 191 - j
                oh[g, 1, t5_bucket(np.array(m * d)), j] = 1.0
            else:
                oh[g, 1, 32, j] = NEG
        oh[g, :, 32, 255] = NEG
    return oh


def build_program(debug=False, stop_after=99):
    nc = bass.Bass("TRN2", target_bir_lowering=False)
    din = lambda n, s, dt=F32: nc.dram_tensor(n, list(s), dt, kind="ExternalInput")
    x_h = din("x", [T, D])
    x = x_h.ap()
    jflag = din("jflag", [128, 1]).ap()
    w_in = din("w_in", [D, INC]).ap()
    prew = din("prew", [128, 8]).ap()
    convw = din("convw", [128, 16, 3]).ap()
    convb = din("convb", [128, 16]).ap()
    gbias_h = din("gbias", [1, 16])
    hnw_h = din("hnw", [1, 1024])
    w_pm = din("w_pm", [1024, 1024]).ap()
    w_pa = din("w_pa", [512, 1024]).ap()
    w_out = din("w_out", [1024, 1024]).ap()
    postw_h = din("postw", [1, 1024])
    relx = din("relx", [33, 24]).ap()
    ident_in = din("ident", [128, 128]).ap()
    tri_in = din("tri", [128, 128]).ap()
    oh_in = din("oh", [3, 2, 33, 256]).ap()
    y = nc.dram_tensor("y", [T, D], F32, kind="ExternalOutput").ap()

    skind = "ExternalOutput" if debug else "Internal"
    scr = lambda n, s, dt=BF16: nc.dram_tensor(n, list(s), dt, kind=skind)
    wb = scr("wb", [D, INC]).ap()
    qkT = scr("qkT", [2048, T]).ap()
    mv = scr("mv", [T, 1024]).ap()
    goz = scr("goz", [T, 1024]).ap()
    aqk = scr("aqk", [6, 512, T]).ap()
    av_ntile = []
    for g, d in enumerate(DILS):
        av_ntile.append(d * (8192 // d // 128 + 1) + d * (4096 // d // 128 + 1))
    av = [scr("av%d" % g, [av_ntile[g], 128, 512]).ap() for g in range(3)]
    azT = scr("azT", [512, T]).ap()
    gT = scr("gT", [2048, T]).ap()
    yAT = scr("yAT", [1024, T]).ap()
    yBT = scr("yBT", [512, T]).ap()
    gs_h = scr("gs", [48, 128, 256], F32)
    gates_dbg = scr("gates_dbg", [128, 96 * 16], F32).ap() if debug else None

    def av_tile_index(g, seg, r, jloc):
        d = DILS[g]
        L = SEG // d
        if seg < 2:
            return r * (2 * L // 128 + 1) + seg * (L // 128) + jloc
        return d * (2 * L // 128 + 1) + r * (L // 128 + 1) + jloc

    with contextlib.ExitStack() as st:
        big = st.enter_context(nc.sbuf_tensor("big", [128, SBUF_WORDS], F32))
        psum = st.enter_context(nc.psum_tensor("psum", [128, 8 * 512], F32))
        M = Mem(big)
        P = Prog(nc)
        BM = P.bankmap
        BM["pb0"] = (0,)
        for k_ in range(4):
            BM[("pacc", k_)] = (k_,)
            BM["g%d" % k_] = (k_,)
        for k_ in range(2):
            BM[("ptr", k_)] = (4 + k_,)
            BM[("bA", k_)] = (k_,)
            BM[("bY", k_)] = (k_,)
            BM[("bN", k_)] = (2 + k_,)
            BM[("bKV", k_)] = (4 + 2 * k_, 5 + 2 * k_)
            BM[("Sps0", k_)] = (k_,)
            BM[("Sps1", k_)] = (6 + k_,)
            BM[("accps0", k_)] = (2 + k_,)
            BM[("accps1", k_)] = (4 + k_,)
            BM[("pm", k_)] = (k_,)
            BM[("pa", k_)] = (2 + k_,)
            BM[("ops", k_)] = (4 + 2 * k_, 5 + 2 * k_)
        BM[("pacc2", 0)] = (4, 5)
        BM[("pacc2", 1)] = (6, 7)
        bank = lambda b: psum[:, b * 512:(b + 1) * 512]
        bank_bf = lambda b: psum[:, b * 512:(b + 1) * 512].bitcast(BF16)

        ident_f = M.f32([128, 128])
        ident_b = M.bf16([128, 128])
        tri_f = M.f32([128, 128])
        triT_f = M.f32([128, 128])
        mask_f = M.bf16([128, 128])
        mask_b = M.bf16([128, 128])
        ones_f = M.f32([128, 128])
        jcol = M.f32([128, 1])
        gates = M.f32([128, NCH, 16])
        wk = M.f32([128, NCH, 8])
        flo = M.f32([128, NCH, 8])
        dec = M.f32([128, NCH, 8])
        decj = M.f32([128, 8])
        hhalo = M.bf16([128, 8, 2])
        P.op("sp", lambda e: e.dma_start(out=ident_f, in_=ident_in), writes=["ident_f"], dma="c0")
        P.op("sp", lambda e: e.dma_start(out=tri_f, in_=tri_in), writes=["tri_f"], dma="c1")
        P.op("sp", lambda e: e.dma_start(out=jcol, in_=jflag), writes=["jcol"], dma="c2")
        P.op("dve", lambda e: e.tensor_copy(out=ident_b, in_=ident_f), reads=["ident_f"], writes=["ident_b"])
        P.op("dve", lambda e: e.tensor_copy(out=mask_f, in_=tri_f), reads=["tri_f"], writes=["mask_f"])
        P.op("dve", lambda e: e.memset(ones_f, 1.0), writes=["ones_f"])
        P.op("pe", lambda e: e.transpose(bank(0)[:, 0:128], tri_f, ident_f), reads=["tri_f", "ident_f"], writes=["pb0"])
        P.op("dve", lambda e: e.tensor_copy(out=triT_f, in_=bank(0)[:, 0:128]), reads=["pb0"], writes=["triT_f"])
        P.op("dve", lambda e: e.tensor_copy(out=mask_b, in_=triT_f), reads=["triT_f"], writes=["mask_b"])
        const_mark = M.mark()

        prew_sb = M.f32([128, 8])
        P.op("sp", lambda e: e.dma_start(out=prew_sb, in_=prew), writes=["prew"], dma="c0")
        wst = Ring("wst", [M.f32([128, 2048]) for _ in range(2)])
        wbs = Ring("wbs", [M.bf16([128, 2048]) for _ in range(2)])
        n = 0
        for k in range(8):
            for c0 in range(0, INC, 2048):
                w = min(2048, INC - c0)
                s_t, s_tok, s_key = wst.next()
                b_t, b_tok, b_key = wbs.next()
                P.op("sp", lambda e, s_t=s_t, k=k, c0=c0, w=w: e.dma_start(out=s_t[:, 0:w], in_=w_in[k * 128:(k + 1) * 128, c0:c0 + w]),
                     writes=[s_tok], dma=s_key)
                if n % 2 == 0:
                    P.op("dve", lambda e, s_t=s_t, b_t=b_t, k=k, w=w: e.tensor_scalar(out=b_t[:, 0:w], in0=s_t[:, 0:w], scalar1=prew_sb[:, k:k + 1], scalar2=None, op0=ALU.mult),
                         reads=[s_tok, "prew"], writes=[b_tok])
                else:
                    P.op("act", lambda e, s_t=s_t, b_t=b_t, k=k, w=w: e.activation(out=b_t[:, 0:w], in_=s_t[:, 0:w], func=AF.Copy, scale=prew_sb[:, k:k + 1]),
                         reads=[s_tok, "prew"], writes=[b_tok])
                P.op("sp", lambda e, b_t=b_t, k=k, c0=c0, w=w: e.dma_start(out=wb[k * 128:(k + 1) * 128, c0:c0 + w], in_=b_t[:, 0:w]),
                     reads=[b_tok], multi=["wb"], dma=b_key)
                n += 1
        P.barrier()
        M.reset(const_mark)
        if stop_after <= 0:
            P.emit()
            return nc

        cw_sb = M.f32([128, 16, 3])
        cb_sb = M.f32([128, 16])
        hnw_bc = M.f32([128, 1024])
        P.op("sp", lambda e: e.dma_start(out=cw_sb, in_=convw), writes=["cw"], dma="c0")
        P.op("sp", lambda e: e.dma_start(out=cb_sb, in_=convb), writes=["cb"], dma="c1")
        P.op("sp", lambda e: e.dma_start(out=hnw_bc, in_=hnw_h.ap().partition_broadcast(128)), writes=["hnw"], dma="c2")
        hT = M.bf16([128, 8, SEG + 2])
        xring = Ring("xt", [M.f32([128, 1024]) for _ in range(3)])
        xsring = Ring("xs", [M.bf16([128, 1024]) for _ in range(2)])
        ssr = Ring("ss", [M.f32([128, 2]) for _ in range(2)])
        wblk = Ring("wblk", [M.bf16([128, 8, 512]) for _ in range(3)])
        pre_r = Ring("pre", [M.f32([128, SEG + 2]) for _ in range(2)])
        cv1 = M.f32([128, SEG])
        fmst = Ring("fmst", [M.bf16([128, SEG]) for _ in range(2)])
        tmst = Ring("tmst", [M.bf16([128, 512]) for _ in range(3)])
        sg1 = M.f32([128, 1024])
        sqj = sg1
        sg2 = M.f32([128, 512])
        sg3 = M.f32([128, 512])
        pacc = Ring("pacc", [bank(b) for b in (0, 1, 2, 3)])
        ptr = Ring("ptr", [bank_bf(b) for b in (4, 5)])
        pacc2 = Ring("pacc2", [psum[:, 4 * 512:6 * 512], psum[:, 6 * 512:8 * 512]])

        def rmsnorm_to_hT(rows, nrow, dst_fn, tag):
            xt, xtok, xkey = xring.next()
            xs, xstok, _ = xsring.next()
            ss, sstok, _ = ssr.next()
            pt, pttok, _ = ptr.next()
            P.op("sp", lambda e: e.dma_start(out=xt[0:nrow, :], in_=rows), writes=[xtok], dma=xkey)
            P.op("act", lambda e: e.activation(out=sqj[0:nrow, :], in_=xt[0:nrow, :], func=AF.Square, accum_out=ss[0:nrow, 0:1]),
                 reads=[xtok], writes=[sstok, "sg1"])
            P.op("act", lambda e: e.activation(out=ss[0:nrow, 1:2], in_=ss[0:nrow, 0:1], func=AF.Ln, scale=1.0 / D, bias=EPS),
                 reads=[sstok], writes=[sstok])
            P.op("act", lambda e: e.activation(out=ss[0:nrow, 0:1], in_=ss[0:nrow, 1:2], func=AF.Exp, scale=-0.5),
                 reads=[sstok], writes=[sstok])
            P.op("dve", lambda e: e.tensor_scalar(out=xs[0:nrow, :], in0=xt[0:nrow, :], scalar1=ss[0:nrow, 0:1], scalar2=None, op0=ALU.mult),
                 reads=[xtok, sstok], writes=[xstok])
            for k in range(8):
                P.op("pe", lambda e, k=k: e.transpose(pt[:, k * 128:k * 128 + nrow], xs[0:nrow, k * 128:(k + 1) * 128], ident_b[0:nrow, 0:nrow]),
                     reads=[xstok, "ident_b"], writes=[pttok])
            return pt, pttok

        pt, pttok = rmsnorm_to_hT(x[SEG - 1:SEG + 1, :], 2, None, "halo")
        P.op("dve", lambda e: e.tensor_scalar(out=hhalo, in0=pt.rearrange("p (k t) -> p k t", k=8)[:, :, 0:2], scalar1=jcol[:, 0:1], scalar2=None, op0=ALU.mult),
             reads=[pttok, "jcol"], writes=["hhalo"])

        evac_rr = [0]

        def evac_engine():
            evac_rr[0] += 1
            return "act" if evac_rr[0] % 2 == 0 else "dve"

        wsched = []
        for c0_ in range(C_MQK, C_MV, 512):
            wsched.append((c0_, 512))
        for g_ in range(3):
            wsched.append((C_AQKV + g_ * 512, 512))
            wsched.append((C_AQKV + (3 + g_) * 512, 512))
        wsched.append((C_AZ, 512))
        for c0_ in range(C_GATE, INC, 512):
            wsched.append((c0_, 512))
        for j_ in range(2):
            wsched.append((C_MV + j_ * 512, 512))
        for j_ in range(2):
            wsched.append((C_MO + j_ * 512, 512))
            wsched.append((C_MZ + j_ * 512, 512))
        wsched.append((C_MI, 16))
        for g_ in range(3):
            wsched.append((C_AQKV + (6 + g_) * 512, 512))
        wsched = wsched * NSEG
        wstate = {"next": 0, "issued": []}

        def _issue_next():
            if wstate["next"] >= len(wsched):
                return
            c0, w = wsched[wstate["next"]]
            wstate["next"] += 1
            wt, wtok, wkey = wblk.next()
            P.op("sp", lambda e: e.dma_start(out=wt[:, :, 0:w], in_=wb.rearrange("(k p) c -> p k c", p=128)[:, :, c0:c0 + w]),
                 reads=["wb"], writes=[wtok], dma=wkey)
            wstate["issued"].append((c0, w, wt, wtok))

        def load_wblk(c0, w):
            if not wstate["issued"]:
                _issue_next()
            c0i, wi, wt, wtok = wstate["issued"].pop(0)
            assert (c0i, wi) == (c0, w), ((c0i, wi), (c0, w))
            if not wstate["issued"]:
                _issue_next()
            return wt, wtok

        for seg in range(NSEG):
            t0 = seg * SEG
            for i in range(SEG // 128):
                pt, pttok = rmsnorm_to_hT(x[t0 + i * 128:t0 + (i + 1) * 128, :], 128, None, "h")
                eng = evac_engine()
                dst = hT[:, :, 1 + i * 128:1 + (i + 1) * 128]
                src = pt.rearrange("p (k t) -> p k t", k=8)
                if eng == "act":
                    P.op("act", lambda e, dst=dst, src=src: e.activation(out=dst, in_=src, func=AF.Copy), reads=[pttok], multi=["hT"])
                else:
                    P.op("dve", lambda e, dst=dst, src=src: e.tensor_copy(out=dst, in_=src), reads=[pttok], multi=["hT"])
            if seg == 0:
                P.op("pool", lambda e: e.memset(hT[:, :, 0:1], 0.0), multi=["hT"])
                P.op("pool", lambda e: e.tensor_copy(out=hT[:, :, SEG + 1:SEG + 2], in_=hhalo[:, :, 1:2]), reads=["hhalo"], multi=["hT"])
            elif seg == 1:
                P.op("pool", lambda e: e.tensor_copy(out=hT[:, :, 0:1], in_=hhalo[:, :, 0:1]), reads=["hhalo"], multi=["hT"])
                P.op("pool", lambda e: e.memset(hT[:, :, SEG + 1:SEG + 2], 0.0), multi=["hT"])
            else:
                P.op("pool", lambda e: e.memset(hT[:, :, 0:1], 0.0), multi=["hT"])
                P.op("pool", lambda e: e.memset(hT[:, :, SEG + 1:SEG + 2], 0.0), multi=["hT"])

            def fm_block(c0, nchunk, kind, arg=None):
                wt, wtok = load_wblk(c0, nchunk * 128)
                for cc in range(nchunk):
                    col = c0 + cc * 128
                    if kind == "mqk":
                        chunk = (col - C_MQK) // 128
                        pre, pretok, _ = pre_r.next()
                        for tt in range(8):
                            pa, patok, _ = pacc.next()
                            for k in range(8):
                                P.op("pe", lambda e, pa=pa, k=k, cc=cc, tt=tt: e.matmul(pa, lhsT=wt[:, k, cc * 128:(cc + 1) * 128], rhs=hT[:, k, 1 + tt * 512:1 + (tt + 1) * 512], start=(k == 0), stop=(k == 7)),
                                     reads=[wtok, "hT"], writes=[patok])
                            eng = "act"
                            dst = pre[:, 1 + tt * 512:1 + (tt + 1) * 512]
                            if eng == "act":
                                P.op("act", lambda e, pa=pa, dst=dst: e.activation(out=dst, in_=pa, func=AF.Copy), reads=[patok], multi=[pretok])
                            else:
                                P.op("dve", lambda e, pa=pa, dst=dst: e.tensor_copy(out=dst, in_=pa), reads=[patok], multi=[pretok])
                        pa, patok, _ = pacc.next()
                        for k in range(8):
                            P.op("pe", lambda e, pa=pa, k=k, cc=cc: e.matmul(pa[:, 0:2], lhsT=wt[:, k, cc * 128:(cc + 1) * 128], rhs=hT[:, k, 0:SEG + 2:SEG + 1], start=(k == 0), stop=(k == 7)),
                                 reads=[wtok, "hT"], writes=[patok])
                        P.op("dve", lambda e, pa=pa, pre=pre: e.tensor_copy(out=pre[:, 0:SEG + 2:SEG + 1], in_=pa[:, 0:2]), reads=[patok], multi=[pretok])
                        P.op("dve", lambda e, ch=chunk, pre=pre: e.tensor_scalar(out=cv1, in0=pre[:, 1:SEG + 1], scalar1=cw_sb[:, ch, 1:2], scalar2=cb_sb[:, ch:ch + 1], op0=ALU.mult, op1=ALU.add),
                             reads=[pretok, "cw", "cb"], writes=["cv1"])
                        P.op("dve", lambda e, ch=chunk, pre=pre: e.scalar_tensor_tensor(out=cv1, in0=pre[:, 0:SEG], scalar=cw_sb[:, ch, 0:1], in1=cv1, op0=ALU.mult, op1=ALU.add),
                             reads=[pretok, "cw", "cv1"], writes=["cv1"])
                        P.op("dve", lambda e, ch=chunk, pre=pre: e.scalar_tensor_tensor(out=cv1, in0=pre[:, 2:SEG + 2], scalar=cw_sb[:, ch, 2:3], in1=cv1, op0=ALU.mult, op1=ALU.add),
                             reads=[pretok, "cw", "cv1"], writes=["cv1"])
                        fs, fstok, fskey = fmst.next()
                        P.op("act", lambda e, fs=fs: e.activation(out=fs, in_=cv1, func=AF.Silu), reads=["cv1"], writes=[fstok])
                        P.op("sp", lambda e, fs=fs, col=col, t0=t0: e.dma_start(out=qkT[col:col + 128, t0:t0 + SEG], in_=fs),
                             reads=[fstok], multi=["qkT"], dma=fskey)
                    else:
                        fs, fstok, fskey = fmst.next()
                        for tt in range(8):
                            pa, patok, _ = pacc.next()
                            for k in range(8):
                                P.op("pe", lambda e, pa=pa, k=k, cc=cc, tt=tt: e.matmul(pa, lhsT=wt[:, k, cc * 128:(cc + 1) * 128], rhs=hT[:, k, 1 + tt * 512:1 + (tt + 1) * 512], start=(k == 0), stop=(k == 7)),
                                     reads=[wtok, "hT"], writes=[patok])
                            if kind == "aqk":
                                g, scale = arg
                                d = DILS[g]
                                if d == 1:
                                    dst = fs[:, tt * 512:(tt + 1) * 512]
                                    src = pa
                                else:
                                    na = 512 // d
                                    dst = fs.rearrange("p (r a) -> p a r", r=d)[:, tt * na:(tt + 1) * na, :]
                                    src = pa.rearrange("p (a r) -> p a r", r=d)
                                eng = evac_engine()
                                if eng == "act":
                                    P.op("act", lambda e, dst=dst, src=src, scale=scale: e.activation(out=dst, in_=src, func=AF.Copy, scale=scale), reads=[patok], multi=[fstok])
                                else:
                                    P.op("dve", lambda e, dst=dst, src=src, scale=scale: e.tensor_scalar(out=dst, in0=src, scalar1=scale, scalar2=None, op0=ALU.mult), reads=[patok], multi=[fstok])
                            else:
                                func = AF.Silu if kind == "az" else AF.Sigmoid
                                dst = fs[:, tt * 512:(tt + 1) * 512]
                                P.op("act", lambda e, dst=dst, pa=pa, func=func: e.activation(out=dst, in_=pa, func=func), reads=[patok], multi=[fstok])
                        if kind == "aqk":
                            idx = (col - C_AQKV) // 512
                            row = (col - C_AQKV) % 512
                            dstd = aqk[idx, row:row + 128, t0:t0 + SEG]
                            mt = "aqk"
                        elif kind == "az":
                            dstd = azT[col - C_AZ:col - C_AZ + 128, t0:t0 + SEG]
                            mt = "azT"
                        else:
                            dstd = gT[col - C_GATE:col - C_GATE + 128, t0:t0 + SEG]
                            mt = "gT"
                        P.op("sp", lambda e, fs=fs, dstd=dstd: e.dma_start(out=dstd, in_=fs), reads=[fstok], multi=[mt], dma=fskey)

            for c0 in range(C_MQK, C_MV, 512):
                fm_block(c0, 4, "mqk")
            for g in range(3):
                fm_block(C_AQKV + g * 512, 4, "aqk", (g, 0.125))
                fm_block(C_AQKV + (3 + g) * 512, 4, "aqk", (g, 1.0))
            fm_block(C_AZ, 4, "az")
            for c0 in range(C_GATE, INC, 512):
                fm_block(c0, 4, "gate")

            for j in range(2):
                wt, wtok = load_wblk(C_MV + j * 512, 512)
                for i in range(32):
                    pa, patok, _ = pacc.next()
                    for k in range(8):
                        P.op("pe", lambda e, pa=pa, k=k, i=i, wt=wt: e.matmul(pa, lhsT=hT[:, k, 1 + i * 128:1 + (i + 1) * 128], rhs=wt[:, k, :], start=(k == 0), stop=(k == 7)),
                             reads=[wtok, "hT"], writes=[patok])
                    ts_, tstok, tskey = tmst.next()
                    eng = evac_engine()
                    if eng == "act":
                        P.op("act", lambda e, ts_=ts_, pa=pa: e.activation(out=ts_, in_=pa, func=AF.Copy), reads=[patok], writes=[tstok])
                    else:
                        P.op("dve", lambda e, ts_=ts_, pa=pa: e.tensor_copy(out=ts_, in_=pa), reads=[patok], writes=[tstok])
                    P.op("sp", lambda e, ts_=ts_, i=i, j=j, t0=t0: e.dma_start(out=mv[t0 + i * 128:t0 + (i + 1) * 128, j * 512:(j + 1) * 512], in_=ts_),
                         reads=[tstok], multi=["mv"], dma=tskey)
            for j in range(2):
                wo, wotok = load_wblk(C_MO + j * 512, 512)
                wz, wztok = load_wblk(C_MZ + j * 512, 512)
                for i in range(32):
                    pa, patok, _ = pacc2.next()
                    for half, (wt_, wtk_) in enumerate(((wo, wotok), (wz, wztok))):
                        for k in range(8):
                            P.op("pe", lambda e, pa=pa, k=k, i=i, wt_=wt_, half=half: e.matmul(pa[:, half * 512:(half + 1) * 512], lhsT=hT[:, k, 1 + i * 128:1 + (i + 1) * 128], rhs=wt_[:, k, :], start=(k == 0), stop=(k == 7)),
                                 reads=[wtk_, "hT"], writes=[patok])
                    P.op("act", lambda e, pa=pa: e.activation(out=sg1, in_=pa, func=AF.Sigmoid), reads=[patok], writes=["sg1"])
                    P.op("dve", lambda e: e.tensor_tensor(out=sg2, in0=sg1[:, 0:512], in1=sg1[:, 512:1024], op=ALU.mult), reads=["sg1"], writes=["sg2"])
                    P.op("dve", lambda e, j=j: e.tensor_tensor(out=sg3, in0=sg2, in1=hnw_bc[:, j * 512:(j + 1) * 512], op=ALU.mult), reads=["sg2", "hnw"], writes=["sg3"])
                    ts_, tstok, tskey = tmst.next()
                    P.op("dve", lambda e, pa=pa, ts_=ts_: e.tensor_tensor(out=ts_, in0=pa[:, 512:1024], in1=sg3, op=ALU.mult), reads=[patok, "sg3"], writes=[tstok])
                    P.op("sp", lambda e, ts_=ts_, i=i, j=j, t0=t0: e.dma_start(out=goz[t0 + i * 128:t0 + (i + 1) * 128, j * 512:(j + 1) * 512], in_=ts_),
                         reads=[tstok], multi=["goz"], dma=tskey)
            wt, wtok = load_wblk(C_MI, 16)
            for i in range(32):
                pa, patok, _ = pacc.next()
                for k in range(8):
                    P.op("pe", lambda e, pa=pa, k=k, i=i, wt=wt: e.matmul(pa[:, 0:16], lhsT=hT[:, k, 1 + i * 128:1 + (i + 1) * 128], rhs=wt[:, k, 0:16], start=(k == 0), stop=(k == 7)),
                         reads=[wtok, "hT"], writes=[patok])
                P.op("dve", lambda e, pa=pa, i=i, seg=seg: e.tensor_copy(out=gates[:, seg * 32 + i, :], in_=pa[:, 0:16]), reads=[patok], multi=["gates"])
            for g, d in enumerate(DILS):
                wt, wtok = load_wblk(C_AQKV + (6 + g) * 512, 512)
                L = SEG // d
                nj = L // 128
                for r in range(d):
                    for jl in range(nj + 1):
                        if jl == 0:
                            p0, p1, a0 = 64, 128, 0
                        elif jl == nj:
                            p0, p1, a0 = 0, 64, L - 64
                        else:
                            p0, p1, a0 = 0, 128, 64 + 128 * (jl - 1)
                        cnt = p1 - p0
                        c_start = 1 + a0 * d + r
                        pa, patok, _ = pacc.next()
                        for k in range(8):
                            P.op("pe", lambda e, pa=pa, k=k, wt=wt, p0=p0, p1=p1, c_start=c_start, cnt=cnt, d=d: e.matmul(pa[p0:p1, :], lhsT=hT[:, k, c_start:c_start + (cnt - 1) * d + 1:d], rhs=wt[:, k, :], start=(k == 0), stop=(k == 7)),
                                 reads=[wtok, "hT"], writes=[patok])
                        ts_, tstok, tskey = tmst.next()
                        eng = evac_engine()
                        if eng == "act":
                            P.op("act", lambda e, ts_=ts_, pa=pa, p0=p0, p1=p1: e.activation(out=ts_[p0:p1, :], in_=pa[p0:p1, :], func=AF.Copy), reads=[patok], writes=[tstok])
                        else:
                            P.op("dve", lambda e, ts_=ts_, pa=pa, p0=p0, p1=p1: e.tensor_copy(out=ts_[p0:p1, :], in_=pa[p0:p1, :]), reads=[patok], writes=[tstok])
                        ti = av_tile_index(g, seg, r, jl)
                        P.op("sp", lambda e, ts_=ts_, g=g, ti=ti, p0=p0, p1=p1: e.dma_start(out=av[g][ti, p0:p1, :], in_=ts_[p0:p1, :]),
                             reads=[tstok], multi=["av%d" % g], dma=tskey)
        P.barrier()
        if debug:
            P.op("sp", lambda e: e.dma_start(out=gates_dbg, in_=gates.rearrange("p c g -> p (c g)")), reads=["gates"], dma="c0")
        M.reset(const_mark)
        if stop_after <= 1:
            P.emit()
            return nc
        LN16 = math.log(16.0)
        cneg = M.f32([128, 1])
        cone = M.f32([128, 1])
        ceps = M.f32([128, 1])
        P.op("pool", lambda e: e.memset(cneg, -LN16), writes=["cneg"])
        P.op("pool", lambda e: e.memset(cone, 1.0), writes=["cone"])
        P.op("pool", lambda e: e.memset(ceps, EPS), writes=["ceps"])
        ph2_mark = M.mark()
        gb = M.f32([128, NCH, 16])
        gsum = M.f32([128, NCH, 16])
        nl = M.f32([128, NCH, 8])
        uu = M.f32([128, NCH, 8])
        P.op("sp", lambda e: e.dma_start(out=gb, in_=bass.AP(gbias_h, 0, [[0, 128], [0, NCH], [1, 16]])), writes=["gb"], dma="c0")
        P.op("dve", lambda e: e.tensor_tensor(out=gsum, in0=gates, in1=gb, op=ALU.add), reads=["gates", "gb"], writes=["gsum"])
        P.op("act", lambda e: e.activation(out=nl, in_=gsum[:, :, 8:16], func=AF.Exp, scale=-1.0), reads=["gsum"], writes=["nl"])
        P.op("act", lambda e: e.activation(out=nl, in_=nl, func=AF.Ln, bias=cone[:, 0:1]), reads=["nl", "cone"], writes=["nl"])
        b3 = lambda b: bank(b)[:, 0:384].rearrange("p (c g) -> p c g", g=4)
        P.op("pe", lambda e: e.matmul(bank(0)[:, 0:384], lhsT=tri_f, rhs=nl[:, :, 0:4], start=True, stop=True), reads=["tri_f", "nl"], writes=["g0"])
        P.op("pe", lambda e: e.matmul(bank(1)[:, 0:384], lhsT=triT_f, rhs=nl[:, :, 4:8], start=True, stop=True), reads=["triT_f", "nl"], writes=["g1"])
        P.op("pe", lambda e: e.matmul(bank(2)[:, 0:384], lhsT=ones_f, rhs=nl[:, :, 0:4], start=True, stop=True), reads=["ones_f", "nl"], writes=["g2"])
        P.op("pe", lambda e: e.matmul(bank(3)[:, 0:384], lhsT=ones_f, rhs=nl[:, :, 4:8], start=True, stop=True), reads=["ones_f", "nl"], writes=["g3"])
        for dr in range(2):
            sl = slice(dr * 4, dr * 4 + 4)
            P.op("dve", lambda e, dr=dr, sl=sl: e.tensor_tensor(out=uu[:, :, sl], in0=gsum[:, :, sl], in1=b3(dr), op=ALU.add), reads=["gsum", "g%d" % dr], writes=["uu%d" % dr])
            P.op("act", lambda e, sl=sl: e.activation(out=wk[:, :, sl], in_=uu[:, :, sl], func=AF.Exp, bias=cneg[:, 0:1]), reads=["uu%d" % dr, "cneg"], writes=["wk%d" % dr])
            P.op("act", lambda e, dr=dr, sl=sl: e.activation(out=flo[:, :, sl], in_=b3(dr), func=AF.Exp), reads=["g%d" % dr], writes=["flo%d" % dr])
            P.op("act", lambda e, dr=dr, sl=sl: e.activation(out=dec[:, :, sl], in_=b3(2 + dr), func=AF.Exp, scale=-1.0), reads=["g%d" % (2 + dr)], writes=["dec%d" % dr])
        P.op("dve", lambda e: e.tensor_scalar(out=decj[:, 0:4], in0=dec[:, 31, 0:4], scalar1=jcol[:, 0:1], scalar2=None, op0=ALU.mult), reads=["dec0", "jcol"], writes=["decj0"])
        P.op("dve", lambda e: e.tensor_scalar(out=decj[:, 4:8], in0=dec[:, 32, 4:8], scalar1=jcol[:, 0:1], scalar2=None, op0=ALU.mult), reads=["dec1", "jcol"], writes=["decj1"])
        gtoks = ["wk0", "wk1", "flo0", "flo1", "dec0", "dec1", "decj0", "decj1"]
        P.barrier()
        M.reset(ph2_mark)
        if stop_after <= 2:
            if debug:
                for nm, tl in (("wk", wk), ("flo", flo), ("dec", dec)):
                    dd = nc.dram_tensor("dbg_" + nm, [128, NCH * 8], F32, kind="ExternalOutput").ap()
                    P.op("sp", lambda e, dd=dd, tl=tl: e.dma_start(out=dd, in_=tl.rearrange("p c g -> p (c g)")), reads=gtoks, dma="c0")
            P.emit()
            return nc

        hpart = M.f32([128, 64, 256])
        NS = 4
        qk_r = [Ring("qk%d" % dr, [M.bf16([128, 4, 128]) for _ in range(NS)]) for dr in range(2)]
        vx_r = [Ring("vx%d" % dr, [M.bf16([128, 258]) for _ in range(NS)]) for dr in range(2)]
        gz_r = [Ring("gz%d" % dr, [M.bf16([128, 256]) for _ in range(NS)]) for dr in range(2)]
        kw_r = [Ring("kw%d" % dr, [M.bf16([128, 256]) for _ in range(3)]) for dr in range(2)]
        sc_r = [Ring("sc%d" % dr, [M.bf16([128, 128]) for _ in range(3)]) for dr in range(2)]
        Tst = [M.f32([128, 2, 257]) for _ in range(2)]
        Cb = [M.bf16([128, 2, 258]) for _ in range(2)]
        sm_r = [Ring("sm%d" % dr, [M.f32([128, 4]) for _ in range(3)]) for dr in range(2)]
        hh_r = Ring("hh", [M.f32([128, 256]) for _ in range(4)])
        ns_r = [Ring("ns%d" % dr, [M.f32([128, 258]) for _ in range(3)]) for dr in range(2)]
        yv_r = Ring("yv", [M.bf16([128, 256]) for _ in range(6)])
        yst_r = Ring("yst", [M.bf16([128, 2, 128]) for _ in range(4)])
        sq3 = M.bf16([128, 256])
        for dr in range(2):
            for k2 in range(NS):
                vt_, _, _ = vx_r[dr].next()
                P.op("pool", lambda e, vt_=vt_: e.memset(vt_[:, 256:257], 1.0), multi=[("vx%d" % dr, k2)])
        qkT_v = qkT.rearrange("(a p) t -> p a t", p=128)
        yAT_v = yAT.rearrange("(a p) t -> p a t", p=128)
        maskd = [mask_f, mask_b]
        def A_regions(dr, par):
            base = 0
            return (bank_bf(dr)[:, 2 * base:2 * base + 256], bank(dr)[:, base + 128:base + 256], ("bA", dr))

        for h in range(4):
            for (cbase, nchk) in ((0, 64), (64, 32)):
                for dr in range(2):
                    P.op("pool", lambda e, dr=dr: e.memset(Tst[dr], 0.0), writes=[("T", dr)])
                    P.op("pool", lambda e, dr=dr: e.memset(Cb[dr], 0.0), writes=[("Cb", dr)])
                U = {}

                def chunk_of(i, dr):
                    return cbase + i if dr == 0 else cbase + nchk - 1 - i

                def loads(i):
                    fin = i >= nchk // 2
                    for dr in range(2):
                        c = chunk_of(i, dr)
                        qk_t, qktok, qkkey = qk_r[dr].next()
                        vx_t, vxtok, vxkey = vx_r[dr].next()
                        u = dict(c=c, qk=qk_t, qktok=qktok, vx=vx_t, vxtok=vxtok, fin=fin)
                        P.op("sp", lambda e, qk_t=qk_t, c=c, h=h: e.dma_start(out=qk_t[:, 0:2, :], in_=qkT_v[:, 2 * h:2 * h + 2, c * 128:(c + 1) * 128]), reads=["qkT"], multi=[qktok], dma=qkkey)
                        P.op("sp", lambda e, qk_t=qk_t, c=c, h=h: e.dma_start(out=qk_t[:, 2:4, :], in_=qkT_v[:, 8 + 2 * h:8 + 2 * h + 2, c * 128:(c + 1) * 128]), reads=["qkT"], multi=[qktok], dma=qkkey)
                        P.op("sp", lambda e, vx_t=vx_t, c=c, h=h: e.dma_start(out=vx_t[:, 0:256], in_=mv[c * 128:(c + 1) * 128, h * 256:(h + 1) * 256]), reads=["mv"], multi=[vxtok], dma=vxkey)
                        if fin:
                            gz_t, gztok, gzkey = gz_r[dr].next()
                            u.update(gz=gz_t, gztok=gztok)
                            P.op("sp", lambda e, gz_t=gz_t, c=c, h=h: e.dma_start(out=gz_t, in_=goz[c * 128:(c + 1) * 128, h * 256:(h + 1) * 256]), reads=["goz"], writes=[gztok], dma=gzkey)
                        U[(i, dr)] = u

                def stage12(i):
                    for dr in range(2):
                        u = U[(i, dr)]
                        c, qk_t, qktok = u["c"], u["qk"], u["qktok"]
                        lane = dr * 4 + h
                        ktp, stp, tA = A_regions(dr, i % 2)
                        for hf in range(2):
                            P.op("pe", lambda e, hf=hf, qk_t=qk_t, ktp=ktp: e.transpose(ktp[:, hf * 128:(hf + 1) * 128], qk_t[:, 2 + hf, :], ident_b), reads=[qktok, "ident_b"], writes=[tA])
                        for hf in range(2):
                            P.op("pe", lambda e, hf=hf, qk_t=qk_t, stp=stp: e.matmul(stp, lhsT=qk_t[:, 2 + hf, :], rhs=qk_t[:, hf, :], start=(hf == 0), stop=(hf == 1)), reads=[qktok], writes=[tA])
                    for dr in range(2):
                        u = U[(i, dr)]
                        c = u["c"]
                        lane = dr * 4 + h
                        ktp, stp, tA = A_regions(dr, i % 2)
                        kw_t, kwtok, _ = kw_r[dr].next()
                        sc_t, sctok, _ = sc_r[dr].next()
                        u.update(kw=kw_t, kwtok=kwtok, sc=sc_t, sctok=sctok)
                        wcol = wk[:, c, lane:lane + 1]
                        P.op("dve", lambda e, kw_t=kw_t, ktp=ktp, wcol=wcol: e.tensor_scalar(out=kw_t, in0=ktp, scalar1=wcol, scalar2=None, op0=ALU.mult), reads=[tA, "wk%d" % dr], writes=[kwtok])
                        P.op("dve", lambda e, sc_t=sc_t, stp=stp, wcol=wcol, dr=dr: e.scalar_tensor_tensor(out=sc_t, in0=stp, scalar=wcol, in1=maskd[dr], op0=ALU.mult, op1=ALU.mult), reads=[tA, "wk%d" % dr, "mask_f", "mask_b"], writes=[sctok])

                def stage3(i):
                    for dr in range(2):
                        u = U[(i, dr)]
                        kvp = psum[:, (4 + 2 * dr) * 512:(6 + 2 * dr) * 512]
                        for hf in range(2):
                            P.op("pe", lambda e, hf=hf, kw_t=u["kw"], vx_t=u["vx"], kvp=kvp: e.matmul(kvp[:, hf * 512:hf * 512 + 257], lhsT=kw_t[:, hf * 128:(hf + 1) * 128], rhs=vx_t[:, 0:257], start=True, stop=True), reads=[u["kwtok"], u["vxtok"]], writes=[("bKV", dr)])
                    for dr in range(2):
                        u = U[(i, dr)]
                        bN = bank(2 + dr)
                        P.op("pe", lambda e, sc_t=u["sc"], vx_t=u["vx"], bN=bN: e.matmul(bN[:, 0:257], lhsT=sc_t, rhs=vx_t[:, 0:257], start=True, stop=False), reads=[u["sctok"], u["vxtok"]], writes=[("bN", dr)])
                        for hf in range(2):
                            P.op("pe", lambda e, hf=hf, qk_t=u["qk"], bN=bN, dr=dr: e.matmul(bN[:, 0:257], lhsT=qk_t[:, hf, :], rhs=Cb[dr][:, hf, 0:257], start=False, stop=(hf == 1)), reads=[u["qktok"], ("Cb", dr)], writes=[("bN", dr)])

                def stage4(i):
                    for dr in range(2):
                        u = U[(i, dr)]
                        c = u["c"]
                        lane = dr * 4 + h
                        cprev = c - 1 if dr == 0 else c + 1
                        kvp = psum[:, (4 + 2 * dr) * 512:(6 + 2 * dr) * 512]
                        joined_prev = (cbase == 0) and ((dr == 0 and c == 32) or (dr == 1 and c == 31))
                        joined_next = (cbase == 0) and ((dr == 0 and c == 31) or (dr == 1 and c == 32))
                        dprev = decj[:, lane:lane + 1] if joined_prev else dec[:, (c if i == 0 else cprev), lane:lane + 1]
                        dnext = decj[:, lane:lane + 1] if joined_next else dec[:, c, lane:lane + 1]
                        kv3 = kvp.rearrange("p (a b) -> p a b", a=2)[:, :, 0:257]
                        P.op("dve", lambda e, dr=dr, dprev=dprev, kv3=kv3: e.scalar_tensor_tensor(out=Tst[dr], in0=Tst[dr], scalar=dprev, in1=kv3, op0=ALU.mult, op1=ALU.add), reads=[("T", dr), ("bKV", dr), "dec%d" % dr, "decj%d" % dr], writes=[("T", dr)])
                        if i < nchk - 1:
                            P.op("act", lambda e, dr=dr, dnext=dnext: e.activation(out=Cb[dr][:, :, 0:257], in_=Tst[dr], func=AF.Copy, scale=dnext), reads=[("T", dr), "dec%d" % dr, "decj%d" % dr], writes=[("Cb", dr)])

                def stage5(i):
                    for dr in range(2):
                        u = U[(i, dr)]
                        c, fin = u["c"], u["fin"]
                        lane = dr * 4 + h
                        bN = bank(2 + dr)
                        tN = ("bN", dr)
                        sm, smtok, _ = sm_r[dr].next()
                        ns, nstok, _ = ns_r[dr].next()
                        P.op("act", lambda e, ns=ns, bN=bN: e.activation(out=ns[:, 0:257], in_=bN[:, 0:257], func=AF.Copy), reads=[tN], writes=[nstok])
                        P.op("dve", lambda e, sm=sm, ns=ns: e.scalar_tensor_tensor(out=sm[:, 0:1], in0=ns[:, 256:257], scalar=-1.0, in1=ns[:, 256:257], op0=ALU.mult, op1=ALU.max), reads=[nstok], writes=[smtok])
                        P.op("dve", lambda e, sm=sm, c=c, lane=lane: e.tensor_tensor(out=sm[:, 1:2], in0=sm[:, 0:1], in1=flo[:, c, lane:lane + 1], op=ALU.max), reads=[smtok, "flo%d" % dr], writes=[smtok])
                        P.op("dve", lambda e, sm=sm: e.reciprocal(out=sm[:, 2:3], in_=sm[:, 1:2]), reads=[smtok], writes=[smtok])
                        hp_ = hpart[:, c - cbase, :]
                        hptok = ("hpart", c - cbase)
                        if not fin:
                            P.op("act", lambda e, hp_=hp_, ns=ns, sm=sm: e.activation(out=hp_, in_=ns[:, 0:256], func=AF.Copy, scale=sm[:, 2:3]), reads=[nstok, smtok], writes=[hptok])
                        else:
                            hh, hhtok, _ = hh_r.next()
                            yv, yvtok, _ = yv_r.next()
                            gz_t, gztok = u["gz"], u["gztok"]
                            P.op("dve", lambda e, hh=hh, ns=ns, sm=sm, hp_=hp_: e.scalar_tensor_tensor(out=hh, in0=ns[:, 0:256], scalar=sm[:, 2:3], in1=hp_, op0=ALU.mult, op1=ALU.add), reads=[nstok, smtok, hptok], writes=[hhtok])
                            P.op("act", lambda e, hh=hh, sm=sm: e.activation(out=sq3, in_=hh, func=AF.Square, accum_out=sm[:, 3:4]), reads=[hhtok, smtok], writes=[smtok, "sq3"])
                            P.op("act", lambda e, sm=sm: e.activation(out=sm[:, 0:1], in_=sm[:, 3:4], func=AF.Ln, scale=1.0 / 256, bias=ceps[:, 0:1]), reads=[smtok, "ceps"], writes=[smtok])
                            P.op("act", lambda e, sm=sm: e.activation(out=sm[:, 1:2], in_=sm[:, 0:1], func=AF.Exp, scale=-0.5), reads=[smtok], writes=[smtok])
                            P.op("dve", lambda e, yv=yv, hh=hh, sm=sm, gz_t=gz_t: e.scalar_tensor_tensor(out=yv, in0=hh, scalar=sm[:, 1:2], in1=gz_t, op0=ALU.mult, op1=ALU.mult), reads=[hhtok, smtok, gztok], writes=[yvtok])
                            u.update(yv=yv, yvtok=yvtok)

                def stage6(i):
                    for dr in range(2):
                        u = U[(i, dr)]
                        if not u["fin"]:
                            continue
                        c = u["c"]
                        yv, yvtok = u["yv"], u["yvtok"]
                        yTp = bank_bf(dr)[:, 512:768]
                        yst, ysttok, ystkey = yst_r.next()
                        for hf in range(2):
                            P.op("pe", lambda e, hf=hf, yv=yv, yTp=yTp: e.transpose(yTp[:, hf * 128:(hf + 1) * 128], yv[:, hf * 128:(hf + 1) * 128], ident_b), reads=[yvtok, "ident_b"], writes=[("bY", dr)])
                        P.op("act", lambda e, yst=yst, yTp=yTp: e.activation(out=yst, in_=yTp.rearrange("p (a b) -> p a b", a=2), func=AF.Copy), reads=[("bY", dr)], writes=[ysttok])
                        P.op("act", lambda e, yst=yst, c=c, h=h: e.dma_start(out=yAT_v[:, 2 * h:2 * h + 2, c * 128:(c + 1) * 128], in_=yst), reads=[ysttok], multi=["yAT"], dma=ystkey)

                PF = 2
                for i in range(min(PF, nchk)):
                    loads(i)
                stage12(0)
                for i in range(nchk):
                    if i + PF < nchk:
                        loads(i + PF)
                    if i + 1 < nchk:
                        stage12(i + 1)
                    if i >= 1:
                        stage6(i - 1)
                    stage3(i)
                    stage4(i)
                    stage5(i)
                stage6(nchk - 1)
        P.barrier()
        M.reset(ph2_mark)
        if stop_after <= 3:
            P.emit()
            return nc
        bias_sb = M.f32([128, 12, 4, 128])
        rel_sb = M.f32([33, 24])
        oh_sb = M.f32([33, 6, 256])
        grep_r = Ring("grep", [M.f32([128, 256]) for _ in range(2)])
        P.op("sp", lambda e: e.dma_start(out=rel_sb, in_=relx), writes=["rel_sb"], dma="c0")
        P.op("sp", lambda e: e.dma_start(out=oh_sb, in_=oh_in.rearrange("g ab b j -> b (g ab) j")), writes=["oh_sb"], dma="c1")
        gs = gs_h.ap()
        for g in range(3):
            for ab in range(2):
                idx = g * 2 + ab
                for h8 in range(8):
                    row = idx * 8 + h8
                    gr_t, grtok, grkey = grep_r.next()
                    P.op("pe", lambda e, g=g, idx=idx, h8=h8: e.matmul(bank(0)[:, 0:256], lhsT=rel_sb[:, g * 8 + h8:g * 8 + h8 + 1].broadcast_to([33, 128]), rhs=oh_sb[:, idx, :], start=True, stop=True), reads=["rel_sb", "oh_sb"], writes=["pb0"])
                    P.op("dve", lambda e, gr_t=gr_t: e.tensor_copy(out=gr_t, in_=bank(0)[:, 0:256]), reads=["pb0"], writes=[grtok])
                    P.op("sp", lambda e, gr_t=gr_t, row=row: e.dma_start(out=gs[row], in_=gr_t), reads=[grtok], multi=["gs"], dma=grkey)
        for g in range(3):
            for hp in range(4):
                for hh in range(2):
                    for ab in range(2):
                        row = (g * 2 + ab) * 8 + hp * 2 + hh
                        src = bass.AP(gs_h, row * 128 * 256 + 127, [[255, 128], [1, 128]])
                        P.op("sp", lambda e, src=src, g=g, hp=hp, hh=hh, ab=ab: e.dma_start(out=bias_sb[:, g * 4 + hp, hh * 2 + ab, :], in_=src), reads=["gs"], multi=["bias_sb"], dma="c%d" % ((hh * 2 + ab) % 3))
        P.barrier()
        ph4_mark = M.mark()
        qT_r = Ring("aq", [M.bf16([128, SEG]) for _ in range(2)])
        kT_r = Ring("ak", [M.bf16([128, SEG + 16 * 128]) for _ in range(2)])
        vt_r = Ring("avt", [M.bf16([128, 48, 3, 64]) for _ in range(2)])
        acc_sb = [M.f32([128, SEG]) for _ in range(2)]
        tb_r = Ring("tb", [M.f32([128, 256]) for _ in range(10)])
        pT_r = Ring("pT", [M.bf16([128, 256]) for _ in range(10)])
        rd = M.f32([128, 1024])
        sz_r = Ring("sz", [M.bf16([128, 1024]) for _ in range(2)])
        tt_ = M.f32([128, 1024])
        yb_r = Ring("ybst", [M.bf16([128, 1024]) for _ in range(2)])
        S_r = [Ring("Sps0", [bank(0)[:, 0:256], bank(1)[:, 0:256]]),
               Ring("Sps1", [bank(6)[:, 0:256], bank(7)[:, 0:256]])]
        acc_r = [Ring("accps0", [bank(2), bank(3)]), Ring("accps1", [bank(4), bank(5)])]
        for seg in range(NSEG):
            t0 = seg * SEG
            ljoin, rjoin = (seg == 1), (seg == 0)
            for hp in range(4):
                for g, d in enumerate(DILS):
                    L = SEG // d
                    nj = L // 128
                    W = L + 128
                    qf, qtok, qkey = qT_r.next()
                    kf, ktok, kkey = kT_r.next()
                    vf, vtok, vkey = vt_r.next()
                    q3 = qf.rearrange("p (r a) -> p r a", r=d)
                    k3 = kf[:, 0:d * W].rearrange("p (r a) -> p r a", r=d)
                    rows = slice(hp * 128, (hp + 1) * 128)
                    P.op("sp", lambda e, qf=qf, g=g, rows=rows, t0=t0: e.dma_start(out=qf, in_=aqk[g, rows, t0:t0 + SEG]), reads=["aqk"], writes=[qtok], dma=qkey)
                    P.op("sp", lambda e, k3=k3, g=g, rows=rows, d=d, L=L, t0=t0: e.dma_start(out=k3[:, :, 64:64 + L], in_=aqk[3 + g, rows, t0:t0 + SEG].rearrange("p (r a) -> p r a", r=d)), reads=["aqk"], multi=[ktok], dma=kkey)
                    if ljoin:
                        P.op("sp", lambda e, k3=k3, g=g, rows=rows, d=d, L=L, t0=t0: e.dma_start(out=k3[:, :, 0:64], in_=aqk[3 + g, rows, t0 - SEG:t0].rearrange("p (r a) -> p r a", r=d)[:, :, L - 64:L]), reads=["aqk"], multi=[ktok], dma=kkey)
                    else:
                        P.op("pool", lambda e, k3=k3: e.memset(k3[:, :, 0:64], 0.0), multi=[ktok])
                    if rjoin:
                        P.op("sp", lambda e, k3=k3, g=g, rows=rows, d=d, L=L, t0=t0: e.dma_start(out=k3[:, :, 64 + L:128 + L], in_=aqk[3 + g, rows, t0 + SEG:t0 + 2 * SEG].rearrange("p (r a) -> p r a", r=d)[:, :, 0:64]), reads=["aqk"], multi=[ktok], dma=kkey)
                    else:
                        P.op("pool", lambda e, k3=k3, L=L: e.memset(k3[:, :, 64 + L:128 + L], 0.0), multi=[ktok])
                    for r in range(d):
                        ti0 = av_tile_index(g, seg, r, 0)
                        for hh in range(2):
                            P.op("sp", lambda e, vf=vf, g=g, r=r, ti0=ti0, nj=nj, hp=hp, hh=hh: e.dma_start(out=vf[:, r * (nj + 1):(r + 1) * (nj + 1), 2 * hh, :], in_=av[g][ti0:ti0 + nj + 1, :, hp * 128 + hh * 64:hp * 128 + (hh + 1) * 64].rearrange("t k e -> k t e")), reads=["av%d" % g], multi=[vtok], dma=vkey)
                    ntile = d * (nj + 1)
                    P.op("pool", lambda e, vf=vf, ntile=ntile: e.memset(vf[:, 0:ntile, 1, :], 1.0), multi=[vtok])
                    vfl = vf.rearrange("p t a e -> p t (a e)")
                    if ljoin:
                        P.op("dve", lambda e, vfl=vfl, nj=nj, ntile=ntile: e.tensor_scalar(out=vfl[0:64, 0:ntile:nj + 1, :], in0=vfl[0:64, 0:ntile:nj + 1, :], scalar1=jcol[0:64, 0:1], scalar2=None, op0=ALU.mult), reads=[vtok, "jcol"], writes=[vtok])
                    else:
                        P.op("pool", lambda e, vfl=vfl, nj=nj, ntile=ntile: e.memset(vfl[0:64, 0:ntile:nj + 1, :], 0.0), reads=[vtok], writes=[vtok])
                    if rjoin:
                        P.op("dve", lambda e, vfl=vfl, nj=nj, ntile=ntile: e.tensor_scalar(out=vfl[64:128, nj:ntile:nj + 1, :], in0=vfl[64:128, nj:ntile:nj + 1, :], scalar1=jcol[64:128, 0:1], scalar2=None, op0=ALU.mult), reads=[vtok, "jcol"], writes=[vtok])
                    else:
                        P.op("pool", lambda e, vfl=vfl, nj=nj, ntile=ntile: e.memset(vfl[64:128, nj:ntile:nj + 1, :], 0.0), reads=[vtok], writes=[vtok])
                    units = []
                    for r in range(d):
                        for i in range(nj):
                            for hh in range(2):
                                units.append((r, i, hh))
                    KSK = 3
                    UU = {}
                    accs = {}

                    def stA(un):
                        r, i, hh = un
                        S, Stok, _ = S_r[hh].next()
                        tb, tbtok, _ = tb_r.next()
                        pT, pTtok, _ = pT_r.next()
                        UU[un] = (pT, pTtok)
                        bsl = bias_sb[:, g * 4 + hp, 2 * hh:2 * hh + 2, :].rearrange("p a b -> p (a b)")
                        for ab in range(2):
                            j = i + ab
                            P.op("pe", lambda e, S=S, hh=hh, r=r, j=j, ab=ab, i=i, k3=k3, q3=q3: e.matmul(S[:, ab * 128:(ab + 1) * 128], lhsT=k3[hh * 64:(hh + 1) * 64, r, 128 * j:128 * j + 128], rhs=q3[hh * 64:(hh + 1) * 64, r, i * 128:(i + 1) * 128], start=True, stop=True), reads=[qtok, ktok], writes=[Stok])
                        P.op("dve", lambda e, tb=tb, S=S, bsl=bsl: e.tensor_tensor(out=tb, in0=S, in1=bsl, op=ALU.add), reads=[Stok, "bias_sb"], writes=[tbtok])
                        P.op("act", lambda e, tb=tb, pT=pT: e.activation(out=pT, in_=tb, func=AF.Exp), reads=[tbtok], writes=[pTtok])

                    def stD(un):
                        r, i, hh = un
                        pT, pTtok = UU.pop(un)
                        i0 = (i // 4) * 4
                        nq = min(4, nj - i0)
                        qi = i - i0
                        key = (r, i0, hh)
                        if key not in accs:
                            accs[key] = acc_r[hh].next()
                        ap_, aptok, _ = accs[key]
                        for ab in range(2):
                            ti = r * (nj + 1) + i + ab
                            P.op("pe", lambda e, hh=hh, ab=ab, ti=ti, qi=qi, pT=pT, ap_=ap_, vf=vf: e.matmul(ap_[:, qi * 128:(qi + 1) * 128], lhsT=vf[:, ti, hh:hh + 2, :].rearrange("p a e -> p (a e)"), rhs=pT[:, ab * 128:(ab + 1) * 128], start=(ab == 0), stop=(ab == 1)), reads=[vtok, pTtok], writes=[aptok])
                        if qi == nq - 1:
                            dst = acc_sb[hh].rearrange("p (a r) -> p r a", r=d)[:, r, i0 * 128:(i0 + nq) * 128]
                            srcp = ap_[:, 0:nq * 128]
                            if g == 0:
                                P.op("act", lambda e, dst=dst, srcp=srcp: e.activation(out=dst, in_=srcp, func=AF.Copy), reads=[aptok], multi=[("accsb", hh)])
                            else:
                                P.op("dve", lambda e, dst=dst, srcp=srcp: e.tensor_tensor(out=dst, in0=srcp, in1=dst, op=ALU.add), reads=[aptok, ("accsb", hh)], multi=[("accsb", hh)])
                            del accs[key]

                    for n_ in range(len(units) + KSK):
                        if n_ < len(units):
                            stA(units[n_])
                        if n_ >= KSK:
                            stD(units[n_ - KSK])
                for pc in range(4):
                    cs = slice(pc * 1024, (pc + 1) * 1024)
                    sz, sztok, szkey = sz_r.next()
                    yb, ybtok, ybkey = yb_r.next()
                    P.op("sp", lambda e, sz=sz, hp=hp, pc=pc, t0=t0: e.dma_start(out=sz, in_=azT[hp * 128:(hp + 1) * 128, t0 + pc * 1024:t0 + (pc + 1) * 1024]), reads=["azT"], writes=[sztok], dma=szkey)
                    P.op("act", lambda e, cs=cs: e.activation(out=rd[0:64, :], in_=acc_sb[0][64:128, cs], func=AF.Copy), reads=[("accsb", 0)], multi=["rd"])
                    P.op("act", lambda e, cs=cs: e.activation(out=rd[64:128, :], in_=acc_sb[1][0:64, cs], func=AF.Copy), reads=[("accsb", 1)], multi=["rd"])
                    P.op("dve", lambda e: e.reciprocal(out=rd, in_=rd), reads=["rd"], writes=["rd"])
                    P.op("dve", lambda e, sz=sz: e.tensor_tensor(out=tt_, in0=rd, in1=sz, op=ALU.mult), reads=["rd", sztok], writes=["tt_"])
                    P.op("dve", lambda e, yb=yb, cs=cs: e.tensor_tensor(out=yb[0:64, :], in0=acc_sb[0][0:64, cs], in1=tt_[0:64, :], op=ALU.mult), reads=[("accsb", 0), "tt_"], multi=[ybtok])
                    P.op("dve", lambda e, yb=yb, cs=cs: e.tensor_tensor(out=yb[64:128, :], in0=acc_sb[1][64:128, cs], in1=tt_[64:128, :], op=ALU.mult), reads=[("accsb", 1), "tt_"], multi=[ybtok])
                    P.op("sp", lambda e, yb=yb, hp=hp, pc=pc, t0=t0: e.dma_start(out=yBT[hp * 128:(hp + 1) * 128, t0 + pc * 1024:t0 + (pc + 1) * 1024], in_=yb), reads=[ybtok], multi=["yBT"], dma=ybkey)
        P.barrier()
        M.reset(ph2_mark)
        if stop_after <= 4:
            P.emit()
            return nc

        wpm_sb = M.bf16([128, 8, 1024])
        wpa_sb = M.bf16([128, 4, 1024])
        wout_sb = M.bf16([128, 8, 1024])
        postw_bc = M.f32([128, 1024])
        P.op("sp", lambda e: e.dma_start(out=postw_bc, in_=postw_h.ap().partition_broadcast(128)), writes=["postw"], dma="c0")
        wst5 = Ring("w5", [M.f32([128, 1024]) for _ in range(2)])
        n = 0
        for (wsrc, wdst, nk, nm) in ((w_pm, wpm_sb, 8, "wpm"), (w_pa, wpa_sb, 4, "wpa"), (w_out, wout_sb, 8, "wout")):
            for k in range(nk):
                s_t, s_tok, s_key = wst5.next()
                P.op("sp", lambda e, s_t=s_t, wsrc=wsrc, k=k: e.dma_start(out=s_t, in_=wsrc[k * 128:(k + 1) * 128, :]), writes=[s_tok], dma=s_key)
                if n % 2 == 0:
                    P.op("dve", lambda e, s_t=s_t, wdst=wdst, k=k: e.tensor_copy(out=wdst[:, k, :], in_=s_t), reads=[s_tok], multi=[nm])
                else:
                    P.op("act", lambda e, s_t=s_t, wdst=wdst, k=k: e.activation(out=wdst[:, k, :], in_=s_t, func=AF.Copy), reads=[s_tok], multi=[nm])
                n += 1
        ya_r = Ring("ya", [M.bf16([128, 8, 512]) for _ in range(2)])
        ybb_r = Ring("ybb", [M.bf16([128, 4, 512]) for _ in range(2)])
        gt_r = Ring("gt", [M.bf16([128, 16, 512]) for _ in range(2)])
        x5_r = Ring("x5", [M.f32([128, 1024]) for _ in range(8)])
        mT_r = Ring("mT", [M.bf16([128, 8, 512]) for _ in range(2)])
        t1_r = Ring("t1", [M.f32([128, 512]) for _ in range(2)])
        t2_r = Ring("t2", [M.f32([128, 512]) for _ in range(2)])
        yn_r = Ring("yn", [M.f32([128, 1024]) for _ in range(2)])
        yo_r = Ring("yo", [M.f32([128, 1024]) for _ in range(2)])
        s5_r = Ring("s5", [M.f32([128, 4]) for _ in range(2)])
        sq5 = M.bf16([128, 1024])
        pm_r = Ring("pm", [bank(0), bank(1)])
        pa_r = Ring("pa", [bank(2), bank(3)])
        o_r = Ring("ops", [psum[:, 4 * 512:6 * 512], psum[:, 6 * 512:8 * 512]])
        yBT_v = yBT.rearrange("(a p) t -> p a t", p=128)
        gT_v = gT.rearrange("(a p) t -> p a t", p=128)
        def load_x5(tt_):
            res_ = []
            for j in range(4):
                x5, x5tok, x5key = x5_r.next()
                r0 = tt_ * 512 + j * 128
                P.op("sp", lambda e, x5=x5, r0=r0: e.dma_start(out=x5, in_=x[r0:r0 + 128, :]), writes=[x5tok], dma=x5key)
                res_.append((x5, x5tok))
            return res_

        for tt in range(T // 512):
            ts0 = tt * 512
            ya, yatok, yakey = ya_r.next()
            ybb, ybbtok, ybbkey = ybb_r.next()
            gt, gttok, gtkey = gt_r.next()
            mT, mTtok, _ = mT_r.next()
            P.op("sp", lambda e, ya=ya, ts0=ts0: e.dma_start(out=ya, in_=yAT_v[:, :, ts0:ts0 + 512]), reads=["yAT"], writes=[yatok], dma=yakey)
            P.op("sp", lambda e, ybb=ybb, ts0=ts0: e.dma_start(out=ybb, in_=yBT_v[:, :, ts0:ts0 + 512]), reads=["yBT"], writes=[ybbtok], dma=ybbkey)
            P.op("sp", lambda e, gt=gt, ts0=ts0: e.dma_start(out=gt, in_=gT_v[:, :, ts0:ts0 + 512]), reads=["gT"], writes=[gttok], dma=gtkey)
            for dc in range(8):
                pm, pmtok, _ = pm_r.next()
                pa, patok, _ = pa_r.next()
                t1, t1tok, _ = t1_r.next()
                t2, t2tok, _ = t2_r.next()
                for kc in range(8):
                    P.op("pe", lambda e, pm=pm, kc=kc, dc=dc, ya=ya: e.matmul(pm, lhsT=wpm_sb[:, kc, dc * 128:(dc + 1) * 128], rhs=ya[:, kc, :], start=(kc == 0), stop=(kc == 7)), reads=["wpm", yatok], writes=[pmtok])
                for kc in range(4):
                    P.op("pe", lambda e, pa=pa, kc=kc, dc=dc, ybb=ybb: e.matmul(pa, lhsT=wpa_sb[:, kc, dc * 128:(dc + 1) * 128], rhs=ybb[:, kc, :], start=(kc == 0), stop=(kc == 3)), reads=["wpa", ybbtok], writes=[patok])
                P.op("dve", lambda e, t1=t1, pm=pm, gt=gt, dc=dc: e.tensor_tensor(out=t1, in0=pm, in1=gt[:, dc, :], op=ALU.mult), reads=[pmtok, gttok], writes=[t1tok])
                P.op("dve", lambda e, t2=t2, pa=pa, gt=gt, dc=dc: e.tensor_tensor(out=t2, in0=pa, in1=gt[:, 8 + dc, :], op=ALU.mult), reads=[patok, gttok], writes=[t2tok])
                P.op("dve", lambda e, t1=t1, t2=t2, mT=mT, dc=dc: e.tensor_tensor(out=mT[:, dc, :], in0=t1, in1=t2, op=ALU.add), reads=[t1tok, t2tok], multi=[mTtok])
            if tt == 0:
                x5next = load_x5(0)
            x5s = x5next
            if tt + 1 < T // 512:
                x5next = load_x5(tt + 1)
            for j in range(4):
                o_, otok, _ = o_r.next()
                x5, x5tok = x5s[j]
                s5, s5tok, _ = s5_r.next()
                yn, yntok, _ = yn_r.next()
                yo, yotok, yokey = yo_r.next()
                r0 = ts0 + j * 128
                for half in range(2):
                    for dc in range(8):
                        P.op("pe", lambda e, o_=o_, half=half, dc=dc, j=j, mT=mT: e.matmul(o_[:, half * 512:(half + 1) * 512], lhsT=mT[:, dc, j * 128:(j + 1) * 128], rhs=wout_sb[:, dc, half * 512:(half + 1) * 512], start=(dc == 0), stop=(dc == 7)), reads=[mTtok, "wout"], writes=[otok])
                P.op("act", lambda e, o_=o_, s5=s5: e.activation(out=sq5, in_=o_, func=AF.Square, accum_out=s5[:, 0:1]), reads=[otok], writes=[s5tok, "sq5"])
                P.op("act", lambda e, s5=s5: e.activation(out=s5[:, 1:2], in_=s5[:, 0:1], func=AF.Ln, scale=1.0 / D, bias=ceps[:, 0:1]), reads=[s5tok, "ceps"], writes=[s5tok])
                P.op("act", lambda e, s5=s5: e.activation(out=s5[:, 2:3], in_=s5[:, 1:2], func=AF.Exp, scale=-0.5), reads=[s5tok], writes=[s5tok])
                P.op("dve", lambda e, yn=yn, o_=o_, s5=s5: e.scalar_tensor_tensor(out=yn, in0=o_, scalar=s5[:, 2:3], in1=postw_bc, op0=ALU.mult, op1=ALU.mult), reads=[otok, s5tok, "postw"], writes=[yntok])
                P.op("dve", lambda e, yo=yo, yn=yn, x5=x5: e.tensor_tensor(out=yo, in0=yn, in1=x5, op=ALU.add), reads=[yntok, x5tok], writes=[yotok])
                P.op("sp", lambda e, yo=yo, r0=r0: e.dma_start(out=y[r0:r0 + 128, :], in_=yo), reads=[yotok], dma=yokey)
        P.emit()
        return nc


def host_inputs(x_prompt, x_sample, pre_norm_w, w_in, m_conv_w, m_conv_b, m_igate_b, m_fgate_b, m_head_norm_w,
                w_proj_m, w_proj_a, w_out, post_norm_w, rel_bias_table):
    f = np.float32
    common = {
        "w_in": np.ascontiguousarray(w_in[0], f),
        "prew": np.ascontiguousarray(pre_norm_w[0].reshape(8, 128).T, f),
        "convw": np.ascontiguousarray(m_conv_w[0].reshape(3, 16, 128).transpose(2, 1, 0), f),
        "convb": np.ascontiguousarray(m_conv_b[0].reshape(16, 128).T, f),
        "gbias": np.concatenate([m_igate_b[0].reshape(-1), m_fgate_b[0].reshape(-1)]).astype(f).reshape(1, 16),
        "hnw": np.ascontiguousarray(m_head_norm_w[0].reshape(1, 1024), f),
        "w_pm": np.ascontiguousarray(w_proj_m[0], f),
        "w_pa": np.ascontiguousarray(w_proj_a[0], f),
        "w_out": np.ascontiguousarray(w_out[0], f),
        "postw": np.ascontiguousarray(post_norm_w[0].reshape(1, 1024), f),
        "relx": np.concatenate([rel_bias_table.reshape(32, 24), np.ones((1, 24), f)], 0).astype(f),
        "ident": np.eye(128, dtype=f),
        "tri": np.triu(np.ones((128, 128), f)),
        "oh": bias_onehots(),
    }
    maps = []
    for c in range(8):
        if c < 4:
            xc = np.concatenate([x_prompt[c], x_sample[c]], 0)
            j = 1.0
        else:
            b = 4 + 3 * (c - 4)
            xc = np.concatenate([x_sample[b], x_sample[b + 1], x_sample[b + 2]], 0)
            j = 0.0
        m = dict(common)
        m["x"] = np.ascontiguousarray(xc, f)
        m["jflag"] = np.full((128, 1), j, f)
        maps.append(m)
    return maps


_CACHE = {}


def kernel(**inputs):
    inputs = {k: np.asarray(v) for k, v in inputs.items()}
    maps = host_inputs(**inputs)
    if "nc" not in _CACHE:
        _CACHE["nc"] = build_program()
    res = run_bass_kernel_spmd(_CACHE["nc"], maps, core_ids=list(range(8)))
    ys = [r["y"] for r in res.results]
    y_prompt = np.stack([ys[c][0:8192] for c in range(4)], 0)
    samples = [None] * 16
    for c in range(4):
        samples[c] = ys[c][8192:12288]
    for c in range(4, 8):
        b = 4 + 3 * (c - 4)
        for i in range(3):
            samples[b + i] = ys[c][i * 4096:(i + 1) * 4096]
    y_sample = np.stack(samples, 0)
    return (y_prompt.astype(np.float32), y_sample.astype(np.float32))
```
